# Optimizing a Trainium2 kernel written in Bass

```python
import math
import jax
import jax.numpy as jnp
from jax import lax
import numpy as np

D_MODEL = 1024
BATCH = 16
SEQ = 4096
DEPTH = 4

GRID_W = 64
CTX_LEN = 256
N_MIXERS = 2
D_RNN = D_MODEL
RG_HEADS = 4
RG_HEAD_DIM = D_RNN // RG_HEADS
RG_CONV_W = 4
RG_C = 8.0
HY_SHORT_W = 3
HY_EMB_DIM = 33
HY_BANDS = (HY_EMB_DIM - 1) // 2
HY_FILTER_DIM = 64
HY_FAST_DECAY = 0.3
HY_SLOW_DECAY = 1.5
HY_TARGET = 1e-2
D_FF = 2816
FFN_CONV_W = 3
N_MOD = 6
EPS = 1e-6

kernel_name = 'hybrid_rglru_hyena_dit_trunk'


def rms_norm(x, g):
    xf = x.astype(jnp.float32)
    y = xf * lax.rsqrt(jnp.mean(xf * xf, axis=-1, keepdims=True) + EPS)
    return (y * g.astype(jnp.float32)).astype(x.dtype)


def dwconv(x, w, b, pad):
    y = lax.conv_general_dilated(x, w[:, None, :].astype(x.dtype), window_strides=(1,), padding=[pad],
                                 dimension_numbers=('NWC', 'WIO', 'NWC'), feature_group_count=x.shape[-1])
    return y + b.astype(x.dtype)


def grid_order(x, rows, col_major):
    if not col_major:
        return x
    b, n, d = x.shape
    return x.reshape(b, rows, GRID_W, d).transpose(0, 2, 1, 3).reshape(b, n, d)


def raster_order(x, rows, col_major):
    if not col_major:
        return x
    b, n, d = x.shape
    return x.reshape(b, GRID_W, rows, d).transpose(0, 2, 1, 3).reshape(b, n, d)


def linear_scan(a, b, h0, reverse):
    if h0 is not None:
        idx = -1 if reverse else 0
        b = b.at[:, idx].add(a[:, idx] * h0)

    def combine(early, late):
        a1, b1 = early
        a2, b2 = late
        return a1 * a2, a2 * b1 + b2

    _, h = lax.associative_scan(combine, (a, b), reverse=reverse, axis=1)
    return h


def rg_lru(u, w_a, b_a, w_i, b_i, lam, h0, reverse):
    bsz, n, _ = u.shape
    uh = u.reshape(bsz, n, RG_HEADS, RG_HEAD_DIM)
    r = jax.nn.sigmoid((jnp.einsum('bnhi,hij->bnhj', uh, w_a).reshape(bsz, n, D_RNN) + b_a).astype(jnp.float32))
    gi = jax.nn.sigmoid((jnp.einsum('bnhi,hij->bnhj', uh, w_i).reshape(bsz, n, D_RNN) + b_i).astype(jnp.float32))
    log_a = -RG_C * r * jax.nn.softplus(-lam.astype(jnp.float32))
    a = jnp.exp(log_a)
    beta = jnp.sqrt(-jnp.expm1(2.0 * log_a))
    return linear_scan(a, beta * gi * u.astype(jnp.float32), h0, reverse)


def rglru_core(hn, w_in, conv_w, conv_b, w_a, b_a, w_i, b_i, lam, h0_f, h0_b, with_gate):
    if with_gate:
        z = hn @ w_in
        gate, u = z[..., :D_RNN], z[..., D_RNN:]
    else:
        gate, u = None, hn @ w_in[:, D_RNN:]
    u = dwconv(u, conv_w, conv_b, (1, RG_CONV_W - 2))
    hf = rg_lru(u, w_a[0], b_a[0], w_i[0], b_i[0], lam[0], h0_f, False)
    hb = rg_lru(u, w_a[1], b_a[1], w_i[1], b_i[1], lam[1], h0_b, True)
    return gate, hf, hb


def rglru_out(gate, hf, hb, w_out):
    return ((hf + hb).astype(gate.dtype) * jax.nn.gelu(gate)) @ w_out


def hyena_filter(n, pe_w1, pe_b1, pe_w2, pe_b2, pe_w3, pe_b3, pe_w4, freq):
    f32 = jnp.float32
    t = jnp.linspace(0.0, 1.0, n, dtype=f32)[:, None]
    w = (2.0 * math.pi / n) * jnp.arange(n, dtype=f32)[:, None]
    bands = jnp.linspace(1e-4, HY_BANDS - 1, HY_BANDS, dtype=f32)[None, :]
    z = jnp.concatenate([t, jnp.cos(bands * w), -jnp.sin(bands * w)], axis=-1)
    fr = freq.astype(f32)
    hdn = jnp.sin(fr * (z @ pe_w1.astype(f32) + pe_b1.astype(f32)))
    hdn = jnp.sin(fr * (hdn @ pe_w2.astype(f32) + pe_b2.astype(f32)))
    hdn = jnp.sin(fr * (hdn @ pe_w3.astype(f32) + pe_b3.astype(f32)))
    k = hdn @ pe_w4.astype(f32)
    centre = n // 2
    dist = jnp.abs(jnp.arange(n) - centre).astype(f32)[:, None] / centre
    deltas = jnp.linspace(math.log(HY_TARGET) / HY_SLOW_DECAY, math.log(HY_TARGET) / HY_FAST_DECAY, D_MODEL, dtype=f32)
    k = k * jnp.exp(-dist * jnp.abs(deltas)[None, :])
    return k / jnp.sum(jnp.abs(k), axis=0, keepdims=True)


def long_conv_centred(u, k):
    n = u.shape[1]
    nfft = 2 * n
    centre = n // 2
    uf = jnp.fft.rfft(u.astype(jnp.float32), n=nfft, axis=1)
    kf = jnp.fft.rfft(k, n=nfft, axis=0)
    return jnp.fft.irfft(uf * kf[None], n=nfft, axis=1)[:, centre:centre + n]


def hyena_mixer(hn, w_in, short_w, short_b, pe_w1, pe_b1, pe_w2, pe_b2, pe_w3, pe_b3, pe_w4, freq, skip, w_out):
    n = hn.shape[1]
    z = dwconv(hn @ w_in, short_w, short_b, (1, 1))
    x0, x1, v = jnp.split(z, 3, axis=-1)
    xv = x1 * v
    k = hyena_filter(n, pe_w1, pe_b1, pe_w2, pe_b2, pe_w3, pe_b3, pe_w4, freq)
    y = long_conv_centred(xv, k) + xv.astype(jnp.float32) * skip.astype(jnp.float32)
    return (x0 * y.astype(hn.dtype)) @ w_out


def conv_ffn(hn, w_up, conv_w, conv_b, w_down):
    z = dwconv(hn @ w_up, conv_w, conv_b, (1, 1))
    g, u = jnp.split(z, 2, axis=-1)
    return (jax.nn.silu(g) * u) @ w_down


def setup_inputs(seed: int = 0) -> dict:
    key = jax.random.key(seed)
    ks = iter(jax.random.split(key, 48))
    D = D_MODEL
    n_a = len(range(0, DEPTH, N_MIXERS))
    n_b = len(range(1, DEPTH, N_MIXERS))

    def nrm(shape, scale):
        return jax.random.normal(next(ks), shape, jnp.float32) * scale

    u_lam = jax.random.uniform(next(ks), (n_a, 2, D_RNN), jnp.float32, minval=0.9, maxval=0.999)
    a_lam = u_lam ** (1.0 / RG_C)
    return {
        'x': nrm((BATCH, SEQ, D), 1.0),
        'c': nrm((BATCH, D), 1.0),
        'ctx': nrm((BATCH, CTX_LEN, D), 1.0),
        'c_ctx': nrm((D,), 1.0),
        'mod_w': nrm((DEPTH, D, N_MOD * D), 0.5 * D ** -0.5),
        'mod_b': nrm((DEPTH, N_MOD * D), 0.02),
        'norm1_g': 1.0 + nrm((DEPTH, D), 0.05),
        'norm2_g': 1.0 + nrm((DEPTH, D), 0.05),
        'final_g': 1.0 + nrm((D,), 0.05),
        'rg_w_in': nrm((n_a, D, 2 * D_RNN), D ** -0.5),
        'rg_conv_w': nrm((n_a, RG_CONV_W, D_RNN), RG_CONV_W ** -0.5),
        'rg_conv_b': nrm((n_a, D_RNN), 0.02),
        'rg_w_a': nrm((n_a, 2, RG_HEADS, RG_HEAD_DIM, RG_HEAD_DIM), RG_HEAD_DIM ** -0.5),
        'rg_b_a': nrm((n_a, 2, D_RNN), 0.02),
        'rg_w_i': nrm((n_a, 2, RG_HEADS, RG_HEAD_DIM, RG_HEAD_DIM), RG_HEAD_DIM ** -0.5),
        'rg_b_i': nrm((n_a, 2, D_RNN), 0.02),
        'rg_lam': jnp.log(a_lam) - jnp.log1p(-a_lam),
        'rg_w_out': nrm((n_a, D_RNN, D), D_RNN ** -0.5),
        'hy_w_in': nrm((n_b, D, 3 * D), D ** -0.5),
        'hy_short_w': nrm((n_b, HY_SHORT_W, 3 * D), HY_SHORT_W ** -0.5),
        'hy_short_b': nrm((n_b, 3 * D), 0.02),
        'hy_pe_w1': nrm((n_b, HY_EMB_DIM, HY_FILTER_DIM), HY_EMB_DIM ** -0.5),
        'hy_pe_b1': nrm((n_b, HY_FILTER_DIM), 0.1),
        'hy_pe_w2': nrm((n_b, HY_FILTER_DIM, HY_FILTER_DIM), HY_FILTER_DIM ** -0.5),
        'hy_pe_b2': nrm((n_b, HY_FILTER_DIM), 0.1),
        'hy_pe_w3': nrm((n_b, HY_FILTER_DIM, HY_FILTER_DIM), HY_FILTER_DIM ** -0.5),
        'hy_pe_b3': nrm((n_b, HY_FILTER_DIM), 0.1),
        'hy_pe_w4': nrm((n_b, HY_FILTER_DIM, D), HY_FILTER_DIM ** -0.5),
        'hy_freq': 1.0 + nrm((n_b, HY_FILTER_DIM), 0.05),
        'hy_skip': nrm((n_b, D), 1.0),
        'hy_w_out': nrm((n_b, D, D), D ** -0.5),
        'ffn_w_up': nrm((DEPTH, D, 2 * D_FF), D ** -0.5),
        'ffn_conv_w': nrm((DEPTH, FFN_CONV_W, 2 * D_FF), FFN_CONV_W ** -0.5),
        'ffn_conv_b': nrm((DEPTH, 2 * D_FF), 0.02),
        'ffn_w_down': nrm((DEPTH, D_FF, D), D_FF ** -0.5),
    }


def reference(x, c, ctx, c_ctx, mod_w, mod_b, norm1_g, norm2_g, final_g,
              rg_w_in, rg_conv_w, rg_conv_b, rg_w_a, rg_b_a, rg_w_i, rg_b_i, rg_lam, rg_w_out,
              hy_w_in, hy_short_w, hy_short_b, hy_pe_w1, hy_pe_b1, hy_pe_w2, hy_pe_b2, hy_pe_w3, hy_pe_b3,
              hy_pe_w4, hy_freq, hy_skip, hy_w_out,
              ffn_w_up, ffn_conv_w, ffn_conv_b, ffn_w_down):
    ROWS = x.shape[1] // GRID_W
    c_act = jax.nn.silu(c)
    cc_act = jax.nn.silu(c_ctx)
    s = ctx
    ctx_needed = [any((l % N_MIXERS) == 0 for l in range(i + 1, DEPTH)) for i in range(DEPTH)]
    for i in range(DEPTH):
        kind = i % N_MIXERS
        j = i // N_MIXERS
        col_major = (i // N_MIXERS) % 2 == 1
        keep_ctx = ctx_needed[i]
        sh1, sc1, g1, sh2, sc2, g2 = jnp.split((c_act @ mod_w[i] + mod_b[i])[:, None, :], N_MOD, axis=-1)
        h = grid_order(x, ROWS, col_major)
        hn = rms_norm(h, norm1_g[i]) * (1.0 + sc1) + sh1
        if kind == 0 or keep_ctx:
            csh1, csc1, cg1, csh2, csc2, cg2 = jnp.split((cc_act @ mod_w[i] + mod_b[i])[None, None, :], N_MOD, axis=-1)
            sn = rms_norm(s, norm1_g[i]) * (1.0 + csc1) + csh1
        if kind == 0:
            rg = (rg_w_in[j], rg_conv_w[j], rg_conv_b[j], rg_w_a[j], rg_b_a[j], rg_w_i[j], rg_b_i[j], rg_lam[j])
            gate_s, hf_s, hb_s = rglru_core(sn, *rg, None, None, keep_ctx)
            gate, hf, hb = rglru_core(hn, *rg, hf_s[:, -1], hb_s[:, 0], True)
            h = h + g1 * rglru_out(gate, hf, hb, rg_w_out[j])
            if keep_ctx:
                s = s + cg1 * rglru_out(gate_s, hf_s, hb_s, rg_w_out[j])
        else:
            hy = (hy_w_in[j], hy_short_w[j], hy_short_b[j], hy_pe_w1[j], hy_pe_b1[j], hy_pe_w2[j], hy_pe_b2[j],
                  hy_pe_w3[j], hy_pe_b3[j], hy_pe_w4[j], hy_freq[j], hy_skip[j], hy_w_out[j])
            h = h + g1 * hyena_mixer(hn, *hy)
            if keep_ctx:
                s = s + cg1 * hyena_mixer(sn, *hy)
        ffn = (ffn_w_up[i], ffn_conv_w[i], ffn_conv_b[i], ffn_w_down[i])
        h = h + g2 * conv_ffn(rms_norm(h, norm2_g[i]) * (1.0 + sc2) + sh2, *ffn)
        if keep_ctx:
            s = s + cg2 * conv_ffn(rms_norm(s, norm2_g[i]) * (1.0 + csc2) + csh2, *ffn)
        x = raster_order(h, ROWS, col_major)
    return rms_norm(x, final_g)
```

```python
import math
import numpy as np
import ml_dtypes
import concourse.bass as bass
import concourse.mybir as mybir
from concourse.bass_utils import run_bass_kernel_spmd
from contextlib import ExitStack

F32 = mybir.dt.float32
BF16 = mybir.dt.bfloat16
I32 = mybir.dt.int32
U8 = mybir.dt.uint8
AF = mybir.ActivationFunctionType
ALU = mybir.AluOpType

D = 1024
SEQ = 4096
CTX = 256
DFF = 2816
NCORE = 8
TWO_PI = 2.0 * math.pi
DEBUG_DUMP = ()


def I(name, **kw):
    return lambda e: getattr(e, name)(**kw)


class Buf:
    __slots__ = ("name", "t", "w", "r", "pr", "sem", "semv", "key")

    def __init__(s, name, t=None):
        s.name = name
        s.t = t
        s.w = {}
        s.r = {}
        s.pr = {}
        s.sem = None
        s.semv = 0
        s.key = None

    def __getitem__(s, k):
        return s.t[k]


class Q:
    def __init__(s, kb, name, eng):
        s.name = name
        s.eng = eng
        s.sem = kb.newsem("q_" + name)
        s.cnt = 0
        s.seen = {}
        s.ops = []
        s.shsem = None
        s.shv = 0


class KB:
    def __init__(s, nc, es):
        s.nc = nc
        s.es = es
        s.sems = []
        s.pe = Q(s, "pe", nc.tensor)
        s.act = Q(s, "act", nc.scalar)
        s.dve = Q(s, "dve", nc.vector)
        s.pool = Q(s, "pool", nc.gpsimd)
        s.sp = Q(s, "sp", nc.sync)
        s.qs = [s.pe, s.act, s.dve, s.pool, s.sp]
        s.semcache = {}
        s.nops = 0

    def newsem(s, name):
        h = s.es.enter_context(s.nc.semaphore(f"{name}_{len(s.sems)}"))
        s.sems.append(h)
        return len(s.sems) - 1

    def ps(s, name, shape, dt=F32):
        return s.es.enter_context(s.nc.psum_tensor(name, list(shape), dt))

    def dram(s, name, shape, dt=F32):
        kind = "ExternalOutput" if (DEBUG_DUMP and name in DEBUG_DUMP) else "Internal"
        return Buf(name, s.nc.dram_tensor(name, list(shape), dt, kind=kind).ap())

    def _waits(s, q, rd, wr, wrp, same_ok=False):
        need = {}

        def add(d):
            for k, v in d.items():
                if need.get(k, 0) < v:
                    need[k] = v

        for b in rd:
            add(b.w)
        for b in wr:
            add(b.w)
            add(b.r)
        for b in wrp:
            add(b.r)
            add(b.pr)
        out = []
        for k, v in need.items():
            if same_ok and k == q.sem:
                continue
            if q.seen.get(k, 0) >= v:
                continue
            q.seen[k] = v
            out.append((k, v))
        return out

    def _mark(s, sem, val, rd, wr, wrp):
        for b in rd:
            if b.r.get(sem, 0) < val:
                b.r[sem] = val
        for b in wr:
            pr = dict(b.w)
            for k_, v_ in b.r.items():
                if pr.get(k_, 0) < v_:
                    pr[k_] = v_
            b.pr = pr
            b.w = {sem: val}
            b.r = {}
        for b in wrp:
            if b.w.get(sem, 0) < val:
                b.w[sem] = val

    def op(s, q, fn, rd=(), wr=(), wrp=()):
        waits = s._waits(q, rd, wr, wrp)
        q.cnt += 1
        q.ops.append((waits, fn, (q.sem, 1)))
        s._mark(q.sem, q.cnt, rd, wr, wrp)
        s.nops += 1

    def dma(s, q, out, in_, rd=(), wr=(), wrp=(), sembuf=None, **kw):
        waits = s._waits(q, rd, wr, wrp)
        if sembuf is not None:
            if sembuf.sem is None:
                key = sembuf.key
                if key is not None and key in s.semcache:
                    sembuf.sem, sembuf.semv = s.semcache[key]
                else:
                    sembuf.sem = s.newsem("d_" + sembuf.name)
            sembuf.semv += 16
            sem, val = sembuf.sem, sembuf.semv
            if sembuf.key is not None:
                s.semcache[sembuf.key] = (sem, val)
        else:
            if q.shsem is None:
                q.shsem = s.newsem("sh_" + q.name)
            if q.shv > 0 and q.seen.get(q.shsem, 0) < q.shv:
                q.seen[q.shsem] = q.shv
                waits.append((q.shsem, q.shv))
            q.shv += 16
            sem, val = q.shsem, q.shv
        q.ops.append((waits, lambda e: e.dma_start(out=out, in_=in_, **kw), (sem, 16)))
        s._mark(sem, val, rd, wr, wrp)
        s.nops += 1

    def mmv(s, items, rd=(), wr=(), wrp=(), transpose=False):
        q = s.pe
        waits = s._waits(q, rd, wr, wrp, same_ok=True)
        q.cnt += 1

        def fn(e):
            ins = None
            for (o, l, r, st, sp) in items:
                if transpose:
                    ins = e.transpose(o, l, r)
                else:
                    ins = e.matmul(o, l, r, start=st, stop=sp)
            return ins

        q.ops.append((waits, fn, (q.sem, 1)))
        s._mark(q.sem, q.cnt, rd, wr, wrp)
        s.nops += len(items)

    def mm(s, out, pairs, rd=(), wr=()):
        n = len(pairs)
        s.mmv([(out, l, r, i == 0, i == n - 1) for i, (l, r) in enumerate(pairs)], rd=rd, wr=wr)

    def finish(s, final_bufs):
        nc = s.nc
        for q in s.qs:
            waits = s._waits(q, final_bufs, (), ())
            q.ops.append((waits, None, None))
        with nc.Block() as block:
            def emit(q):
                def body(e):
                    for waits, fn, inc in q.ops:
                        for k, v in waits:
                            e.wait_ge(s.sems[k], v)
                        if fn is not None:
                            ins = fn(e)
                            ins.then_inc(s.sems[inc[0]], inc[1])
                return body
            block.tensor(emit(s.pe))
            block.scalar(emit(s.act))
            block.vector(emit(s.dve))
            block.gpsimd(emit(s.pool))
            block.sync(emit(s.sp))


class Arena:
    def __init__(s, tensor, size):
        s.t = tensor
        s.size = size
        s.off = 0
        s.hist = []

    def alloc(s, name, shape, dt=F32):
        esz = 2 if dt == BF16 else 4
        n = int(np.prod(shape[1:])) * esz
        nal = (n + 63) // 64 * 64
        off = s.off
        s.off += nal
        assert s.off <= s.size, (name, s.off, s.size)
        ap = s.t[:, off:off + n].bitcast(dt)
        if len(shape) == 3:
            ap = ap.rearrange("p (a b) -> p a b", a=shape[1])
        if shape[0] < 128:
            ap = ap[0:shape[0]]
        b = Buf(name, ap)
        b.key = (off, n)
        keep = []
        for (o, e, ob) in s.hist:
            if o < off + nal and e > off:
                for d in (ob.w, ob.r):
                    for k, v in d.items():
                        if b.r.get(k, 0) < v:
                            b.r[k] = v
                if o >= off and e <= off + nal:
                    continue
            keep.append((o, e, ob))
        keep.append((off, off + nal, b))
        s.hist = keep
        return b

    def mark(s):
        return s.off

    def release(s, m):
        s.off = m


def _fft_consts(n):
    nJ = n // 128
    N = 2 * n
    F2 = nJ + 1
    J = np.arange(nJ)[:, None, None]
    jj = np.arange(128)[None, :, None]
    f2 = np.arange(F2)[None, None, :]
    ang = 2 * np.pi * ((f2 * (128 * J + jj)) % N) / N
    ma = np.concatenate([np.cos(ang), -np.sin(ang)], axis=2)
    w = np.full(F2, 2.0)
    w[0] = 1.0
    w[F2 - 1] = 1.0
    f2b = np.arange(F2)[:, None, None]
    tt = np.arange(128)[None, :, None]
    To = np.arange(nJ)[None, None, :]
    tf = 128 * (To + nJ // 2) + tt
    ang2 = 2 * np.pi * ((f2b * tf) % N) / N
    mi = np.concatenate([w[:, None, None] / N * np.cos(ang2), -w[:, None, None] / N * np.sin(ang2)], axis=0)
    return ma.astype(ml_dtypes.bfloat16), mi.astype(ml_dtypes.bfloat16)


def _filter_consts(n):
    t = np.linspace(0.0, 1.0, n, dtype=np.float32)[:, None]
    w = (np.float32(2.0 * math.pi / n) * np.arange(n, dtype=np.float32))[:, None]
    bands = np.linspace(1e-4, 15, 16, dtype=np.float32)[None, :]
    z = np.concatenate([t, np.cos(bands * w), -np.sin(bands * w)], axis=-1).astype(np.float32)
    centre = n // 2
    dist = (np.abs(np.arange(n) - centre).astype(np.float32) / np.float32(centre)).astype(np.float32)
    return np.ascontiguousarray(z.T), np.ascontiguousarray(dist.reshape(n // 128, 128).T)


def _consts():
    c = {}
    a = np.arange(128)
    ang = 2 * np.pi * ((a[:, None] * a[None, :]) % 128) / 128
    c["c_cos"] = np.cos(ang).astype(ml_dtypes.bfloat16)
    c["c_sin"] = np.sin(ang).astype(ml_dtypes.bfloat16)
    c["c_nsin"] = (-np.sin(ang)).astype(ml_dtypes.bfloat16)
    c["c_identf"] = np.eye(128, dtype=np.float32)
    c["c_identb"] = np.eye(128).astype(ml_dtypes.bfloat16)
    deltas = np.linspace(math.log(1e-2) / 1.5, math.log(1e-2) / 0.3, D, dtype=np.float32)
    c["c_nad"] = (-np.abs(deltas)).astype(np.float32)
    for n in (SEQ, CTX):
        ma, mi = _fft_consts(n)
        zt, dist = _filter_consts(n)
        c[f"c_ma{n}"] = ma
        c[f"c_mi{n}"] = mi
        c[f"c_zt{n}"] = zt
        c[f"c_dist{n}"] = dist
    return c


INPUT_NAMES = ['x', 'c', 'ctx', 'c_ctx', 'mod_w', 'mod_b', 'norm1_g', 'norm2_g', 'final_g',
               'rg_w_in', 'rg_conv_w', 'rg_conv_b', 'rg_w_a', 'rg_b_a', 'rg_w_i', 'rg_b_i', 'rg_lam', 'rg_w_out',
               'hy_w_in', 'hy_short_w', 'hy_short_b', 'hy_pe_w1', 'hy_pe_b1', 'hy_pe_w2', 'hy_pe_b2', 'hy_pe_w3',
               'hy_pe_b3', 'hy_pe_w4', 'hy_freq', 'hy_skip', 'hy_w_out',
               'ffn_w_up', 'ffn_conv_w', 'ffn_conv_b', 'ffn_w_down']

SHAPES = {
    'x': [2, SEQ, D], 'c': [2, D], 'ctx': [2, CTX, D], 'c_ctx': [D], 'mod_w': [4, D, 6 * D], 'mod_b': [4, 6 * D],
    'norm1_g': [4, D], 'norm2_g': [4, D], 'final_g': [D], 'rg_w_in': [2, D, 2 * D], 'rg_conv_w': [2, 4, D],
    'rg_conv_b': [2, D], 'rg_w_a': [2, 2, 4, 256, 256], 'rg_b_a': [2, 2, D], 'rg_w_i': [2, 2, 4, 256, 256],
    'rg_b_i': [2, 2, D], 'rg_lam': [2, 2, D], 'rg_w_out': [2, D, D], 'hy_w_in': [2, D, 3 * D],
    'hy_short_w': [2, 3, 3 * D], 'hy_short_b': [2, 3 * D], 'hy_pe_w1': [2, 33, 64], 'hy_pe_b1': [2, 64],
    'hy_pe_w2': [2, 64, 64], 'hy_pe_b2': [2, 64], 'hy_pe_w3': [2, 64, 64], 'hy_pe_b3': [2, 64],
    'hy_pe_w4': [2, 64, D], 'hy_freq': [2, 64], 'hy_skip': [2, D], 'hy_w_out': [2, D, D],
    'ffn_w_up': [4, D, 2 * DFF], 'ffn_conv_w': [4, 3, 2 * DFF], 'ffn_conv_b': [4, 2 * DFF], 'ffn_w_down': [4, DFF, D],
}


def build(depth=4, stop_after=None):
    nc = bass.Bass("TRN2", target_bir_lowering=False)
    es = ExitStack()
    consts = _consts()
    with es:
        k = KB(nc, es)
        IN = {}
        for nm in INPUT_NAMES:
            IN[nm] = Buf(nm, nc.dram_tensor(nm, SHAPES[nm], F32, kind="ExternalInput").ap())
        CI = {}
        for nm, arr in consts.items():
            dt = BF16 if arr.dtype == ml_dtypes.bfloat16 else F32
            CI[nm] = Buf(nm, nc.dram_tensor(nm, list(arr.shape), dt, kind="ExternalInput").ap())
        OUT = Buf("out", nc.dram_tensor("out", [2, SEQ, D], F32, kind="ExternalOutput").ap())

        arena_t = es.enter_context(nc.sbuf_tensor("arena", [128, 200 * 1024], U8))
        A = Arena(arena_t, 200 * 1024)
        PSt = [k.ps(f"ps{i}", [128, 1024]) for i in range(4)]
        PB = []
        for i in range(8):
            PB.append(Buf(f"bank{i}", PSt[i // 2][:, (i % 2) * 512:(i % 2) * 512 + 512]))

        def pbf(i):
            return PB[i].t.bitcast(BF16)

        XS = [[Buf(f"x{p}_{b}", None) for b in range(2)] for p in range(2)]
        SS = [[Buf(f"s{p}_{b}", None) for b in range(2)] for p in range(2)]
        for p in range(2):
            xt_ = nc.dram_tensor(f"xs{p}", [2, SEQ, D], F32, kind="Internal").ap()
            st_ = nc.dram_tensor(f"ss{p}", [2, CTX, D], F32, kind="Internal").ap()
            for b in range(2):
                XS[p][b].t = xt_[b]
                SS[p][b].t = st_[b]
        XIN = [Buf(f"xin{b}", IN['x'].t[b]) for b in range(2)]
        SIN = [Buf(f"sin{b}", IN['ctx'].t[b]) for b in range(2)]
        OUTB = [Buf(f"out{b}", OUT.t[b]) for b in range(2)]

        def wS(name, K, M):
            return k.dram(name, [M // 128, 128, K // 128, 128], BF16)

        def wN(name, K, M):
            return k.dram(name, [K, M], BF16)

        W = {}
        for j in range(2):
            W[f"rg_in{j}"] = wS(f"w_rg_in{j}", D, 2 * D)
            W[f"rg_out{j}"] = wN(f"w_rg_out{j}", D, D)
            for d_ in range(2):
                for h in range(4):
                    W[f"rg_a{j}{d_}{h}"] = wS(f"w_rg_a{j}{d_}{h}", 256, 256)
                    W[f"rg_i{j}{d_}{h}"] = wS(f"w_rg_i{j}{d_}{h}", 256, 256)
            W[f"hy_in{j}"] = wS(f"w_hy_in{j}", D, 3 * D)
            W[f"hy_out{j}"] = wN(f"w_hy_out{j}", D, D)
        for i in range(4):
            W[f"up{i}"] = wS(f"w_up{i}", D, 2 * DFF)
            W[f"down{i}"] = wN(f"w_down{i}", DFF, D)
        FM = [k.dram(f"fm{i}", [D, SEQ]) for i in range(3)]
        TOK = k.dram("tok", [SEQ, D], BF16)
        KTOK = k.dram("ktok", [SEQ, D], BF16)
        PD = k.dram("pd", [66, 128, D], BF16)
        QD = k.dram("qd", [128, 66, D], BF16)
        KF = {SEQ: k.dram("kf4096", [33, 2, 128, D]), CTX: k.dram("kf256", [3, 2, 128, D])}

        identf = A.alloc("identf", [128, 128])
        identb = A.alloc("identb", [128, 128], BF16)
        ones = A.alloc("ones", [128, 128])
        cactT = A.alloc("cactT", [128, 8, 3])
        MODT = [A.alloc(f"modt{i}", [128, D]) for i in range(6)]
        MOD3 = k.dram("mod3", [3, 6 * D])
        Y2 = k.dram("y2", [8, 128, SEQ], BF16)
        xio = [A.alloc(f"xio{i}", [128, D]) for i in range(4)]
        k.dma(k.sp, identf[:], CI["c_identf"][:], rd=[CI["c_identf"]], wr=[identf])
        k.dma(k.sp, identb[:], CI["c_identb"][:], rd=[CI["c_identb"]], wr=[identb])
        k.op(k.pool, I("memset", ap=ones[:], constant=1.0), wr=[ones])
        def row(ap):
            return ap.rearrange("(o n) -> o n", o=1)
        PERSIST = A.mark()
        xio_i = [0]

        def next_xio():
            b = xio[xio_i[0] % 4]
            xio_i[0] += 1
            return b

        rr = [0]

        def anyeng():
            rr[0] += 1
            return [k.act, k.dve, k.pool][rr[0] % 3]

        def cast_op(q, out, in_, rd, wr, wrp=()):
            if q is k.act:
                k.op(q, I("copy", out=out, in_=in_), rd=rd, wr=wr, wrp=wrp)
            else:
                k.op(q, I("tensor_copy", out=out, in_=in_), rd=rd, wr=wr, wrp=wrp)

        def cast_weight(src_buf, src2d, K, M, dst, layout, stf, stb, cnt):
            for kc in range(K // 128):
                sf = stf[cnt[0] % 2]
                sb_ = stb[cnt[0] % 2]
                cnt[0] += 1
                k.dma(k.sp, sf[:, :M], src2d[kc * 128:(kc + 1) * 128, :], rd=[src_buf], wr=[sf], sembuf=sf)
                cast_op(anyeng(), sb_[:, :M], sf[:, :M], [sf], [sb_])
                if layout == 'S':
                    for g0 in range(0, M // 128, 16):
                        g1 = min(M // 128, g0 + 16)
                        k.dma(k.pool, dst.t[g0:g1, :, kc, :].rearrange("m p i -> p m i"),
                              sb_[:, g0 * 128:g1 * 128].rearrange("p (m i) -> p m i", i=128), rd=[sb_], wrp=[dst], sembuf=sb_)
                else:
                    k.dma(k.pool, dst.t[kc * 128:(kc + 1) * 128, :], sb_[:, :M], rd=[sb_], wrp=[dst], sembuf=sb_)

        m0 = A.mark()
        stf = [A.alloc(f"stf{i}", [128, 2 * DFF]) for i in range(2)]
        stb = [A.alloc(f"stb{i}", [128, 2 * DFF], BF16) for i in range(2)]
        cnt = [0]
        for i in range(depth):
            j = i // 2
            if i % 2 == 0:
                cast_weight(IN['rg_w_in'], IN['rg_w_in'].t[j], D, 2 * D, W[f"rg_in{j}"], 'S', stf, stb, cnt)
                cast_weight(IN['rg_w_out'], IN['rg_w_out'].t[j], D, D, W[f"rg_out{j}"], 'N', stf, stb, cnt)
                for d_ in range(2):
                    for h in range(4):
                        cast_weight(IN['rg_w_a'], IN['rg_w_a'].t[j, d_, h], 256, 256, W[f"rg_a{j}{d_}{h}"], 'S', stf, stb, cnt)
                        cast_weight(IN['rg_w_i'], IN['rg_w_i'].t[j, d_, h], 256, 256, W[f"rg_i{j}{d_}{h}"], 'S', stf, stb, cnt)
            else:
                cast_weight(IN['hy_w_in'], IN['hy_w_in'].t[j], D, 3 * D, W[f"hy_in{j}"], 'S', stf, stb, cnt)
                cast_weight(IN['hy_w_out'], IN['hy_w_out'].t[j], D, D, W[f"hy_out{j}"], 'N', stf, stb, cnt)
            cast_weight(IN['ffn_w_up'], IN['ffn_w_up'].t[i], D, 2 * DFF, W[f"up{i}"], 'S', stf, stb, cnt)
            cast_weight(IN['ffn_w_down'], IN['ffn_w_down'].t[i], DFF, D, W[f"down{i}"], 'N', stf, stb, cnt)
        A.release(m0)

        m0 = A.mark()
        crow = A.alloc("crow", [3, D])
        k.dma(k.sp, crow[0:2, :], IN['c'][:, :], rd=[IN['c']], wr=[crow])
        k.dma(k.sp, crow[2:3, :], row(IN['c_ctx'].t), rd=[IN['c_ctx']], wrp=[crow])
        crow2 = A.alloc("crow2", [3, D])
        k.op(k.act, I("activation", out=crow2[:], in_=crow[:], func=AF.Silu), rd=[crow], wr=[crow2])
        k.mmv([(PB[0][:, kk * 3:kk * 3 + 3], crow2[0:3, kk * 128:(kk + 1) * 128], identf[0:3, 0:3], True, True) for kk in range(8)],
              rd=[crow2, identf], wr=[PB[0]], transpose=True)
        k.op(k.dve, I("tensor_copy", out=cactT[:].rearrange("p a b -> p (a b)"), in_=PB[0][:, 0:24]), rd=[PB[0]], wr=[cactT])
        A.release(m0)

        def load_cols(dst, rows):
            R = len(rows)
            ncols = rows[0][1].shape[0]
            nch = ncols // 128
            m_ = A.mark()
            st = A.alloc("lc_st", [R, ncols])
            for r, (sbuf_, ap) in enumerate(rows):
                k.dma(k.sp, st[r:r + 1, :], row(ap), rd=[sbuf_], wrp=[st])
            done = 0
            while done < nch:
                nb = min(nch - done, 512 // R)
                k.mmv([(PB[0][:, q_ * R:(q_ + 1) * R], st[0:R, (done + q_) * 128:(done + q_ + 1) * 128], identf[0:R, 0:R], True, True)
                       for q_ in range(nb)], rd=[st, identf], wr=[PB[0]], transpose=True)
                k.op(k.dve, I("tensor_copy", out=dst[:, done:done + nb, :].rearrange("p a b -> p (a b)"), in_=PB[0][:, 0:nb * R]),
                     rd=[PB[0]], wrp=[dst])
                done += nb
            A.release(m_)

        def compute_mod(layer):
            m_ = A.mark()
            mw = [A.alloc(f"mw{i}", [128, 8, 512]) for i in range(2)]
            mb = A.alloc("mb", [1, 6 * D])
            m3 = A.alloc("m3", [3, 6 * D])
            k.dma(k.sp, mb[:], row(IN['mod_b'].t[layer]), rd=[IN['mod_b']], wr=[mb])
            for ns in range(12):
                t = mw[ns % 2]
                k.dma(k.sp, t[:], IN['mod_w'].t[layer].rearrange("(kk p) n -> p kk n", p=128)[:, :, ns * 512:(ns + 1) * 512],
                      rd=[IN['mod_w']], wr=[t], sembuf=t)
                bank = PB[ns % 2]
                pairs = [(cactT[:, kk, :], t[:, kk, :]) for kk in range(8)] + [(ones[0:1, 0:3], mb[0:1, ns * 512:(ns + 1) * 512])]
                k.mm(bank[0:3, :], pairs, rd=[cactT, t, ones, mb], wr=[bank])
                k.op(k.act, I("copy", out=m3[:, ns * 512:(ns + 1) * 512], in_=bank[0:3, :]), rd=[bank], wrp=[m3])
            k.dma(k.pool, MOD3[:, :], m3[:], rd=[m3], wr=[MOD3])
            A.release(m_)

        def bcast_mod(layer, r):
            m_ = A.mark()
            gn = A.alloc("gn", [128, D])
            for part in range(6):
                kind = part % 3
                dst = MODT[(part // 3) * 3 + {0: 1, 1: 0, 2: 2}[kind]]
                k.dma(k.sp, dst[:], MOD3.t[r, part * D:(part + 1) * D].partition_broadcast(128), rd=[MOD3], wr=[dst])
                if kind == 1:
                    nm = 'norm1_g' if part < 3 else 'norm2_g'
                    k.dma(k.sp, gn[:], IN[nm].t[layer].partition_broadcast(128), rd=[IN[nm]], wr=[gn])
                    k.op(k.dve, I("scalar_tensor_tensor", out=dst[:], in0=dst[:], scalar=1.0, in1=gn[:], op0=ALU.add, op1=ALU.mult),
                         rd=[dst, gn], wr=[dst])
            A.release(m_)

        def norm_to_hnT(xsrc, L, Amod, Bmod, hnT, npad):
            m_ = A.mark()
            junk = A.alloc("junk", [128, D], BF16)
            hnb = [A.alloc(f"hnb{i}", [128, D], BF16) for i in range(2)]
            tt0 = A.alloc("ntmp0", [128, D])
            tt_ = [tt0, tt0]
            ss = [A.alloc(f"ss{i}", [128, 1]) for i in range(2)]
            sd = [A.alloc(f"sd{i}", [128, 1]) for i in range(2)]
            rs = [A.alloc(f"rs{i}", [128, 1]) for i in range(2)]
            k.op(k.pool, I("memset", ap=hnT[:, :, 0:1], constant=0.0), wrp=[hnT])
            k.op(k.pool, I("memset", ap=hnT[:, :, L + 1:L + 1 + npad], constant=0.0), wrp=[hnT])
            for i in range(L // 128):
                xt = next_xio()
                k.dma(k.sp, xt[:], xsrc.t[i * 128:(i + 1) * 128, :], rd=[xsrc], wr=[xt], sembuf=xt)
                p = i % 2
                k.op(k.act, I("activation", out=junk[:], in_=xt[:], func=AF.Square, accum_out=ss[p][:]), rd=[xt], wr=[junk, ss[p]])
                k.op(k.act, I("activation", out=sd[p][:], in_=ss[p][:], func=AF.Sqrt, scale=1.0 / D, bias=1e-6), rd=[ss[p]], wr=[sd[p]])
                k.op(k.dve, I("reciprocal", out=rs[p][:], in_=sd[p][:]), rd=[sd[p]], wr=[rs[p]])
                k.op(k.dve, I("scalar_tensor_tensor", out=tt_[p][:], in0=xt[:], scalar=rs[p][:], in1=Amod[:], op0=ALU.mult, op1=ALU.mult),
                     rd=[xt, rs[p], Amod], wr=[tt_[p]])
                k.op(k.pool, I("tensor_tensor", out=hnb[p][:], in0=tt_[p][:], in1=Bmod[:], op=ALU.add), rd=[tt_[p], Bmod], wr=[hnb[p]])
                bank = PB[6 + p]
                pv = pbf(6 + p)
                k.mmv([(pv[:, kk * 128:(kk + 1) * 128], hnb[p][:, kk * 128:(kk + 1) * 128], identb[:], True, True) for kk in range(8)],
                      rd=[hnb[p], identb], wr=[bank], transpose=True)
                q = k.act if i % 2 == 0 else k.dve
                cast_op(q, hnT[:, :, 1 + i * 128:1 + (i + 1) * 128], pv[:, :].rearrange("p (a b) -> p a b", a=8), [bank], (), [hnT])
            A.release(m_)

        def conv_taps(q_act, bank, ncols, nv, off1, taps, cw, cidx, bias_ap, dst):
            k.op(k.act, I("activation", out=dst[:, :nv], in_=bank[:, off1:off1 + nv], func=AF.Identity,
                          scale=cw[:, cidx, taps[off1]:taps[off1] + 1], bias=bias_ap), rd=[bank, cw], wr=[dst])
            for t in range(len(taps)):
                if t == off1:
                    continue
                k.op(k.dve, I("scalar_tensor_tensor", out=dst[:, :nv], in0=bank[:, t:t + nv], scalar=cw[:, cidx, taps[t]:taps[t] + 1],
                              in1=dst[:, :nv], op0=ALU.mult, op1=ALU.add), rd=[bank, cw, dst], wr=[dst])

        def residual_out(po_banks, nt, xsrc, xdst, row0, Gmod, dst_is_final=False):
            xt = next_xio()
            k.dma(k.sp, xt[:nt, :], xsrc.t[row0:row0 + nt, :], rd=[xsrc], wr=[xt], sembuf=xt)
            xo = next_xio()
            for h in range(2):
                hs = slice(h * 512, (h + 1) * 512)
                k.op(k.dve, I("tensor_tensor", out=xo[:nt, hs], in0=po_banks[h][:nt, :], in1=Gmod[:nt, hs], op=ALU.mult),
                     rd=[po_banks[h], Gmod], wr=[xo] if h == 0 else (), wrp=[xo] if h == 1 else ())
            k.op(k.pool, I("tensor_tensor", out=xo[:nt, :], in0=xo[:nt, :], in1=xt[:nt, :], op=ALU.add), rd=[xt, xo], wr=[xo])
            k.dma(k.pool, xdst.t[row0:row0 + nt, :], xo[:nt, :], rd=[xo], wrp=[xdst], sembuf=xo)

        def ffn_layer_consts(layer):
            cw = A.alloc("ffn_cw", [128, 44, 4])
            load_cols(cw, [(IN['ffn_conv_w'], IN['ffn_conv_w'].t[layer, t]) for t in range(3)] + [(IN['ffn_conv_b'], IN['ffn_conv_b'].t[layer])])
            wd = A.alloc("ffn_wd", [128, 22, D], BF16)
            k.dma(k.sp, wd[:], W[f"down{layer}"].t.rearrange("(j p) n -> p j n", p=128), rd=[W[f"down{layer}"]], wr=[wd])
            return cw, wd

        def ffn_seq(layer, xsrc, xdst, L, cw, wd):
            m_ = A.mark()
            wg = [A.alloc(f"wg{i}", [128, 8, 128], BF16) for i in range(2)]
            wu = [A.alloc(f"wu{i}", [128, 8, 128], BF16) for i in range(2)]
            tg = [A.alloc(f"tg{i}", [128, 512]) for i in range(2)]
            tu = [A.alloc(f"tu{i}", [128, 512]) for i in range(2)]
            sg = tg
            act = A.alloc("act", [128, 22, 512], BF16)
            hnT = A.alloc("hnT", [128, 8, L + 2], BF16)
            norm_to_hnT(xsrc, L, MODT[3], MODT[4], hnT, 1)
            Wup = W[f"up{layer}"]
            it = 0
            for w0 in range(0, L, 510):
                ncols = min(512, L + 2 - w0)
                nv = ncols - 2
                for j in range(22):
                    p = it % 2
                    it += 1
                    k.dma(k.sp, wg[p][:], Wup.t[j], rd=[Wup], wr=[wg[p]], sembuf=wg[p])
                    k.dma(k.sp, wu[p][:], Wup.t[22 + j], rd=[Wup], wr=[wu[p]], sembuf=wu[p])
                    bg, bu = PB[2 * p], PB[2 * p + 1]
                    k.mm(bg[:, :ncols], [(wg[p][:, kk, :], hnT[:, kk, w0:w0 + ncols]) for kk in range(8)], rd=[wg[p], hnT], wr=[bg])
                    k.mm(bu[:, :ncols], [(wu[p][:, kk, :], hnT[:, kk, w0:w0 + ncols]) for kk in range(8)], rd=[wu[p], hnT], wr=[bu])
                    conv_taps(k.act, bg, ncols, nv, 1, [0, 1, 2], cw, j, cw[:, j, 3:4], tg[p])
                    conv_taps(k.act, bu, ncols, nv, 1, [0, 1, 2], cw, 22 + j, cw[:, 22 + j, 3:4], tu[p])
                    k.op(k.act, I("activation", out=sg[p][:, :nv], in_=tg[p][:, :nv], func=AF.Silu), rd=[tg[p]], wr=[sg[p]])
                    k.op(k.pool, I("tensor_tensor", out=act[:, j, :nv], in0=sg[p][:, :nv], in1=tu[p][:, :nv], op=ALU.mult),
                         rd=[sg[p], tu[p]], wrp=[act])
                for kt in range((nv + 127) // 128):
                    nt = min(128, nv - kt * 128)
                    pb = [PB[4 + 2 * (kt % 2)], PB[5 + 2 * (kt % 2)]]
                    for h in range(2):
                        k.mm(pb[h][:nt, :], [(act[:, j, kt * 128:kt * 128 + nt], wd[:, j, h * 512:(h + 1) * 512]) for j in range(22)],
                             rd=[act, wd], wr=[pb[h]])
                    residual_out(pb, nt, xsrc, xdst, w0 + kt * 128, MODT[5])
            A.release(m_)

        def rg_layer_consts(j):
            cw = A.alloc("rg_cw", [128, 8, 5])
            load_cols(cw, [(IN['rg_conv_w'], IN['rg_conv_w'].t[j, t]) for t in range(4)] + [(IN['rg_conv_b'], IN['rg_conv_b'].t[j])])
            gv = A.alloc("rg_gv", [128, 8, 6])
            load_cols(gv, [(IN[nm], IN[nm].t[j, d_]) for d_ in range(2) for nm in ('rg_b_a', 'rg_b_i', 'rg_lam')])
            cl = A.alloc("rg_cl", [128, 8, 2])
            tmp = A.alloc("rg_cltmp", [128, 8, 2])
            for d_ in range(2):
                k.op(k.act, I("activation", out=tmp[:, :, d_:d_ + 1], in_=gv[:, :, d_ * 3 + 2:d_ * 3 + 3], func=AF.Exp, scale=-1.0),
                     rd=[gv], wrp=[tmp])
            k.op(k.act, I("activation", out=cl[:], in_=tmp[:], func=AF.Ln, scale=1.0, bias=1.0), rd=[tmp], wr=[cl])
            k.op(k.dve, I("tensor_scalar", out=cl[:], in0=cl[:], scalar1=-8.0, scalar2=0.0, op0=ALU.mult, op1=ALU.add), rd=[cl], wr=[cl])
            wa = A.alloc("rg_wa", [128, 64, 128], BF16)
            wi = A.alloc("rg_wi", [128, 64, 128], BF16)
            for d_ in range(2):
                for h in range(4):
                    for mo in range(2):
                        b0 = ((d_ * 4 + h) * 2 + mo) * 2
                        k.dma(k.sp, wa[:, b0:b0 + 2, :], W[f"rg_a{j}{d_}{h}"].t[mo], rd=[W[f"rg_a{j}{d_}{h}"]], wrp=[wa])
                        k.dma(k.sp, wi[:, b0:b0 + 2, :], W[f"rg_i{j}{d_}{h}"].t[mo], rd=[W[f"rg_i{j}{d_}{h}"]], wrp=[wi])
            return cw, gv, cl, wa, wi

        def rg_gates(d_, m, nv, ub, gv, cl, wa, wi, uf, st, rev, bufs, it):
            p = it % 2
            h, mo = m // 2, m % 2
            b0 = ((d_ * 4 + h) * 2 + mo) * 2
            br, bi = PB[2 * p], PB[2 * p + 1]
            k.mm(br[:, :nv], [(wa[:, b0 + kk, :], ub[:, 2 * h + kk, :nv]) for kk in range(2)], rd=[wa, ub], wr=[br])
            k.mm(bi[:, :nv], [(wi[:, b0 + kk, :], ub[:, 2 * h + kk, :nv]) for kk in range(2)], rd=[wi, ub], wr=[bi])
            r_, g_, t_, h_ = bufs[p]
            a_ = r_
            k.op(k.act, I("activation", out=r_[:, :nv], in_=br[:, :nv], func=AF.Sigmoid, bias=gv[:, m, d_ * 3:d_ * 3 + 1], scale=1.0), rd=[br, gv], wr=[r_])
            k.op(k.act, I("activation", out=a_[:, :nv], in_=r_[:, :nv], func=AF.Exp, scale=cl[:, m, d_:d_ + 1]), rd=[r_, cl], wr=[a_])
            k.op(k.act, I("activation", out=g_[:, :nv], in_=bi[:, :nv], func=AF.Sigmoid, bias=gv[:, m, d_ * 3 + 1:d_ * 3 + 2], scale=1.0), rd=[bi, gv], wr=[g_])
            k.op(k.pool, I("tensor_tensor", out=t_[:, :nv], in0=a_[:, :nv], in1=a_[:, :nv], op=ALU.mult), rd=[a_], wr=[t_])
            k.op(k.act, I("activation", out=t_[:, :nv], in_=t_[:, :nv], func=AF.Sqrt, scale=-1.0, bias=1.0), rd=[t_], wr=[t_])
            k.op(k.pool, I("tensor_tensor", out=g_[:, :nv], in0=g_[:, :nv], in1=uf[:, m, :nv], op=ALU.mult), rd=[g_, uf], wr=[g_])
            k.op(k.dve, I("tensor_tensor", out=t_[:, :nv], in0=t_[:, :nv], in1=g_[:, :nv], op=ALU.mult), rd=[t_, g_], wr=[t_])
            if not rev:
                k.op(k.dve, I("tensor_tensor_scan", out=h_[:, :nv], data0=a_[:, :nv], data1=t_[:, :nv], initial=st[:, m, d_:d_ + 1],
                              op0=ALU.mult, op1=ALU.add), rd=[a_, t_, st], wr=[h_])
                k.op(k.act, I("copy", out=st[:, m, d_:d_ + 1], in_=h_[:, nv - 1:nv]), rd=[h_], wr=[st])
            else:
                k.op(k.dve, I("tensor_tensor_scan", out=h_[:, :nv][:, ::-1], data0=a_[:, :nv][:, ::-1],
                              data1=t_[:, :nv][:, ::-1], initial=st[:, m, d_:d_ + 1], op0=ALU.mult, op1=ALU.add), rd=[a_, t_, st], wr=[h_])
                k.op(k.act, I("copy", out=st[:, m, d_:d_ + 1], in_=h_[:, 0:1]), rd=[h_], wr=[st])
            return h_

        def rg_seq(j, xsrc, xdst, L, consts_, st, write_out):
            cw, gv, cl, wa, wi = consts_
            Win = W[f"rg_in{j}"]
            GG, UU, HF = FM
            m_ = A.mark()
            wch = [A.alloc(f"rgw{i}", [128, 8, 128], BF16) for i in range(2)]
            ggt = [A.alloc(f"ggt{i}", [128, 512]) for i in range(2)]
            uf = A.alloc("uf", [128, 8, 512])
            ub = A.alloc("ub", [128, 8, 512], BF16)
            bufs = [[A.alloc(f"rgb{i}_{q}", [128, 512]) for q in range(4)] for i in range(2)]
            wins = []
            for w0 in range(0, L, 509):
                ncols = min(512, L + 3 - w0)
                wins.append((w0, ncols, ncols - 3))
            it = 0
            m1 = A.mark()
            hnT = A.alloc("hnT", [128, 8, L + 3], BF16)
            norm_to_hnT(xsrc, L, MODT[0], MODT[1], hnT, 2)
            for (w0, ncols, nv) in wins:
                for m in range(16):
                    p = it % 2
                    it += 1
                    k.dma(k.sp, wch[p][:], Win.t[m], rd=[Win], wr=[wch[p]], sembuf=wch[p])
                    bank = PB[4 + p]
                    k.mm(bank[:, :ncols], [(wch[p][:, kk, :], hnT[:, kk, w0:w0 + ncols]) for kk in range(8)], rd=[wch[p], hnT], wr=[bank])
                    if m < 8:
                        g = ggt[p]
                        k.op(k.act, I("activation", out=g[:, :nv], in_=bank[:, 1:1 + nv], func=AF.Gelu), rd=[bank], wr=[g])
                        k.dma(k.pool, GG.t[m * 128:(m + 1) * 128, w0:w0 + nv], g[:, :nv], rd=[g], wrp=[GG], sembuf=g)
                    else:
                        mm_ = m - 8
                        k.op(k.act, I("activation", out=uf[:, mm_, :nv], in_=bank[:, 1:1 + nv], func=AF.Identity,
                                      scale=cw[:, mm_, 1:2], bias=cw[:, mm_, 4:5]), rd=[bank, cw], wr=[uf] if mm_ == 0 else (), wrp=[uf] if mm_ else ())
                        for t in (0, 2, 3):
                            k.op(k.dve, I("scalar_tensor_tensor", out=uf[:, mm_, :nv], in0=bank[:, t:t + nv], scalar=cw[:, mm_, t:t + 1],
                                          in1=uf[:, mm_, :nv], op0=ALU.mult, op1=ALU.add), rd=[bank, cw, uf], wrp=[uf])
                k.op(k.pool, I("tensor_copy", out=ub[:, :, :nv], in_=uf[:, :, :nv]), rd=[uf], wr=[ub])
                k.dma(k.pool, UU.t.rearrange("(m p) t -> p m t", p=128)[:, :, w0:w0 + nv], uf[:, :, :nv], rd=[uf], wrp=[UU], sembuf=uf)
                for m in range(8):
                    h_ = rg_gates(0, m, nv, ub, gv, cl, wa, wi, uf, st, False, bufs, it)
                    it += 1
                    k.dma(k.pool, HF.t[m * 128:(m + 1) * 128, w0:w0 + nv], h_[:, :nv], rd=[h_], wrp=[HF], sembuf=h_)
            A.release(m1)
            m1 = A.mark()
            hfl = [A.alloc(f"hfl{i}", [128, 512]) for i in range(2)]
            wo = A.alloc("rg_wo", [128, 8, D], BF16)
            yb = A.alloc("yb", [128, 8, 512], BF16)
            if write_out:
                k.dma(k.sp, wo[:], W[f"rg_out{j}"].t.rearrange("(kk p) n -> p kk n", p=128), rd=[W[f"rg_out{j}"]], wr=[wo])
            for (w0, ncols, nv) in reversed(wins):
                k.dma(k.sp, uf[:, :, :nv], UU.t.rearrange("(m p) t -> p m t", p=128)[:, :, w0:w0 + nv], rd=[UU], wr=[uf], sembuf=uf)
                k.op(k.pool, I("tensor_copy", out=ub[:, :, :nv], in_=uf[:, :, :nv]), rd=[uf], wr=[ub])
                for m in range(8):
                    h_ = rg_gates(1, m, nv, ub, gv, cl, wa, wi, uf, st, True, bufs, it)
                    p = it % 2
                    it += 1
                    if write_out:
                        k.dma(k.sp, hfl[p][:, :nv], HF.t[m * 128:(m + 1) * 128, w0:w0 + nv], rd=[HF], wr=[hfl[p]], sembuf=hfl[p])
                        k.dma(k.sp, ggt[p][:, :nv], GG.t[m * 128:(m + 1) * 128, w0:w0 + nv], rd=[GG], wr=[ggt[p]], sembuf=ggt[p])
                        k.op(k.pool, I("tensor_tensor", out=hfl[p][:, :nv], in0=hfl[p][:, :nv], in1=h_[:, :nv], op=ALU.add), rd=[hfl[p], h_], wr=[hfl[p]])
                        k.op(k.dve, I("tensor_tensor", out=yb[:, m, :nv], in0=hfl[p][:, :nv], in1=ggt[p][:, :nv], op=ALU.mult),
                             rd=[hfl[p], ggt[p]], wr=[yb] if m == 0 else (), wrp=[yb] if m else ())
                if write_out:
                    for kt in range((nv + 127) // 128):
                        nt = min(128, nv - kt * 128)
                        pb = [PB[4 + 2 * (kt % 2)], PB[5 + 2 * (kt % 2)]]
                        for h in range(2):
                            k.mm(pb[h][:nt, :], [(yb[:, m, kt * 128:kt * 128 + nt], wo[:, m, h * 512:(h + 1) * 512]) for m in range(8)],
                                 rd=[yb, wo], wr=[pb[h]])
                        residual_out(pb, nt, xsrc, xdst, w0 + kt * 128, MODT[2])
            A.release(m1)
            A.release(m_)

        def fft_fwd(src_tok, n, filter_mode, kf, consts_fft):
            Cc, Cs, Cns = consts_fft
            nJ = n // 128
            F2 = nJ + 1
            NF = 2 * F2
            m_ = A.mark()
            ua = [A.alloc(f"ua{i}", [nJ, 8, D], BF16) for i in range(2)]
            pa = [A.alloc(f"pa{i}", [NF, D], BF16) for i in range(2)]
            ma = A.alloc("ma", [nJ, 128, NF], BF16)
            k.dma(k.sp, ma[:], CI[f"c_ma{n}"][:, :, :], rd=[CI[f"c_ma{n}"]], wr=[ma])
            srcv = src_tok.t[0:n, :].rearrange("(J q) c -> J q c", q=128)
            for jc in range(16):
                u = ua[jc % 2]
                k.dma(k.sp, u[:], srcv[:, jc * 8:(jc + 1) * 8, :], rd=[src_tok], wr=[u], sembuf=u)
                for jl in range(8):
                    jj = jc * 8 + jl
                    p = jj % 2
                    banks = [PB[2 * p], PB[2 * p + 1]]
                    for h in range(2):
                        k.mm(banks[h][0:NF, :], [(ma[:, jj, :], u[:, jl, h * 512:(h + 1) * 512])], rd=[ma, u], wr=[banks[h]])
                    q = k.act if jj % 2 == 0 else k.dve
                    for h in range(2):
                        cast_op(q, pa[p][:, h * 512:(h + 1) * 512], banks[h][0:NF, :], [banks[h]], [pa[p]] if h == 0 else (), [pa[p]] if h else ())
                    k.dma(k.pool, PD.t[0:NF, jj, :], pa[p][:], rd=[pa[p]], wrp=[PD], sembuf=pa[p])
            A.release(m_)
            m_ = A.mark()
            pbt = [A.alloc(f"pbt{i}", [128, 2, D], BF16) for i in range(2)]
            if filter_mode:
                xo = [A.alloc(f"kfo{i}", [128, 2, D]) for i in range(2)]
            else:
                kft = [A.alloc(f"kft{i}", [128, 2, D]) for i in range(2)]
                tm = [A.alloc(f"ytm{i}", [128, D]) for i in range(4)]
                yy = [A.alloc(f"yy{i}", [128, 2, D], BF16) for i in range(2)]
                qt = [A.alloc(f"qt{i}", [128, 2, D], BF16) for i in range(2)]
            for f2 in range(F2):
                p = f2 % 2
                t = pbt[p]
                k.dma(k.sp, t[:, 0, :], PD.t[f2], rd=[PD], wr=[t], sembuf=t)
                k.dma(k.sp, t[:, 1, :], PD.t[F2 + f2], rd=[PD], wrp=[t], sembuf=t)
                for h in range(2):
                    hs = slice(h * 512, (h + 1) * 512)
                    k.mm(PB[h][:, :], [(Cc[:], t[:, 0, hs]), (Cs[:], t[:, 1, hs])], rd=[Cc, Cs, t], wr=[PB[h]])
                    k.mm(PB[2 + h][:, :], [(Cc[:], t[:, 1, hs]), (Cns[:], t[:, 0, hs])], rd=[Cc, Cns, t], wr=[PB[2 + h]])
                if filter_mode:
                    o = xo[p]
                    for ri in range(2):
                        for h in range(2):
                            hs = slice(h * 512, (h + 1) * 512)
                            first = (ri == 0 and h == 0)
                            k.op(k.act if h == 0 else k.dve, I("copy" if h == 0 else "tensor_copy", out=o[:, ri, hs], in_=PB[2 * ri + h][:, :]),
                                 rd=[PB[2 * ri + h]], wr=[o] if first else (), wrp=() if first else [o])
                    k.dma(k.pool, kf.t[f2].rearrange("r p c -> p r c"), o[:], rd=[o], wrp=[kf], sembuf=o)
                else:
                    kt_ = kft[p]
                    k.dma(k.sp, kt_[:], kf.t[f2].rearrange("r p c -> p r c"), rd=[kf], wr=[kt_], sembuf=kt_)
                    y = yy[p]
                    for h in range(2):
                        hs = slice(h * 512, (h + 1) * 512)
                        xr, xi = PB[h], PB[2 + h]
                        k.op(k.dve, I("tensor_tensor", out=tm[0][:, hs], in0=xr[:, :], in1=kt_[:, 0, hs], op=ALU.mult), rd=[xr, kt_], wr=[tm[0]] if h == 0 else (), wrp=[tm[0]] if h else ())
                        k.op(k.dve, I("tensor_tensor", out=tm[1][:, hs], in0=xi[:, :], in1=kt_[:, 1, hs], op=ALU.mult), rd=[xi, kt_], wr=[tm[1]] if h == 0 else (), wrp=[tm[1]] if h else ())
                        k.op(k.dve, I("tensor_tensor", out=tm[2][:, hs], in0=xr[:, :], in1=kt_[:, 1, hs], op=ALU.mult), rd=[xr, kt_], wr=[tm[2]] if h == 0 else (), wrp=[tm[2]] if h else ())
                        k.op(k.dve, I("tensor_tensor", out=tm[3][:, hs], in0=xi[:, :], in1=kt_[:, 0, hs], op=ALU.mult), rd=[xi, kt_], wr=[tm[3]] if h == 0 else (), wrp=[tm[3]] if h else ())
                    k.op(k.pool, I("tensor_tensor", out=y[:, 0, :], in0=tm[0][:], in1=tm[1][:], op=ALU.subtract), rd=[tm[0], tm[1]], wr=[y])
                    k.op(k.pool, I("tensor_tensor", out=y[:, 1, :], in0=tm[2][:], in1=tm[3][:], op=ALU.add), rd=[tm[2], tm[3]], wrp=[y])
                    for h in range(2):
                        hs = slice(h * 512, (h + 1) * 512)
                        k.mm(PB[4 + h][:, :], [(Cc[:], y[:, 0, hs]), (Cns[:], y[:, 1, hs])], rd=[Cc, Cns, y], wr=[PB[4 + h]])
                        k.mm(PB[6 + h][:, :], [(Cc[:], y[:, 1, hs]), (Cs[:], y[:, 0, hs])], rd=[Cc, Cs, y], wr=[PB[6 + h]])
                    q_ = qt[p]
                    for ri in range(2):
                        for h in range(2):
                            hs = slice(h * 512, (h + 1) * 512)
                            first = (ri == 0 and h == 0)
                            k.op(k.act, I("copy", out=q_[:, ri, hs], in_=PB[4 + 2 * ri + h][:, :]), rd=[PB[4 + 2 * ri + h]],
                                 wr=[q_] if first else (), wrp=() if first else [q_])
                    k.dma(k.pool, QD.t[:, f2, :], q_[:, 0, :], rd=[q_], wrp=[QD], sembuf=q_)
                    k.dma(k.pool, QD.t[:, F2 + f2, :], q_[:, 1, :], rd=[q_], wrp=[QD], sembuf=q_)
            A.release(m_)

        def hy_filter(j, n, consts_fft, invl1):
            nJ = n // 128
            m_ = A.mark()
            zt = A.alloc("zt", [33, n])
            k.dma(k.sp, zt[:], CI[f"c_zt{n}"][:, :], rd=[CI[f"c_zt{n}"]], wr=[zt])
            dist = A.alloc("dist", [128, nJ])
            k.dma(k.sp, dist[:], CI[f"c_dist{n}"][:, :], rd=[CI[f"c_dist{n}"]], wr=[dist])
            nad = A.alloc("nad", [128, D])
            k.dma(k.sp, nad[:], CI["c_nad"].t.partition_broadcast(128), rd=[CI["c_nad"]], wr=[nad])
            w1 = A.alloc("pw1", [33, 64])
            k.dma(k.sp, w1[:], IN['hy_pe_w1'].t[j], rd=[IN['hy_pe_w1']], wr=[w1])
            w2 = A.alloc("pw2", [64, 64])
            k.dma(k.sp, w2[:], IN['hy_pe_w2'].t[j], rd=[IN['hy_pe_w2']], wr=[w2])
            w3 = A.alloc("pw3", [64, 64])
            k.dma(k.sp, w3[:], IN['hy_pe_w3'].t[j], rd=[IN['hy_pe_w3']], wr=[w3])
            w4 = A.alloc("pw4", [64, D])
            k.dma(k.sp, w4[:], IN['hy_pe_w4'].t[j], rd=[IN['hy_pe_w4']], wr=[w4])
            fv = A.alloc("fv", [128, 1, 4])
            st = A.alloc("fst", [4, 64])
            for r, nm in enumerate(('hy_freq', 'hy_pe_b1', 'hy_pe_b2', 'hy_pe_b3')):
                k.dma(k.sp, st[r:r + 1, :], row(IN[nm].t[j]), rd=[IN[nm]], wrp=[st])
            k.mmv([(PB[0][0:64, 0:4], st[0:4, 0:64], identf[0:4, 0:4], True, True)], rd=[st, identf], wr=[PB[0]], transpose=True)
            k.op(k.dve, I("tensor_copy", out=fv[0:64, 0, :], in_=PB[0][0:64, 0:4]), rd=[PB[0]], wr=[fv])
            sc = A.alloc("fsc", [64, 4])
            k.op(k.dve, I("tensor_scalar", out=sc[:, 0:1], in0=fv[0:64, 0, 0:1], scalar1=1.0 / TWO_PI, scalar2=0.0, op0=ALU.mult, op1=ALU.add), rd=[fv], wr=[sc])
            for l in range(1, 4):
                k.op(k.dve, I("tensor_tensor", out=sc[:, l:l + 1], in0=fv[0:64, 0, l:l + 1], in1=sc[:, 0:1], op=ALU.mult), rd=[fv, sc], wr=[sc])
            hT = [A.alloc(f"hT{i}", [64, n]) for i in range(2)]
            t1 = A.alloc("ft1", [64, 512])
            ti = A.alloc("fti", [64, 512], I32)
            t2 = A.alloc("ft2", [64, 512])
            for cw_ in range(n // 512 if n >= 512 else 1):
                nc_ = min(512, n)
                cs = slice(cw_ * 512, cw_ * 512 + nc_)
                for l in range(3):
                    bank = PB[l % 2]
                    if l == 0:
                        k.mm(bank[0:64, :nc_], [(w1[:], zt[:, cs])], rd=[w1, zt], wr=[bank])
                    else:
                        wl = w2 if l == 1 else w3
                        k.mm(bank[0:64, :nc_], [(wl[:], hT[(l - 1) % 2][:, cs])], rd=[wl, hT[(l - 1) % 2]], wr=[bank])
                    k.op(k.dve, I("tensor_scalar", out=t1[:, :nc_], in0=bank[0:64, :nc_], scalar1=sc[:, 0:1], scalar2=sc[:, l + 1:l + 2],
                                  op0=ALU.mult, op1=ALU.add), rd=[bank, sc], wr=[t1])
                    k.op(k.dve, I("tensor_copy", out=ti[:, :nc_], in_=t1[:, :nc_]), rd=[t1], wr=[ti])
                    k.op(k.pool, I("tensor_copy", out=t2[:, :nc_], in_=ti[:, :nc_]), rd=[ti], wr=[t2])
                    k.op(k.dve, I("tensor_tensor", out=t1[:, :nc_], in0=t1[:, :nc_], in1=t2[:, :nc_], op=ALU.subtract), rd=[t1, t2], wr=[t1])
                    dst = hT[l % 2]
                    k.op(k.act, I("activation", out=dst[:, cs], in_=t1[:, :nc_], func=AF.Sin, scale=TWO_PI), rd=[t1], wrp=[dst])
            h3 = hT[0]
            kw = [A.alloc(f"kw{i}", [128, D]) for i in range(2)]
            win = [A.alloc(f"win{i}", [128, D]) for i in range(2)]
            kwb = [A.alloc(f"kwb{i}", [128, D], BF16) for i in range(2)]
            l1row = A.alloc("l1row", [1, D])
            for a in range(nJ):
                p = a % 2
                for h in range(2):
                    k.mm(PB[2 + h][:, :], [(h3[:, a * 128:(a + 1) * 128], w4[:, h * 512:(h + 1) * 512])], rd=[h3, w4], wr=[PB[2 + h]])
                k.op(k.act, I("activation", out=win[p][:], in_=nad[:], func=AF.Exp, scale=dist[:, a:a + 1]), rd=[nad, dist], wr=[win[p]])
                for h in range(2):
                    hs = slice(h * 512, (h + 1) * 512)
                    k.op(k.dve, I("tensor_tensor", out=kw[p][:, hs], in0=PB[2 + h][:, :], in1=win[p][:, hs], op=ALU.mult), rd=[PB[2 + h], win[p]],
                         wr=[kw[p]] if h == 0 else (), wrp=[kw[p]] if h else ())
                k.op(k.pool, I("tensor_copy", out=kwb[p][:], in_=kw[p][:]), rd=[kw[p]], wr=[kwb[p]])
                k.dma(k.pool, KTOK.t[a * 128:(a + 1) * 128, :], kwb[p][:], rd=[kwb[p]], wrp=[KTOK], sembuf=kwb[p])
                k.op(k.act, I("activation", out=win[p][:], in_=kw[p][:], func=AF.Abs), rd=[kw[p]], wr=[win[p]])
                for h in range(2):
                    k.mmv([(PB[4 + h][0:1, :], ones[:, 0:1], win[p][:, h * 512:(h + 1) * 512], a == 0, a == nJ - 1)], rd=[ones, win[p]],
                          wr=[PB[4 + h]] if a == 0 else (), wrp=() if a == 0 else [PB[4 + h]])
            for h in range(2):
                k.op(k.act, I("copy", out=l1row[:, h * 512:(h + 1) * 512], in_=PB[4 + h][0:1, :]), rd=[PB[4 + h]], wr=[l1row] if h == 0 else (), wrp=[l1row] if h else ())
            k.mmv([(PB[0][:, m:m + 1], l1row[0:1, m * 128:(m + 1) * 128], ones[0:1, 0:1], True, True) for m in range(8)], rd=[l1row, ones], wr=[PB[0]])
            k.op(k.dve, I("reciprocal", out=invl1[:], in_=PB[0][:, 0:8]), rd=[PB[0]], wr=[invl1])
            A.release(m_)
            fft_fwd(KTOK, n, True, KF[n], consts_fft)

        def hy_layer_consts(j):
            cw = A.alloc("hy_cw", [128, 24, 4])
            load_cols(cw, [(IN['hy_short_w'], IN['hy_short_w'].t[j, t]) for t in range(3)] + [(IN['hy_short_b'], IN['hy_short_b'].t[j])])
            sk = A.alloc("hy_sk", [128, 8, 1])
            load_cols(sk, [(IN['hy_skip'], IN['hy_skip'].t[j])])
            wo = A.alloc("hy_wo", [128, 8, D], BF16)
            k.dma(k.sp, wo[:], W[f"hy_out{j}"].t.rearrange("(kk p) n -> p kk n", p=128), rd=[W[f"hy_out{j}"]], wr=[wo])
            Cc = A.alloc("Cc", [128, 128], BF16)
            Cs = A.alloc("Cs", [128, 128], BF16)
            Cns = A.alloc("Cns", [128, 128], BF16)
            k.dma(k.sp, Cc[:], CI["c_cos"][:, :], rd=[CI["c_cos"]], wr=[Cc])
            k.dma(k.sp, Cs[:], CI["c_sin"][:, :], rd=[CI["c_sin"]], wr=[Cs])
            k.dma(k.sp, Cns[:], CI["c_nsin"][:, :], rd=[CI["c_nsin"]], wr=[Cns])
            return cw, sk, wo, Cc, Cs, Cns

        def hy_seq(j, xsrc, xdst, n, consts_, invl1):
            cw, sk, wo, Cc, Cs, Cns = consts_
            Win = W[f"hy_in{j}"]
            X0, XV, _ = FM
            nJ = n // 128
            F2 = nJ + 1
            NF = 2 * F2
            m_ = A.mark()
            wch = [A.alloc(f"hyw{i}", [128, 8, 128], BF16) for i in range(6)]
            tz = [[A.alloc(f"tz{i}_{q}", [128, 512]) for q in range(3)] for i in range(2)]
            xvb = A.alloc("xvb", [128, 8, 512], BF16)
            xvt = [A.alloc(f"xvt{i}", [128, D], BF16) for i in range(2)]
            hnT = A.alloc("hnT", [128, 8, n + 2], BF16)
            norm_to_hnT(xsrc, n, MODT[0], MODT[1], hnT, 1)
            it = 0
            for w0 in range(0, n, 510):
                ncols = min(512, n + 2 - w0)
                nv = ncols - 2
                for m in range(8):
                    p = it % 2
                    it += 1
                    for q in range(3):
                        wq = wch[p * 3 + q]
                        k.dma(k.sp, wq[:], Win.t[q * 8 + m], rd=[Win], wr=[wq], sembuf=wq)
                        bank = PB[p * 3 + q]
                        k.mm(bank[:, :ncols], [(wq[:, kk, :], hnT[:, kk, w0:w0 + ncols]) for kk in range(8)], rd=[wq, hnT], wr=[bank])
                        conv_taps(k.act, bank, ncols, nv, 1, [0, 1, 2], cw, q * 8 + m, cw[:, q * 8 + m, 3:4], tz[p][q])
                    k.op(k.pool, I("tensor_tensor", out=tz[p][1][:, :nv], in0=tz[p][1][:, :nv], in1=tz[p][2][:, :nv], op=ALU.mult),
                         rd=[tz[p][1], tz[p][2]], wr=[tz[p][1]])
                    k.dma(k.pool, X0.t[m * 128:(m + 1) * 128, w0:w0 + nv], tz[p][0][:, :nv], rd=[tz[p][0]], wrp=[X0], sembuf=tz[p][0])
                    k.dma(k.pool, XV.t[m * 128:(m + 1) * 128, w0:w0 + nv], tz[p][1][:, :nv], rd=[tz[p][1]], wrp=[XV], sembuf=tz[p][1])
                    k.op(k.act, I("copy", out=xvb[:, m, :nv], in_=tz[p][1][:, :nv]), rd=[tz[p][1]], wr=[xvb] if m == 0 else (), wrp=[xvb] if m else ())
                for kt in range((nv + 127) // 128):
                    nt = min(128, nv - kt * 128)
                    p = kt % 2
                    pv = pbf(6 + p)
                    k.mmv([(pv[:nt, m * 128:(m + 1) * 128], xvb[:, m, kt * 128:kt * 128 + nt], identb[:], True, True) for m in range(8)],
                          rd=[xvb, identb], wr=[PB[6 + p]], transpose=True)
                    k.op(k.dve, I("tensor_copy", out=xvt[p][:nt, :], in_=pv[:nt, :]), rd=[PB[6 + p]], wr=[xvt[p]])
                    k.dma(k.pool, TOK.t[w0 + kt * 128:w0 + kt * 128 + nt, :], xvt[p][:nt, :], rd=[xvt[p]], wrp=[TOK], sembuf=xvt[p])
            A.release(m_)
            fft_fwd(TOK, n, False, KF[n], (Cc, Cs, Cns))
            m_ = A.mark()
            qa = A.alloc("qa", [NF, 128, 128], BF16)
            yc = A.alloc("yc", [128, SEQ])
            xvl = A.alloc("xvl", [128, SEQ])
            x0l = A.alloc("x0l", [128, SEQ])
            y2c = [A.alloc(f"y2c{i}", [128, SEQ], BF16) for i in range(2)]
            mi = A.alloc("mi", [NF, 128, nJ], BF16)
            k.dma(k.sp, mi[:], CI[f"c_mi{n}"][:, :, :], rd=[CI[f"c_mi{n}"]], wr=[mi])
            ngrp = 16
            for m in range(8):
                for tq in range(4):
                    k.dma(k.sp, qa[:, tq * 32:(tq + 1) * 32, :], QD.t[tq * 32:(tq + 1) * 32, 0:NF, m * 128:(m + 1) * 128].rearrange("t f c -> f t c"),
                          rd=[QD], wr=[qa] if tq == 0 else (), wrp=[qa] if tq else (), sembuf=qa)
                k.dma(k.sp, xvl[:, 0:n], XV.t[m * 128:(m + 1) * 128, 0:n], rd=[XV], wr=[xvl], sembuf=xvl)
                k.dma(k.sp, x0l[:, 0:n], X0.t[m * 128:(m + 1) * 128, 0:n], rd=[X0], wr=[x0l], sembuf=x0l)
                for tg in range(128 // ngrp):
                    bank = PB[tg % 4]
                    k.mmv([(bank[:, tl * nJ:(tl + 1) * nJ], qa[:, tg * ngrp + tl, :], mi[:, tg * ngrp + tl, :], True, True) for tl in range(ngrp)],
                          rd=[qa, mi], wr=[bank])
                    qe = k.act if tg % 2 == 0 else k.dve
                    cast_op(qe, yc[:, 0:n].rearrange("p (T t) -> p T t", t=128)[:, :, tg * ngrp:(tg + 1) * ngrp],
                            bank[:, 0:ngrp * nJ].rearrange("p (t T) -> p T t", T=nJ), [bank], [yc] if tg == 0 else (), [yc] if tg else ())
                k.op(k.act, I("activation", out=xvl[:, 0:n], in_=xvl[:, 0:n], func=AF.Identity, scale=sk[:, m, 0:1], bias=0.0), rd=[xvl, sk], wr=[xvl])
                k.op(k.dve, I("scalar_tensor_tensor", out=yc[:, 0:n], in0=yc[:, 0:n], scalar=invl1[:, m:m + 1], in1=xvl[:, 0:n], op0=ALU.mult, op1=ALU.add),
                     rd=[yc, invl1, xvl], wr=[yc])
                y2 = y2c[m % 2]
                k.op(k.pool, I("tensor_tensor", out=y2[:, 0:n], in0=yc[:, 0:n], in1=x0l[:, 0:n], op=ALU.mult), rd=[yc, x0l], wr=[y2])
                k.dma(k.pool, Y2.t[m, :, 0:n], y2[:, 0:n], rd=[y2], wrp=[Y2], sembuf=y2)
            A.release(m_)
            m_ = A.mark()
            y2w = [A.alloc(f"y2w{i}", [128, 8, 512], BF16) for i in range(2)]
            for wi_ in range((n + 511) // 512):
                c0 = wi_ * 512
                ncw = min(512, n - c0)
                yw = y2w[wi_ % 2]
                k.dma(k.sp, yw[:, :, 0:ncw], Y2.t[:, :, c0:c0 + ncw].rearrange("m p t -> p m t"), rd=[Y2], wr=[yw], sembuf=yw)
                for kt in range(ncw // 128):
                    pb = [PB[4 + 2 * (kt % 2)], PB[5 + 2 * (kt % 2)]]
                    for h in range(2):
                        k.mm(pb[h][:, :], [(yw[:, m, kt * 128:(kt + 1) * 128], wo[:, m, h * 512:(h + 1) * 512]) for m in range(8)], rd=[yw, wo], wr=[pb[h]])
                    residual_out(pb, 128, xsrc, xdst, c0 + kt * 128, MODT[2])
            A.release(m_)

        ctx_needed = [True, True, False, False]
        cur = [XIN[0], XIN[1]]
        curs = [SIN[0], SIN[1]]
        pp = 0
        sub = 0

        def nextbufs(which, b):
            nonlocal_pp = None
            return None

        xflip = [0, 0]
        sflip = [0, 0]

        def xdst_for(b):
            d_ = XS[xflip[b]][b]
            xflip[b] ^= 1
            return d_

        def sdst_for(b):
            d_ = SS[sflip[b]][b]
            sflip[b] ^= 1
            return d_

        for layer in range(depth):
            kind = layer % 2
            j = layer // 2
            keep = ctx_needed[layer]
            if layer == 2:
                for b in range(2):
                    dst = xdst_for(b)
                    for w_ in range(64):
                        k.dma(k.sp if w_ % 2 else k.pool, dst.t[w_ * 64:(w_ + 1) * 64, :],
                              cur[b].t.rearrange("(r w) d -> w r d", w=64)[w_], rd=[cur[b]], wrp=[dst])
                    cur[b] = dst
            compute_mod(layer)
            mL = A.mark()
            if kind == 0:
                cs_ = rg_layer_consts(j)
                st = A.alloc("rg_state", [128, 8, 2])
                for b in range(2):
                    k.op(k.pool, I("memset", ap=st[:], constant=0.0), wr=[st])
                    bcast_mod(layer, 2)
                    sd_ = sdst_for(b) if keep else None
                    rg_seq(j, curs[b], sd_, CTX, cs_, st, keep)
                    snew = sd_
                    bcast_mod(layer, b)
                    xd = xdst_for(b)
                    rg_seq(j, cur[b], xd, SEQ, cs_, st, True)
                    cur[b] = xd
                    if keep:
                        curs[b] = snew
            else:
                cs_ = hy_layer_consts(j)
                invl1 = {}
                ns = [SEQ, CTX] if keep else [SEQ]
                for n in ns:
                    invl1[n] = A.alloc(f"invl1_{n}", [128, 8])
                    hy_filter(j, n, (cs_[3], cs_[4], cs_[5]), invl1[n])
                for b in range(2):
                    if keep:
                        bcast_mod(layer, 2)
                        sd_ = sdst_for(b)
                        hy_seq(j, curs[b], sd_, CTX, cs_, invl1[CTX])
                        curs[b] = sd_
                    bcast_mod(layer, b)
                    xd = xdst_for(b)
                    hy_seq(j, cur[b], xd, SEQ, cs_, invl1[SEQ])
                    cur[b] = xd
            A.release(mL)
            if stop_after == (layer, 'mix'):
                break
            mL = A.mark()
            fc = ffn_layer_consts(layer)
            for b in range(2):
                if keep:
                    bcast_mod(layer, 2)
                    sd_ = sdst_for(b)
                    ffn_seq(layer, curs[b], sd_, CTX, *fc)
                    curs[b] = sd_
                bcast_mod(layer, b)
                xd = xdst_for(b)
                ffn_seq(layer, cur[b], xd, SEQ, *fc)
                cur[b] = xd
            A.release(mL)

        colmajor = depth > 2
        m_ = A.mark()
        fg = A.alloc("fg", [128, D])
        k.dma(k.sp, fg[:], IN['final_g'].t.partition_broadcast(128), rd=[IN['final_g']], wr=[fg])
        junk = A.alloc("fjunk", [128, D], BF16)
        ss = [A.alloc(f"fss{i}", [128, 1]) for i in range(2)]
        sd = [A.alloc(f"fsd{i}", [128, 1]) for i in range(2)]
        rs = [A.alloc(f"frs{i}", [128, 1]) for i in range(2)]
        for b in range(2):
            for i in range(SEQ // 128):
                xt = next_xio()
                k.dma(k.sp, xt[:], cur[b].t[i * 128:(i + 1) * 128, :], rd=[cur[b]], wr=[xt], sembuf=xt)
                p = i % 2
                k.op(k.act, I("activation", out=junk[:], in_=xt[:], func=AF.Square, accum_out=ss[p][:]), rd=[xt], wr=[junk, ss[p]])
                k.op(k.act, I("activation", out=sd[p][:], in_=ss[p][:], func=AF.Sqrt, scale=1.0 / D, bias=1e-6), rd=[ss[p]], wr=[sd[p]])
                k.op(k.dve, I("reciprocal", out=rs[p][:], in_=sd[p][:]), rd=[sd[p]], wr=[rs[p]])
                xo = next_xio()
                k.op(k.dve, I("scalar_tensor_tensor", out=xo[:], in0=xt[:], scalar=rs[p][:], in1=fg[:], op0=ALU.mult, op1=ALU.mult),
                     rd=[xt, rs[p], fg], wr=[xo])
                if colmajor:
                    ov = OUTB[b].t.rearrange("(r w) d -> w r d", w=64)
                    k.dma(k.pool, ov[2 * i], xo[0:64, :], rd=[xo], wrp=[OUTB[b]], sembuf=xo)
                    k.dma(k.pool, ov[2 * i + 1], xo[64:128, :], rd=[xo], wrp=[OUTB[b]], sembuf=xo)
                else:
                    k.dma(k.pool, OUTB[b].t[i * 128:(i + 1) * 128, :], xo[:], rd=[xo], wrp=[OUTB[b]], sembuf=xo)
        A.release(m_)
        k.finish([OUTB[0], OUTB[1]])
        print("build: ops", k.nops, "sems", len(k.sems))
    return nc, consts


_CACHE = {}


def kernel(**inputs):
    if "nc" not in _CACHE:
        _CACHE["nc"] = build()
    nc, consts = _CACHE["nc"]
    in_maps = []
    for core in range(NCORE):
        m = {}
        for nm in INPUT_NAMES:
            a = np.asarray(inputs[nm])
            if nm in ('x', 'c', 'ctx'):
                a = a[2 * core:2 * core + 2]
            m[nm] = np.ascontiguousarray(a, dtype=np.float32)
        for nm, arr in consts.items():
            m[nm] = arr
        in_maps.append(m)
    res = run_bass_kernel_spmd(nc, in_maps, core_ids=list(range(NCORE)))
    out = np.concatenate([np.asarray(r["out"]) for r in res.results], axis=0)
    return out.astype(np.float32)
```

```python
import math
import numpy as np
import ml_dtypes
import concourse.bass as bass
import concourse.mybir as mybir
from concourse.bass_utils import run_bass_kernel_spmd
from contextlib import ExitStack

F32 = mybir.dt.float32
BF16 = mybir.dt.bfloat16
I32 = mybir.dt.int32
U8 = mybir.dt.uint8
AF = mybir.ActivationFunctionType
ALU = mybir.AluOpType

D = 1024
SEQ = 4096
CTX = 256
DFF = 2816
NCORE = 8
TWO_PI = 2.0 * math.pi
DEBUG_DUMP = ()


def I(name, **kw):
    return lambda e: getattr(e, name)(**kw)


class Buf:
    __slots__ = ("name", "t", "w", "r", "pr", "sem", "semv", "key")

    def __init__(s, name, t=None):
        s.name = name
        s.t = t
        s.w = {}
        s.r = {}
        s.pr = {}
        s.sem = None
        s.semv = 0
        s.key = None

    def __getitem__(s, k):
        return s.t[k]


class Q:
    def __init__(s, kb, name, eng):
        s.name = name
        s.eng = eng
        s.sem = kb.newsem("q_" + name)
        s.cnt = 0
        s.seen = {}
        s.ops = []
        s.shsem = None
        s.shv = 0


class KB:
    def __init__(s, nc, es):
        s.nc = nc
        s.es = es
        s.sems = []
        s.pe = Q(s, "pe", nc.tensor)
        s.act = Q(s, "act", nc.scalar)
        s.dve = Q(s, "dve", nc.vector)
        s.pool = Q(s, "pool", nc.gpsimd)
        s.sp = Q(s, "sp", nc.sync)
        s.qs = [s.pe, s.act, s.dve, s.pool, s.sp]
        s.semcache = {}
        s.nops = 0

    def newsem(s, name):
        h = s.es.enter_context(s.nc.semaphore(f"{name}_{len(s.sems)}"))
        s.sems.append(h)
        return len(s.sems) - 1

    def ps(s, name, shape, dt=F32):
        return s.es.enter_context(s.nc.psum_tensor(name, list(shape), dt))

    def dram(s, name, shape, dt=F32):
        kind = "ExternalOutput" if (DEBUG_DUMP and name in DEBUG_DUMP) else "Internal"
        return Buf(name, s.nc.dram_tensor(name, list(shape), dt, kind=kind).ap())

    def _waits(s, q, rd, wr, wrp, same_ok=False):
        need = {}

        def add(d):
            for k, v in d.items():
                if need.get(k, 0) < v:
                    need[k] = v

        for b in rd:
            add(b.w)
        for b in wr:
            add(b.w)
            add(b.r)
        for b in wrp:
            add(b.r)
            add(b.pr)
        out = []
        for k, v in need.items():
            if same_ok and k == q.sem:
                continue
            if q.seen.get(k, 0) >= v:
                continue
            q.seen[k] = v
            out.append((k, v))
        return out

    def _mark(s, sem, val, rd, wr, wrp):
        for b in rd:
            if b.r.get(sem, 0) < val:
                b.r[sem] = val
        for b in wr:
            pr = dict(b.w)
            for k_, v_ in b.r.items():
                if pr.get(k_, 0) < v_:
                    pr[k_] = v_
            b.pr = pr
            b.w = {sem: val}
            b.r = {}
        for b in wrp:
            if b.w.get(sem, 0) < val:
                b.w[sem] = val

    def op(s, q, fn, rd=(), wr=(), wrp=()):
        waits = s._waits(q, rd, wr, wrp)
        q.cnt += 1
        q.ops.append((waits, fn, (q.sem, 1)))
        s._mark(q.sem, q.cnt, rd, wr, wrp)
        s.nops += 1

    def dma(s, q, out, in_, rd=(), wr=(), wrp=(), sembuf=None, **kw):
        waits = s._waits(q, rd, wr, wrp)
        if sembuf is not None:
            if sembuf.sem is None:
                key = sembuf.key
                if key is not None and key in s.semcache:
                    sembuf.sem, sembuf.semv = s.semcache[key]
                else:
                    sembuf.sem = s.newsem("d_" + sembuf.name)
            sembuf.semv += 16
            sem, val = sembuf.sem, sembuf.semv
            if sembuf.key is not None:
                s.semcache[sembuf.key] = (sem, val)
        else:
            if q.shsem is None:
                q.shsem = s.newsem("sh_" + q.name)
            if q.shv > 0 and q.seen.get(q.shsem, 0) < q.shv:
                q.seen[q.shsem] = q.shv
                waits.append((q.shsem, q.shv))
            q.shv += 16
            sem, val = q.shsem, q.shv
        q.ops.append((waits, lambda e: e.dma_start(out=out, in_=in_, **kw), (sem, 16)))
        s._mark(sem, val, rd, wr, wrp)
        s.nops += 1

    def mmv(s, items, rd=(), wr=(), wrp=(), transpose=False):
        q = s.pe
        waits = s._waits(q, rd, wr, wrp, same_ok=True)
        q.cnt += 1

        def fn(e):
            ins = None
            for (o, l, r, st, sp) in items:
                if transpose:
                    ins = e.transpose(o, l, r)
                else:
                    ins = e.matmul(o, l, r, start=st, stop=sp)
            return ins

        q.ops.append((waits, fn, (q.sem, 1)))
        s._mark(q.sem, q.cnt, rd, wr, wrp)
        s.nops += len(items)

    def mm(s, out, pairs, rd=(), wr=()):
        n = len(pairs)
        s.mmv([(out, l, r, i == 0, i == n - 1) for i, (l, r) in enumerate(pairs)], rd=rd, wr=wr)

    def finish(s, final_bufs):
        nc = s.nc
        for q in s.qs:
            waits = s._waits(q, final_bufs, (), ())
            q.ops.append((waits, None, None))
        with nc.Block() as block:
            def emit(q):
                def body(e):
                    for waits, fn, inc in q.ops:
                        for k, v in waits:
                            e.wait_ge(s.sems[k], v)
                        if fn is not None:
                            ins = fn(e)
                            ins.then_inc(s.sems[inc[0]], inc[1])
                return body
            block.tensor(emit(s.pe))
            block.scalar(emit(s.act))
            block.vector(emit(s.dve))
            block.gpsimd(emit(s.pool))
            block.sync(emit(s.sp))


class Arena:
    def __init__(s, tensor, size):
        s.t = tensor
        s.size = size
        s.off = 0
        s.hist = []

    def alloc(s, name, shape, dt=F32):
        esz = 2 if dt == BF16 else 4
        n = int(np.prod(shape[1:])) * esz
        nal = (n + 63) // 64 * 64
        off = s.off
        s.off += nal
        assert s.off <= s.size, (name, s.off, s.size)
        ap = s.t[:, off:off + n].bitcast(dt)
        if len(shape) == 3:
            ap = ap.rearrange("p (a b) -> p a b", a=shape[1])
        if shape[0] < 128:
            ap = ap[0:shape[0]]
        b = Buf(name, ap)
        b.key = (off, n)
        keep = []
        for (o, e, ob) in s.hist:
            if o < off + nal and e > off:
                for d in (ob.w, ob.r):
                    for k, v in d.items():
                        if b.r.get(k, 0) < v:
                            b.r[k] = v
                if o >= off and e <= off + nal:
                    continue
            keep.append((o, e, ob))
        keep.append((off, off + nal, b))
        s.hist = keep
        return b

    def mark(s):
        return s.off

    def release(s, m):
        s.off = m


def _fft_consts(n):
    nJ = n // 128
    N = 2 * n
    F2 = nJ + 1
    J = np.arange(nJ)[:, None, None]
    jj = np.arange(128)[None, :, None]
    f2 = np.arange(F2)[None, None, :]
    ang = 2 * np.pi * ((f2 * (128 * J + jj)) % N) / N
    ma = np.concatenate([np.cos(ang), -np.sin(ang)], axis=2)
    w = np.full(F2, 2.0)
    w[0] = 1.0
    w[F2 - 1] = 1.0
    f2b = np.arange(F2)[:, None, None]
    tt = np.arange(128)[None, :, None]
    To = np.arange(nJ)[None, None, :]
    tf = 128 * (To + nJ // 2) + tt
    ang2 = 2 * np.pi * ((f2b * tf) % N) / N
    mi = np.concatenate([w[:, None, None] / N * np.cos(ang2), -w[:, None, None] / N * np.sin(ang2)], axis=0)
    return ma.astype(ml_dtypes.bfloat16), mi.astype(ml_dtypes.bfloat16)


def _filter_consts(n):
    t = np.linspace(0.0, 1.0, n, dtype=np.float32)[:, None]
    w = (np.float32(2.0 * math.pi / n) * np.arange(n, dtype=np.float32))[:, None]
    bands = np.linspace(1e-4, 15, 16, dtype=np.float32)[None, :]
    z = np.concatenate([t, np.cos(bands * w), -np.sin(bands * w)], axis=-1).astype(np.float32)
    centre = n // 2
    dist = (np.abs(np.arange(n) - centre).astype(np.float32) / np.float32(centre)).astype(np.float32)
    return np.ascontiguousarray(z.T), np.ascontiguousarray(dist.reshape(n // 128, 128).T)


def _consts():
    c = {}
    a = np.arange(128)
    ang = 2 * np.pi * ((a[:, None] * a[None, :]) % 128) / 128
    c["c_cos"] = np.cos(ang).astype(ml_dtypes.bfloat16)
    c["c_sin"] = np.sin(ang).astype(ml_dtypes.bfloat16)
    c["c_nsin"] = (-np.sin(ang)).astype(ml_dtypes.bfloat16)
    c["c_identf"] = np.eye(128, dtype=np.float32)
    c["c_identb"] = np.eye(128).astype(ml_dtypes.bfloat16)
    deltas = np.linspace(math.log(1e-2) / 1.5, math.log(1e-2) / 0.3, D, dtype=np.float32)
    c["c_nad"] = (-np.abs(deltas)).astype(np.float32)
    for n in (SEQ, CTX):
        ma, mi = _fft_consts(n)
        zt, dist = _filter_consts(n)
        c[f"c_ma{n}"] = ma
        c[f"c_mi{n}"] = mi
        c[f"c_zt{n}"] = zt
        c[f"c_dist{n}"] = dist
    return c


INPUT_NAMES = ['x', 'c', 'ctx', 'c_ctx', 'mod_w', 'mod_b', 'norm1_g', 'norm2_g', 'final_g',
               'rg_w_in', 'rg_conv_w', 'rg_conv_b', 'rg_w_a', 'rg_b_a', 'rg_w_i', 'rg_b_i', 'rg_lam', 'rg_w_out',
               'hy_w_in', 'hy_short_w', 'hy_short_b', 'hy_pe_w1', 'hy_pe_b1', 'hy_pe_w2', 'hy_pe_b2', 'hy_pe_w3',
               'hy_pe_b3', 'hy_pe_w4', 'hy_freq', 'hy_skip', 'hy_w_out',
               'ffn_w_up', 'ffn_conv_w', 'ffn_conv_b', 'ffn_w_down']

SHAPES = {
    'x': [2, SEQ, D], 'c': [2, D], 'ctx': [2, CTX, D], 'c_ctx': [D], 'mod_w': [4, D, 6 * D], 'mod_b': [4, 6 * D],
    'norm1_g': [4, D], 'norm2_g': [4, D], 'final_g': [D], 'rg_w_in': [2, D, 2 * D], 'rg_conv_w': [2, 4, D],
    'rg_conv_b': [2, D], 'rg_w_a': [2, 2, 4, 256, 256], 'rg_b_a': [2, 2, D], 'rg_w_i': [2, 2, 4, 256, 256],
    'rg_b_i': [2, 2, D], 'rg_lam': [2, 2, D], 'rg_w_out': [2, D, D], 'hy_w_in': [2, D, 3 * D],
    'hy_short_w': [2, 3, 3 * D], 'hy_short_b': [2, 3 * D], 'hy_pe_w1': [2, 33, 64], 'hy_pe_b1': [2, 64],
    'hy_pe_w2': [2, 64, 64], 'hy_pe_b2': [2, 64], 'hy_pe_w3': [2, 64, 64], 'hy_pe_b3': [2, 64],
    'hy_pe_w4': [2, 64, D], 'hy_freq': [2, 64], 'hy_skip': [2, D], 'hy_w_out': [2, D, D],
    'ffn_w_up': [4, D, 2 * DFF], 'ffn_conv_w': [4, 3, 2 * DFF], 'ffn_conv_b': [4, 2 * DFF], 'ffn_w_down': [4, DFF, D],
}


def build(depth=4, stop_after=None):
    nc = bass.Bass("TRN2", target_bir_lowering=False)
    es = ExitStack()
    consts = _consts()
    with es:
        k = KB(nc, es)
        IN = {}
        for nm in INPUT_NAMES:
            IN[nm] = Buf(nm, nc.dram_tensor(nm, SHAPES[nm], F32, kind="ExternalInput").ap())
        CI = {}
        for nm, arr in consts.items():
            dt = BF16 if arr.dtype == ml_dtypes.bfloat16 else F32
            CI[nm] = Buf(nm, nc.dram_tensor(nm, list(arr.shape), dt, kind="ExternalInput").ap())
        OUT = Buf("out", nc.dram_tensor("out", [2, SEQ, D], F32, kind="ExternalOutput").ap())

        arena_t = es.enter_context(nc.sbuf_tensor("arena", [128, 200 * 1024], U8))
        A = Arena(arena_t, 200 * 1024)
        PSt = [k.ps(f"ps{i}", [128, 1024]) for i in range(4)]
        PB = []
        for i in range(8):
            PB.append(Buf(f"bank{i}", PSt[i // 2][:, (i % 2) * 512:(i % 2) * 512 + 512]))

        def pbf(i):
            return PB[i].t.bitcast(BF16)

        XS = [[Buf(f"x{p}_{b}", None) for b in range(2)] for p in range(2)]
        SS = [[Buf(f"s{p}_{b}", None) for b in range(2)] for p in range(2)]
        for p in range(2):
            xt_ = nc.dram_tensor(f"xs{p}", [2, SEQ, D], F32, kind="Internal").ap()
            st_ = nc.dram_tensor(f"ss{p}", [2, CTX, D], F32, kind="Internal").ap()
            for b in range(2):
                XS[p][b].t = xt_[b]
                SS[p][b].t = st_[b]
        XIN = [Buf(f"xin{b}", IN['x'].t[b]) for b in range(2)]
        SIN = [Buf(f"sin{b}", IN['ctx'].t[b]) for b in range(2)]
        OUTB = [Buf(f"out{b}", OUT.t[b]) for b in range(2)]

        def wS(name, K, M):
            return k.dram(name, [M // 128, 128, K // 128, 128], BF16)

        def wN(name, K, M):
            return k.dram(name, [K, M], BF16)

        W = {}
        for j in range(2):
            W[f"rg_in{j}"] = wS(f"w_rg_in{j}", D, 2 * D)
            W[f"rg_out{j}"] = wN(f"w_rg_out{j}", D, D)
            for d_ in range(2):
                for h in range(4):
                    W[f"rg_a{j}{d_}{h}"] = wS(f"w_rg_a{j}{d_}{h}", 256, 256)
                    W[f"rg_i{j}{d_}{h}"] = wS(f"w_rg_i{j}{d_}{h}", 256, 256)
            W[f"hy_in{j}"] = wS(f"w_hy_in{j}", D, 3 * D)
            W[f"hy_out{j}"] = wN(f"w_hy_out{j}", D, D)
        for i in range(4):
            W[f"up{i}"] = wS(f"w_up{i}", D, 2 * DFF)
            W[f"down{i}"] = wN(f"w_down{i}", DFF, D)
        FM = [k.dram(f"fm{i}", [D, SEQ]) for i in range(3)]
        TOK = k.dram("tok", [SEQ, D], BF16)
        KTOK = k.dram("ktok", [SEQ, D], BF16)
        PD = k.dram("pd", [66, 128, D], BF16)
        QD = k.dram("qd", [128, 66, D], BF16)
        KF = {SEQ: k.dram("kf4096", [33, 2, 128, D]), CTX: k.dram("kf256", [3, 2, 128, D])}

        identf = A.alloc("identf", [128, 128])
        identb = A.alloc("identb", [128, 128], BF16)
        ones = A.alloc("ones", [128, 128])
        cactT = A.alloc("cactT", [128, 8, 3])
        MODT = [A.alloc(f"modt{i}", [128, D]) for i in range(6)]
        MOD3 = k.dram("mod3", [3, 6 * D])
        Y2 = k.dram("y2", [8, 128, SEQ], BF16)
        xio = [A.alloc(f"xio{i}", [128, D]) for i in range(4)]
        k.dma(k.sp, identf[:], CI["c_identf"][:], rd=[CI["c_identf"]], wr=[identf])
        k.dma(k.sp, identb[:], CI["c_identb"][:], rd=[CI["c_identb"]], wr=[identb])
        k.op(k.pool, I("memset", ap=ones[:], constant=1.0), wr=[ones])
        def row(ap):
            return ap.rearrange("(o n) -> o n", o=1)
        PERSIST = A.mark()
        xio_i = [0]

        def next_xio():
            b = xio[xio_i[0] % 4]
            xio_i[0] += 1
            return b

        rr = [0]

        def anyeng():
            rr[0] += 1
            return [k.act, k.dve, k.pool][rr[0] % 3]

        def cast_op(q, out, in_, rd, wr, wrp=()):
            if q is k.act:
                k.op(q, I("copy", out=out, in_=in_), rd=rd, wr=wr, wrp=wrp)
            else:
                k.op(q, I("tensor_copy", out=out, in_=in_), rd=rd, wr=wr, wrp=wrp)

        def cast_weight(src_buf, src2d, K, M, dst, layout, stf, stb, cnt):
            for kc in range(K // 128):
                sf = stf[cnt[0] % 2]
                sb_ = stb[cnt[0] % 2]
                cnt[0] += 1
                k.dma(k.sp, sf[:, :M], src2d[kc * 128:(kc + 1) * 128, :], rd=[src_buf], wr=[sf], sembuf=sf)
                cast_op(anyeng(), sb_[:, :M], sf[:, :M], [sf], [sb_])
                if layout == 'S':
                    for g0 in range(0, M // 128, 16):
                        g1 = min(M // 128, g0 + 16)
                        k.dma(k.pool, dst.t[g0:g1, :, kc, :].rearrange("m p i -> p m i"),
                              sb_[:, g0 * 128:g1 * 128].rearrange("p (m i) -> p m i", i=128), rd=[sb_], wrp=[dst], sembuf=sb_)
                else:
                    k.dma(k.pool, dst.t[kc * 128:(kc + 1) * 128, :], sb_[:, :M], rd=[sb_], wrp=[dst], sembuf=sb_)

        m0 = A.mark()
        stf = [A.alloc(f"stf{i}", [128, 2 * DFF]) for i in range(2)]
        stb = [A.alloc(f"stb{i}", [128, 2 * DFF], BF16) for i in range(2)]
        cnt = [0]
        for i in range(depth):
            j = i // 2
            if i % 2 == 0:
                cast_weight(IN['rg_w_in'], IN['rg_w_in'].t[j], D, 2 * D, W[f"rg_in{j}"], 'S', stf, stb, cnt)
                cast_weight(IN['rg_w_out'], IN['rg_w_out'].t[j], D, D, W[f"rg_out{j}"], 'N', stf, stb, cnt)
                for d_ in range(2):
                    for h in range(4):
                        cast_weight(IN['rg_w_a'], IN['rg_w_a'].t[j, d_, h], 256, 256, W[f"rg_a{j}{d_}{h}"], 'S', stf, stb, cnt)
                        cast_weight(IN['rg_w_i'], IN['rg_w_i'].t[j, d_, h], 256, 256, W[f"rg_i{j}{d_}{h}"], 'S', stf, stb, cnt)
            else:
                cast_weight(IN['hy_w_in'], IN['hy_w_in'].t[j], D, 3 * D, W[f"hy_in{j}"], 'S', stf, stb, cnt)
                cast_weight(IN['hy_w_out'], IN['hy_w_out'].t[j], D, D, W[f"hy_out{j}"], 'N', stf, stb, cnt)
            cast_weight(IN['ffn_w_up'], IN['ffn_w_up'].t[i], D, 2 * DFF, W[f"up{i}"], 'S', stf, stb, cnt)
            cast_weight(IN['ffn_w_down'], IN['ffn_w_down'].t[i], DFF, D, W[f"down{i}"], 'N', stf, stb, cnt)
        A.release(m0)

        m0 = A.mark()
        crow = A.alloc("crow", [3, D])
        k.dma(k.sp, crow[0:2, :], IN['c'][:, :], rd=[IN['c']], wr=[crow])
        k.dma(k.sp, crow[2:3, :], row(IN['c_ctx'].t), rd=[IN['c_ctx']], wrp=[crow])
        crow2 = A.alloc("crow2", [3, D])
        k.op(k.act, I("activation", out=crow2[:], in_=crow[:], func=AF.Silu), rd=[crow], wr=[crow2])
        k.mmv([(PB[0][:, kk * 3:kk * 3 + 3], crow2[0:3, kk * 128:(kk + 1) * 128], identf[0:3, 0:3], True, True) for kk in range(8)],
              rd=[crow2, identf], wr=[PB[0]], transpose=True)
        k.op(k.dve, I("tensor_copy", out=cactT[:].rearrange("p a b -> p (a b)"), in_=PB[0][:, 0:24]), rd=[PB[0]], wr=[cactT])
        A.release(m0)

        def load_cols(dst, rows):
            R = len(rows)
            ncols = rows[0][1].shape[0]
            nch = ncols // 128
            m_ = A.mark()
            st = A.alloc("lc_st", [R, ncols])
            for r, (sbuf_, ap) in enumerate(rows):
                k.dma(k.sp, st[r:r + 1, :], row(ap), rd=[sbuf_], wrp=[st])
            done = 0
            while done < nch:
                nb = min(nch - done, 512 // R)
                k.mmv([(PB[0][:, q_ * R:(q_ + 1) * R], st[0:R, (done + q_) * 128:(done + q_ + 1) * 128], identf[0:R, 0:R], True, True)
                       for q_ in range(nb)], rd=[st, identf], wr=[PB[0]], transpose=True)
                k.op(k.dve, I("tensor_copy", out=dst[:, done:done + nb, :].rearrange("p a b -> p (a b)"), in_=PB[0][:, 0:nb * R]),
                     rd=[PB[0]], wrp=[dst])
                done += nb
            A.release(m_)

        def compute_mod(layer):
            m_ = A.mark()
            mw = [A.alloc(f"mw{i}", [128, 8, 512]) for i in range(2)]
            mb = A.alloc("mb", [1, 6 * D])
            m3 = A.alloc("m3", [3, 6 * D])
            k.dma(k.sp, mb[:], row(IN['mod_b'].t[layer]), rd=[IN['mod_b']], wr=[mb])
            for ns in range(12):
                t = mw[ns % 2]
                k.dma(k.sp, t[:], IN['mod_w'].t[layer].rearrange("(kk p) n -> p kk n", p=128)[:, :, ns * 512:(ns + 1) * 512],
                      rd=[IN['mod_w']], wr=[t], sembuf=t)
                bank = PB[ns % 2]
                pairs = [(cactT[:, kk, :], t[:, kk, :]) for kk in range(8)] + [(ones[0:1, 0:3], mb[0:1, ns * 512:(ns + 1) * 512])]
                k.mm(bank[0:3, :], pairs, rd=[cactT, t, ones, mb], wr=[bank])
                k.op(k.act, I("copy", out=m3[:, ns * 512:(ns + 1) * 512], in_=bank[0:3, :]), rd=[bank], wrp=[m3])
            k.dma(k.pool, MOD3[:, :], m3[:], rd=[m3], wr=[MOD3])
            A.release(m_)

        def bcast_mod(layer, r):
            m_ = A.mark()
            gn = A.alloc("gn", [128, D])
            for part in range(6):
                kind = part % 3
                dst = MODT[(part // 3) * 3 + {0: 1, 1: 0, 2: 2}[kind]]
                k.dma(k.sp, dst[:], MOD3.t[r, part * D:(part + 1) * D].partition_broadcast(128), rd=[MOD3], wr=[dst])
                if kind == 1:
                    nm = 'norm1_g' if part < 3 else 'norm2_g'
                    k.dma(k.sp, gn[:], IN[nm].t[layer].partition_broadcast(128), rd=[IN[nm]], wr=[gn])
                    k.op(k.dve, I("scalar_tensor_tensor", out=dst[:], in0=dst[:], scalar=1.0, in1=gn[:], op0=ALU.add, op1=ALU.mult),
                         rd=[dst, gn], wr=[dst])
            A.release(m_)

        def norm_to_hnT(xsrc, L, Amod, Bmod, hnT, npad):
            m_ = A.mark()
            junk = A.alloc("junk", [128, D], BF16)
            hnb = [A.alloc(f"hnb{i}", [128, D], BF16) for i in range(2)]
            tt0 = A.alloc("ntmp0", [128, D])
            tt_ = [tt0, tt0]
            ss = [A.alloc(f"ss{i}", [128, 1]) for i in range(2)]
            sd = [A.alloc(f"sd{i}", [128, 1]) for i in range(2)]
            rs = [A.alloc(f"rs{i}", [128, 1]) for i in range(2)]
            k.op(k.pool, I("memset", ap=hnT[:, :, 0:1], constant=0.0), wrp=[hnT])
            k.op(k.pool, I("memset", ap=hnT[:, :, L + 1:L + 1 + npad], constant=0.0), wrp=[hnT])
            for i in range(L // 128):
                xt = next_xio()
                k.dma(k.sp, xt[:], xsrc.t[i * 128:(i + 1) * 128, :], rd=[xsrc], wr=[xt], sembuf=xt)
                p = i % 2
                k.op(k.act, I("activation", out=junk[:], in_=xt[:], func=AF.Square, accum_out=ss[p][:]), rd=[xt], wr=[junk, ss[p]])
                k.op(k.act, I("activation", out=sd[p][:], in_=ss[p][:], func=AF.Sqrt, scale=1.0 / D, bias=1e-6), rd=[ss[p]], wr=[sd[p]])
                k.op(k.dve, I("reciprocal", out=rs[p][:], in_=sd[p][:]), rd=[sd[p]], wr=[rs[p]])
                k.op(k.dve, I("scalar_tensor_tensor", out=tt_[p][:], in0=xt[:], scalar=rs[p][:], in1=Amod[:], op0=ALU.mult, op1=ALU.mult),
                     rd=[xt, rs[p], Amod], wr=[tt_[p]])
                k.op(k.pool, I("tensor_tensor", out=hnb[p][:], in0=tt_[p][:], in1=Bmod[:], op=ALU.add), rd=[tt_[p], Bmod], wr=[hnb[p]])
                bank = PB[6 + p]
                pv = pbf(6 + p)
                k.mmv([(pv[:, kk * 128:(kk + 1) * 128], hnb[p][:, kk * 128:(kk + 1) * 128], identb[:], True, True) for kk in range(8)],
                      rd=[hnb[p], identb], wr=[bank], transpose=True)
                q = k.act if i % 2 == 0 else k.dve
                cast_op(q, hnT[:, :, 1 + i * 128:1 + (i + 1) * 128], pv[:, :].rearrange("p (a b) -> p a b", a=8), [bank], (), [hnT])
            A.release(m_)

        def conv_taps(q_act, bank, ncols, nv, off1, taps, cw, cidx, bias_ap, dst):
            k.op(k.act, I("activation", out=dst[:, :nv], in_=bank[:, off1:off1 + nv], func=AF.Identity,
                          scale=cw[:, cidx, taps[off1]:taps[off1] + 1], bias=bias_ap), rd=[bank, cw], wr=[dst])
            for t in range(len(taps)):
                if t == off1:
                    continue
                k.op(k.dve, I("scalar_tensor_tensor", out=dst[:, :nv], in0=bank[:, t:t + nv], scalar=cw[:, cidx, taps[t]:taps[t] + 1],
                              in1=dst[:, :nv], op0=ALU.mult, op1=ALU.add), rd=[bank, cw, dst], wr=[dst])

        def residual_out(po_banks, nt, xsrc, xdst, row0, Gmod, dst_is_final=False):
            xt = next_xio()
            k.dma(k.sp, xt[:nt, :], xsrc.t[row0:row0 + nt, :], rd=[xsrc], wr=[xt], sembuf=xt)
            xo = next_xio()
            for h in range(2):
                hs = slice(h * 512, (h + 1) * 512)
                k.op(k.dve, I("tensor_tensor", out=xo[:nt, hs], in0=po_banks[h][:nt, :], in1=Gmod[:nt, hs], op=ALU.mult),
                     rd=[po_banks[h], Gmod], wr=[xo] if h == 0 else (), wrp=[xo] if h == 1 else ())
            k.op(k.pool, I("tensor_tensor", out=xo[:nt, :], in0=xo[:nt, :], in1=xt[:nt, :], op=ALU.add), rd=[xt, xo], wr=[xo])
            k.dma(k.pool, xdst.t[row0:row0 + nt, :], xo[:nt, :], rd=[xo], wrp=[xdst], sembuf=xo)

        def ffn_layer_consts(layer):
            cw = A.alloc("ffn_cw", [128, 44, 4])
            load_cols(cw, [(IN['ffn_conv_w'], IN['ffn_conv_w'].t[layer, t]) for t in range(3)] + [(IN['ffn_conv_b'], IN['ffn_conv_b'].t[layer])])
            wd = A.alloc("ffn_wd", [128, 22, D], BF16)
            k.dma(k.sp, wd[:], W[f"down{layer}"].t.rearrange("(j p) n -> p j n", p=128), rd=[W[f"down{layer}"]], wr=[wd])
            return cw, wd

        def ffn_seq(layer, xsrc, xdst, L, cw, wd):
            m_ = A.mark()
            wg = [A.alloc(f"wg{i}", [128, 8, 128], BF16) for i in range(2)]
            wu = [A.alloc(f"wu{i}", [128, 8, 128], BF16) for i in range(2)]
            tg = [A.alloc(f"tg{i}", [128, 512]) for i in range(2)]
            tu = [A.alloc(f"tu{i}", [128, 512]) for i in range(2)]
            sg = tg
            act = A.alloc("act", [128, 22, 512], BF16)
            hnT = A.alloc("hnT", [128, 8, L + 2], BF16)
            norm_to_hnT(xsrc, L, MODT[3], MODT[4], hnT, 1)
            Wup = W[f"up{layer}"]
            it = 0
            for w0 in range(0, L, 510):
                ncols = min(512, L + 2 - w0)
                nv = ncols - 2
                for j in range(22):
                    p = it % 2
                    it += 1
                    k.dma(k.sp, wg[p][:], Wup.t[j], rd=[Wup], wr=[wg[p]], sembuf=wg[p])
                    k.dma(k.sp, wu[p][:], Wup.t[22 + j], rd=[Wup], wr=[wu[p]], sembuf=wu[p])
                    bg, bu = PB[2 * p], PB[2 * p + 1]
                    k.mm(bg[:, :ncols], [(wg[p][:, kk, :], hnT[:, kk, w0:w0 + ncols]) for kk in range(8)], rd=[wg[p], hnT], wr=[bg])
                    k.mm(bu[:, :ncols], [(wu[p][:, kk, :], hnT[:, kk, w0:w0 + ncols]) for kk in range(8)], rd=[wu[p], hnT], wr=[bu])
                    conv_taps(k.act, bg, ncols, nv, 1, [0, 1, 2], cw, j, cw[:, j, 3:4], tg[p])
                    conv_taps(k.act, bu, ncols, nv, 1, [0, 1, 2], cw, 22 + j, cw[:, 22 + j, 3:4], tu[p])
                    k.op(k.act, I("activation", out=sg[p][:, :nv], in_=tg[p][:, :nv], func=AF.Silu), rd=[tg[p]], wr=[sg[p]])
                    k.op(k.pool, I("tensor_tensor", out=act[:, j, :nv], in0=sg[p][:, :nv], in1=tu[p][:, :nv], op=ALU.mult),
                         rd=[sg[p], tu[p]], wrp=[act])
                for kt in range((nv + 127) // 128):
                    nt = min(128, nv - kt * 128)
                    pb = [PB[4 + 2 * (kt % 2)], PB[5 + 2 * (kt % 2)]]
                    for h in range(2):
                        k.mm(pb[h][:nt, :], [(act[:, j, kt * 128:kt * 128 + nt], wd[:, j, h * 512:(h + 1) * 512]) for j in range(22)],
                             rd=[act, wd], wr=[pb[h]])
                    residual_out(pb, nt, xsrc, xdst, w0 + kt * 128, MODT[5])
            A.release(m_)

        def rg_layer_consts(j):
            cw = A.alloc("rg_cw", [128, 8, 5])
            load_cols(cw, [(IN['rg_conv_w'], IN['rg_conv_w'].t[j, t]) for t in range(4)] + [(IN['rg_conv_b'], IN['rg_conv_b'].t[j])])
            gv = A.alloc("rg_gv", [128, 8, 6])
            load_cols(gv, [(IN[nm], IN[nm].t[j, d_]) for d_ in range(2) for nm in ('rg_b_a', 'rg_b_i', 'rg_lam')])
            cl = A.alloc("rg_cl", [128, 8, 2])
            tmp = A.alloc("rg_cltmp", [128, 8, 2])
            for d_ in range(2):
                k.op(k.act, I("activation", out=tmp[:, :, d_:d_ + 1], in_=gv[:, :, d_ * 3 + 2:d_ * 3 + 3], func=AF.Exp, scale=-1.0),
                     rd=[gv], wrp=[tmp])
            k.op(k.act, I("activation", out=cl[:], in_=tmp[:], func=AF.Ln, scale=1.0, bias=1.0), rd=[tmp], wr=[cl])
            k.op(k.dve, I("tensor_scalar", out=cl[:], in0=cl[:], scalar1=-8.0, scalar2=0.0, op0=ALU.mult, op1=ALU.add), rd=[cl], wr=[cl])
            hv = A.alloc("rg_hv", [128, 8, 6])
            for d_ in range(2):
                k.op(k.dve, I("tensor_scalar", out=hv[:, :, d_ * 3:d_ * 3 + 2], in0=gv[:, :, d_ * 3:d_ * 3 + 2], scalar1=0.5, scalar2=0.0,
                              op0=ALU.mult, op1=ALU.add), rd=[gv], wr=[hv] if d_ == 0 else (), wrp=[hv] if d_ else ())
                k.op(k.dve, I("tensor_scalar", out=hv[:, :, d_ * 3 + 2:d_ * 3 + 3], in0=cl[:, :, d_:d_ + 1], scalar1=0.5, scalar2=0.0,
                              op0=ALU.mult, op1=ALU.add), rd=[cl], wrp=[hv])
            wa = A.alloc("rg_wa", [128, 64, 128], BF16)
            wi = A.alloc("rg_wi", [128, 64, 128], BF16)
            for d_ in range(2):
                for h in range(4):
                    for mo in range(2):
                        b0 = ((d_ * 4 + h) * 2 + mo) * 2
                        k.dma(k.sp, wa[:, b0:b0 + 2, :], W[f"rg_a{j}{d_}{h}"].t[mo], rd=[W[f"rg_a{j}{d_}{h}"]], wrp=[wa])
                        k.dma(k.sp, wi[:, b0:b0 + 2, :], W[f"rg_i{j}{d_}{h}"].t[mo], rd=[W[f"rg_i{j}{d_}{h}"]], wrp=[wi])
            return cw, hv, cl, wa, wi

        def rg_gates_pair(d_, m0, nv, ub, hv, wa, wi, uf, st, rev, bufs):
            ms = (m0, m0 + 1)
            for idx, m in enumerate(ms):
                h, mo = m // 2, m % 2
                b0 = ((d_ * 4 + h) * 2 + mo) * 2
                br, bi = PB[2 * idx], PB[2 * idx + 1]
                k.mm(br[:, :nv], [(wa[:, b0 + kk, :], ub[:, 2 * h + kk, :nv]) for kk in range(2)], rd=[wa, ub], wr=[br])
                k.mm(bi[:, :nv], [(wi[:, b0 + kk, :], ub[:, 2 * h + kk, :nv]) for kk in range(2)], rd=[wi, ub], wr=[bi])
            for idx, m in enumerate(ms):
                r_, g_, t_, h_ = bufs[idx]
                br, bi = PB[2 * idx], PB[2 * idx + 1]
                k.op(k.act, I("activation", out=r_[:, :nv], in_=br[:, :nv], func=AF.Tanh, bias=hv[:, m, d_ * 3:d_ * 3 + 1], scale=0.5), rd=[br, hv], wr=[r_])
                k.op(k.act, I("activation", out=g_[:, :nv], in_=bi[:, :nv], func=AF.Tanh, bias=hv[:, m, d_ * 3 + 1:d_ * 3 + 2], scale=0.5), rd=[bi, hv], wr=[g_])
            for idx, m in enumerate(ms):
                r_, g_, t_, h_ = bufs[idx]
                k.op(k.act, I("activation", out=r_[:, :nv], in_=r_[:, :nv], func=AF.Exp, scale=hv[:, m, d_ * 3 + 2:d_ * 3 + 3],
                              bias=hv[:, m, d_ * 3 + 2:d_ * 3 + 3]), rd=[r_, hv], wr=[r_])
                k.op(k.pool, I("tensor_tensor", out=t_[:, :nv], in0=r_[:, :nv], in1=r_[:, :nv], op=ALU.mult), rd=[r_], wr=[t_])
            for idx, m in enumerate(ms):
                r_, g_, t_, h_ = bufs[idx]
                k.op(k.act, I("activation", out=t_[:, :nv], in_=t_[:, :nv], func=AF.Sqrt, scale=-1.0, bias=1.0), rd=[t_], wr=[t_])
            outs = []
            for idx, m in enumerate(ms):
                r_, g_, t_, h_ = bufs[idx]
                a_ = r_
                k.op(k.dve, I("scalar_tensor_tensor", out=g_[:, :nv], in0=g_[:, :nv], scalar=1.0, in1=uf[:, m, :nv], op0=ALU.add, op1=ALU.mult),
                     rd=[g_, uf], wr=[g_])
                k.op(k.dve, I("scalar_tensor_tensor", out=t_[:, :nv], in0=t_[:, :nv], scalar=0.5, in1=g_[:, :nv], op0=ALU.mult, op1=ALU.mult),
                     rd=[t_, g_], wr=[t_])
                sm = st[d_][m]
                if not rev:
                    k.op(k.dve, I("tensor_tensor_scan", out=h_[:, :nv], data0=a_[:, :nv], data1=t_[:, :nv], initial=sm[:, 0:1],
                                  op0=ALU.mult, op1=ALU.add), rd=[a_, t_, sm], wr=[h_])
                    k.op(k.act, I("copy", out=sm[:, 0:1], in_=h_[:, nv - 1:nv]), rd=[h_], wr=[sm])
                else:
                    k.op(k.dve, I("tensor_tensor_scan", out=h_[:, :nv][:, ::-1], data0=a_[:, :nv][:, ::-1],
                                  data1=t_[:, :nv][:, ::-1], initial=sm[:, 0:1], op0=ALU.mult, op1=ALU.add), rd=[a_, t_, sm], wr=[h_])
                    k.op(k.act, I("copy", out=sm[:, 0:1], in_=h_[:, 0:1]), rd=[h_], wr=[sm])
                outs.append(h_)
            return outs

        def rg_seq(j, xsrc, xdst, L, consts_, st, write_out):
            cw, hv, cl, wa, wi = consts_
            Win = W[f"rg_in{j}"]
            GG, UU, HF = FM
            m_ = A.mark()
            wch = [A.alloc(f"rgw{i}", [128, 8, 128], BF16) for i in range(2)]
            ggt = [A.alloc(f"ggt{i}", [128, 512]) for i in range(2)]
            uf = A.alloc("uf", [128, 8, 512])
            ub = A.alloc("ub", [128, 8, 512], BF16)
            bufs = [[A.alloc(f"rgb{i}_{q}", [128, 512]) for q in range(4)] for i in range(2)]
            wins = []
            for w0 in range(0, L, 509):
                ncols = min(512, L + 3 - w0)
                wins.append((w0, ncols, ncols - 3))
            it = 0
            m1 = A.mark()
            hnT = A.alloc("hnT", [128, 8, L + 3], BF16)
            norm_to_hnT(xsrc, L, MODT[0], MODT[1], hnT, 2)
            for (w0, ncols, nv) in wins:
                for m in range(16):
                    p = it % 2
                    it += 1
                    k.dma(k.sp, wch[p][:], Win.t[m], rd=[Win], wr=[wch[p]], sembuf=wch[p])
                    bank = PB[4 + p]
                    k.mm(bank[:, :ncols], [(wch[p][:, kk, :], hnT[:, kk, w0:w0 + ncols]) for kk in range(8)], rd=[wch[p], hnT], wr=[bank])
                    if m < 8:
                        g = ggt[p]
                        k.op(k.act, I("activation", out=g[:, :nv], in_=bank[:, 1:1 + nv], func=AF.Gelu), rd=[bank], wr=[g])
                        k.dma(k.pool, GG.t[m * 128:(m + 1) * 128, w0:w0 + nv], g[:, :nv], rd=[g], wrp=[GG], sembuf=g)
                    else:
                        mm_ = m - 8
                        k.op(k.act, I("activation", out=uf[:, mm_, :nv], in_=bank[:, 1:1 + nv], func=AF.Identity,
                                      scale=cw[:, mm_, 1:2], bias=cw[:, mm_, 4:5]), rd=[bank, cw], wr=[uf] if mm_ == 0 else (), wrp=[uf] if mm_ else ())
                        for t in (0, 2, 3):
                            k.op(k.dve, I("scalar_tensor_tensor", out=uf[:, mm_, :nv], in0=bank[:, t:t + nv], scalar=cw[:, mm_, t:t + 1],
                                          in1=uf[:, mm_, :nv], op0=ALU.mult, op1=ALU.add), rd=[bank, cw, uf], wrp=[uf])
                        k.op(k.pool, I("tensor_copy", out=ub[:, mm_, :nv], in_=uf[:, mm_, :nv]), rd=[uf], wr=[ub] if mm_ == 0 else (), wrp=[ub] if mm_ else ())
                k.dma(k.pool, UU.t.rearrange("(m p) t -> p m t", p=128)[:, :, w0:w0 + nv], uf[:, :, :nv], rd=[uf], wrp=[UU], sembuf=uf)
                for m0 in range(0, 8, 2):
                    hs_ = rg_gates_pair(0, m0, nv, ub, hv, wa, wi, uf, st, False, bufs)
                    for idx, h_ in enumerate(hs_):
                        m = m0 + idx
                        k.dma(k.pool, HF.t[m * 128:(m + 1) * 128, w0:w0 + nv], h_[:, :nv], rd=[h_], wrp=[HF], sembuf=h_)
            A.release(m1)
            m1 = A.mark()
            hfl = [A.alloc(f"hfl{i}", [128, 512]) for i in range(2)]
            wo = A.alloc("rg_wo", [128, 8, D], BF16)
            yb = A.alloc("yb", [128, 8, 512], BF16)
            if write_out:
                k.dma(k.sp, wo[:], W[f"rg_out{j}"].t.rearrange("(kk p) n -> p kk n", p=128), rd=[W[f"rg_out{j}"]], wr=[wo])
            for (w0, ncols, nv) in reversed(wins):
                k.dma(k.sp, uf[:, :, :nv], UU.t.rearrange("(m p) t -> p m t", p=128)[:, :, w0:w0 + nv], rd=[UU], wr=[uf], sembuf=uf)
                for mm_ in range(8):
                    k.op(k.pool, I("tensor_copy", out=ub[:, mm_, :nv], in_=uf[:, mm_, :nv]), rd=[uf], wr=[ub] if mm_ == 0 else (), wrp=[ub] if mm_ else ())
                for m in range(8):
                    if m % 2 == 0:
                        hs_ = rg_gates_pair(1, m, nv, ub, hv, wa, wi, uf, st, True, bufs)
                    h_ = hs_[m % 2]
                    p = m % 2
                    if write_out:
                        k.dma(k.sp, hfl[p][:, :nv], HF.t[m * 128:(m + 1) * 128, w0:w0 + nv], rd=[HF], wr=[hfl[p]], sembuf=hfl[p])
                        k.dma(k.sp, ggt[p][:, :nv], GG.t[m * 128:(m + 1) * 128, w0:w0 + nv], rd=[GG], wr=[ggt[p]], sembuf=ggt[p])
                        k.op(k.pool, I("tensor_tensor", out=hfl[p][:, :nv], in0=hfl[p][:, :nv], in1=h_[:, :nv], op=ALU.add), rd=[hfl[p], h_], wr=[hfl[p]])
                        k.op(k.dve, I("tensor_tensor", out=yb[:, m, :nv], in0=hfl[p][:, :nv], in1=ggt[p][:, :nv], op=ALU.mult),
                             rd=[hfl[p], ggt[p]], wr=[yb] if m == 0 else (), wrp=[yb] if m else ())
                if write_out:
                    for kt in range((nv + 127) // 128):
                        nt = min(128, nv - kt * 128)
                        pb = [PB[4 + 2 * (kt % 2)], PB[5 + 2 * (kt % 2)]]
                        for h in range(2):
                            k.mm(pb[h][:nt, :], [(yb[:, m, kt * 128:kt * 128 + nt], wo[:, m, h * 512:(h + 1) * 512]) for m in range(8)],
                                 rd=[yb, wo], wr=[pb[h]])
                        residual_out(pb, nt, xsrc, xdst, w0 + kt * 128, MODT[2])
            A.release(m1)
            A.release(m_)

        def fft_fwd(src_tok, n, filter_mode, kf, consts_fft):
            Cc, Cs, Cns = consts_fft
            nJ = n // 128
            F2 = nJ + 1
            NF = 2 * F2
            m_ = A.mark()
            ua = [A.alloc(f"ua{i}", [nJ, 8, D], BF16) for i in range(2)]
            pa = [A.alloc(f"pa{i}", [NF, D], BF16) for i in range(6)]
            ma = A.alloc("ma", [nJ, 128, NF], BF16)
            k.dma(k.sp, ma[:], CI[f"c_ma{n}"][:, :, :], rd=[CI[f"c_ma{n}"]], wr=[ma])
            srcv = src_tok.t[0:n, :].rearrange("(J q) c -> J q c", q=128)
            for jc in range(16):
                u = ua[jc % 2]
                k.dma(k.sp, u[:], srcv[:, jc * 8:(jc + 1) * 8, :], rd=[src_tok], wr=[u], sembuf=u)
                for jl in range(8):
                    jj = jc * 8 + jl
                    p = jj % 6
                    banks = [PB[2 * (jj % 4)], PB[2 * (jj % 4) + 1]]
                    for h in range(2):
                        k.mm(banks[h][0:NF, :], [(ma[:, jj, :], u[:, jl, h * 512:(h + 1) * 512])], rd=[ma, u], wr=[banks[h]])
                    q = k.act if jj % 2 == 0 else k.dve
                    for h in range(2):
                        cast_op(q, pa[p][:, h * 512:(h + 1) * 512], banks[h][0:NF, :], [banks[h]], [pa[p]] if h == 0 else (), [pa[p]] if h else ())
                    k.dma(k.pool, PD.t[0:NF, jj, :], pa[p][:], rd=[pa[p]], wrp=[PD], sembuf=pa[p])
            A.release(m_)
            m_ = A.mark()
            pbt = [A.alloc(f"pbt{i}", [128, 2, D], BF16) for i in range(2)]
            if filter_mode:
                xo = [A.alloc(f"kfo{i}", [128, 2, D]) for i in range(2)]
            else:
                kft = [A.alloc(f"kft{i}", [128, 2, D]) for i in range(2)]
                tm = [A.alloc(f"ytm{i}", [128, D]) for i in range(4)]
                yy = [A.alloc(f"yy{i}", [128, 2, D], BF16) for i in range(2)]
                qt = [A.alloc(f"qt{i}", [128, 2, D], BF16) for i in range(2)]
            for f2 in range(F2):
                p = f2 % 2
                t = pbt[p]
                k.dma(k.sp, t[:, 0, :], PD.t[f2], rd=[PD], wr=[t], sembuf=t)
                k.dma(k.sp, t[:, 1, :], PD.t[F2 + f2], rd=[PD], wrp=[t], sembuf=t)
                for h in range(2):
                    hs = slice(h * 512, (h + 1) * 512)
                    k.mm(PB[h][:, :], [(Cc[:], t[:, 0, hs]), (Cs[:], t[:, 1, hs])], rd=[Cc, Cs, t], wr=[PB[h]])
                    k.mm(PB[2 + h][:, :], [(Cc[:], t[:, 1, hs]), (Cns[:], t[:, 0, hs])], rd=[Cc, Cns, t], wr=[PB[2 + h]])
                if filter_mode:
                    o = xo[p]
                    for ri in range(2):
                        for h in range(2):
                            hs = slice(h * 512, (h + 1) * 512)
                            first = (ri == 0 and h == 0)
                            k.op(k.act if h == 0 else k.dve, I("copy" if h == 0 else "tensor_copy", out=o[:, ri, hs], in_=PB[2 * ri + h][:, :]),
                                 rd=[PB[2 * ri + h]], wr=[o] if first else (), wrp=() if first else [o])
                    k.dma(k.pool, kf.t[f2].rearrange("r p c -> p r c"), o[:], rd=[o], wrp=[kf], sembuf=o)
                else:
                    kt_ = kft[p]
                    k.dma(k.sp, kt_[:], kf.t[f2].rearrange("r p c -> p r c"), rd=[kf], wr=[kt_], sembuf=kt_)
                    y = yy[p]
                    for h in range(2):
                        hs = slice(h * 512, (h + 1) * 512)
                        xr, xi = PB[h], PB[2 + h]
                        k.op(k.dve, I("tensor_tensor", out=tm[0][:, hs], in0=xr[:, :], in1=kt_[:, 0, hs], op=ALU.mult), rd=[xr, kt_], wr=[tm[0]] if h == 0 else (), wrp=[tm[0]] if h else ())
                        k.op(k.dve, I("tensor_tensor", out=tm[1][:, hs], in0=xi[:, :], in1=kt_[:, 1, hs], op=ALU.mult), rd=[xi, kt_], wr=[tm[1]] if h == 0 else (), wrp=[tm[1]] if h else ())
                        k.op(k.dve, I("tensor_tensor", out=tm[2][:, hs], in0=xr[:, :], in1=kt_[:, 1, hs], op=ALU.mult), rd=[xr, kt_], wr=[tm[2]] if h == 0 else (), wrp=[tm[2]] if h else ())
                        k.op(k.dve, I("tensor_tensor", out=tm[3][:, hs], in0=xi[:, :], in1=kt_[:, 0, hs], op=ALU.mult), rd=[xi, kt_], wr=[tm[3]] if h == 0 else (), wrp=[tm[3]] if h else ())
                    k.op(k.pool, I("tensor_tensor", out=y[:, 0, :], in0=tm[0][:], in1=tm[1][:], op=ALU.subtract), rd=[tm[0], tm[1]], wr=[y])
                    k.op(k.pool, I("tensor_tensor", out=y[:, 1, :], in0=tm[2][:], in1=tm[3][:], op=ALU.add), rd=[tm[2], tm[3]], wrp=[y])
                    for h in range(2):
                        hs = slice(h * 512, (h + 1) * 512)
                        k.mm(PB[4 + h][:, :], [(Cc[:], y[:, 0, hs]), (Cns[:], y[:, 1, hs])], rd=[Cc, Cns, y], wr=[PB[4 + h]])
                        k.mm(PB[6 + h][:, :], [(Cc[:], y[:, 1, hs]), (Cs[:], y[:, 0, hs])], rd=[Cc, Cs, y], wr=[PB[6 + h]])
                    q_ = qt[p]
                    for ri in range(2):
                        for h in range(2):
                            hs = slice(h * 512, (h + 1) * 512)
                            first = (ri == 0 and h == 0)
                            k.op(k.act, I("copy", out=q_[:, ri, hs], in_=PB[4 + 2 * ri + h][:, :]), rd=[PB[4 + 2 * ri + h]],
                                 wr=[q_] if first else (), wrp=() if first else [q_])
                    k.dma(k.pool, QD.t[:, f2, :], q_[:, 0, :], rd=[q_], wrp=[QD], sembuf=q_)
                    k.dma(k.pool, QD.t[:, F2 + f2, :], q_[:, 1, :], rd=[q_], wrp=[QD], sembuf=q_)
            A.release(m_)

        def hy_filter(j, n, consts_fft, invl1):
            nJ = n // 128
            m_ = A.mark()
            zt = A.alloc("zt", [33, n])
            k.dma(k.sp, zt[:], CI[f"c_zt{n}"][:, :], rd=[CI[f"c_zt{n}"]], wr=[zt])
            dist = A.alloc("dist", [128, nJ])
            k.dma(k.sp, dist[:], CI[f"c_dist{n}"][:, :], rd=[CI[f"c_dist{n}"]], wr=[dist])
            nad = A.alloc("nad", [128, D])
            k.dma(k.sp, nad[:], CI["c_nad"].t.partition_broadcast(128), rd=[CI["c_nad"]], wr=[nad])
            w1 = A.alloc("pw1", [33, 64])
            k.dma(k.sp, w1[:], IN['hy_pe_w1'].t[j], rd=[IN['hy_pe_w1']], wr=[w1])
            w2 = A.alloc("pw2", [64, 64])
            k.dma(k.sp, w2[:], IN['hy_pe_w2'].t[j], rd=[IN['hy_pe_w2']], wr=[w2])
            w3 = A.alloc("pw3", [64, 64])
            k.dma(k.sp, w3[:], IN['hy_pe_w3'].t[j], rd=[IN['hy_pe_w3']], wr=[w3])
            w4 = A.alloc("pw4", [64, D])
            k.dma(k.sp, w4[:], IN['hy_pe_w4'].t[j], rd=[IN['hy_pe_w4']], wr=[w4])
            fv = A.alloc("fv", [128, 1, 4])
            st = A.alloc("fst", [4, 64])
            for r, nm in enumerate(('hy_freq', 'hy_pe_b1', 'hy_pe_b2', 'hy_pe_b3')):
                k.dma(k.sp, st[r:r + 1, :], row(IN[nm].t[j]), rd=[IN[nm]], wrp=[st])
            k.mmv([(PB[0][0:64, 0:4], st[0:4, 0:64], identf[0:4, 0:4], True, True)], rd=[st, identf], wr=[PB[0]], transpose=True)
            k.op(k.dve, I("tensor_copy", out=fv[0:64, 0, :], in_=PB[0][0:64, 0:4]), rd=[PB[0]], wr=[fv])
            sc = A.alloc("fsc", [64, 4])
            k.op(k.dve, I("tensor_scalar", out=sc[:, 0:1], in0=fv[0:64, 0, 0:1], scalar1=1.0 / TWO_PI, scalar2=0.0, op0=ALU.mult, op1=ALU.add), rd=[fv], wr=[sc])
            for l in range(1, 4):
                k.op(k.dve, I("tensor_tensor", out=sc[:, l:l + 1], in0=fv[0:64, 0, l:l + 1], in1=sc[:, 0:1], op=ALU.mult), rd=[fv, sc], wr=[sc])
            hT = [A.alloc(f"hT{i}", [64, n]) for i in range(2)]
            t1 = A.alloc("ft1", [64, 512])
            ti = A.alloc("fti", [64, 512], I32)
            t2 = A.alloc("ft2", [64, 512])
            for cw_ in range(n // 512 if n >= 512 else 1):
                nc_ = min(512, n)
                cs = slice(cw_ * 512, cw_ * 512 + nc_)
                for l in range(3):
                    bank = PB[l % 2]
                    if l == 0:
                        k.mm(bank[0:64, :nc_], [(w1[:], zt[:, cs])], rd=[w1, zt], wr=[bank])
                    else:
                        wl = w2 if l == 1 else w3
                        k.mm(bank[0:64, :nc_], [(wl[:], hT[(l - 1) % 2][:, cs])], rd=[wl, hT[(l - 1) % 2]], wr=[bank])
                    k.op(k.dve, I("tensor_scalar", out=t1[:, :nc_], in0=bank[0:64, :nc_], scalar1=sc[:, 0:1], scalar2=sc[:, l + 1:l + 2],
                                  op0=ALU.mult, op1=ALU.add), rd=[bank, sc], wr=[t1])
                    k.op(k.dve, I("tensor_copy", out=ti[:, :nc_], in_=t1[:, :nc_]), rd=[t1], wr=[ti])
                    k.op(k.pool, I("tensor_copy", out=t2[:, :nc_], in_=ti[:, :nc_]), rd=[ti], wr=[t2])
                    k.op(k.dve, I("tensor_tensor", out=t1[:, :nc_], in0=t1[:, :nc_], in1=t2[:, :nc_], op=ALU.subtract), rd=[t1, t2], wr=[t1])
                    dst = hT[l % 2]
                    k.op(k.act, I("activation", out=dst[:, cs], in_=t1[:, :nc_], func=AF.Sin, scale=TWO_PI), rd=[t1], wrp=[dst])
            h3 = hT[0]
            kw = [A.alloc(f"kw{i}", [128, D]) for i in range(2)]
            win = [A.alloc(f"win{i}", [128, D]) for i in range(2)]
            kwb = [A.alloc(f"kwb{i}", [128, D], BF16) for i in range(2)]
            l1row = A.alloc("l1row", [1, D])
            for a in range(nJ):
                p = a % 2
                for h in range(2):
                    k.mm(PB[2 + h][:, :], [(h3[:, a * 128:(a + 1) * 128], w4[:, h * 512:(h + 1) * 512])], rd=[h3, w4], wr=[PB[2 + h]])
                k.op(k.act, I("activation", out=win[p][:], in_=nad[:], func=AF.Exp, scale=dist[:, a:a + 1]), rd=[nad, dist], wr=[win[p]])
                for h in range(2):
                    hs = slice(h * 512, (h + 1) * 512)
                    k.op(k.dve, I("tensor_tensor", out=kw[p][:, hs], in0=PB[2 + h][:, :], in1=win[p][:, hs], op=ALU.mult), rd=[PB[2 + h], win[p]],
                         wr=[kw[p]] if h == 0 else (), wrp=[kw[p]] if h else ())
                k.op(k.pool, I("tensor_copy", out=kwb[p][:], in_=kw[p][:]), rd=[kw[p]], wr=[kwb[p]])
                k.dma(k.pool, KTOK.t[a * 128:(a + 1) * 128, :], kwb[p][:], rd=[kwb[p]], wrp=[KTOK], sembuf=kwb[p])
                k.op(k.act, I("activation", out=win[p][:], in_=kw[p][:], func=AF.Abs), rd=[kw[p]], wr=[win[p]])
                for h in range(2):
                    k.mmv([(PB[4 + h][0:1, :], ones[:, 0:1], win[p][:, h * 512:(h + 1) * 512], a == 0, a == nJ - 1)], rd=[ones, win[p]],
                          wr=[PB[4 + h]] if a == 0 else (), wrp=() if a == 0 else [PB[4 + h]])
            for h in range(2):
                k.op(k.act, I("copy", out=l1row[:, h * 512:(h + 1) * 512], in_=PB[4 + h][0:1, :]), rd=[PB[4 + h]], wr=[l1row] if h == 0 else (), wrp=[l1row] if h else ())
            k.mmv([(PB[0][:, m:m + 1], l1row[0:1, m * 128:(m + 1) * 128], ones[0:1, 0:1], True, True) for m in range(8)], rd=[l1row, ones], wr=[PB[0]])
            k.op(k.dve, I("reciprocal", out=invl1[:], in_=PB[0][:, 0:8]), rd=[PB[0]], wr=[invl1])
            A.release(m_)
            fft_fwd(KTOK, n, True, KF[n], consts_fft)

        def hy_layer_consts(j):
            cw = A.alloc("hy_cw", [128, 24, 4])
            load_cols(cw, [(IN['hy_short_w'], IN['hy_short_w'].t[j, t]) for t in range(3)] + [(IN['hy_short_b'], IN['hy_short_b'].t[j])])
            sk = A.alloc("hy_sk", [128, 8, 1])
            load_cols(sk, [(IN['hy_skip'], IN['hy_skip'].t[j])])
            wo = A.alloc("hy_wo", [128, 8, D], BF16)
            k.dma(k.sp, wo[:], W[f"hy_out{j}"].t.rearrange("(kk p) n -> p kk n", p=128), rd=[W[f"hy_out{j}"]], wr=[wo])
            Cc = A.alloc("Cc", [128, 128], BF16)
            Cs = A.alloc("Cs", [128, 128], BF16)
            Cns = A.alloc("Cns", [128, 128], BF16)
            k.dma(k.sp, Cc[:], CI["c_cos"][:, :], rd=[CI["c_cos"]], wr=[Cc])
            k.dma(k.sp, Cs[:], CI["c_sin"][:, :], rd=[CI["c_sin"]], wr=[Cs])
            k.dma(k.sp, Cns[:], CI["c_nsin"][:, :], rd=[CI["c_nsin"]], wr=[Cns])
            return cw, sk, wo, Cc, Cs, Cns

        def hy_seq(j, xsrc, xdst, n, consts_, invl1):
            cw, sk, wo, Cc, Cs, Cns = consts_
            Win = W[f"hy_in{j}"]
            X0, XV, _ = FM
            nJ = n // 128
            F2 = nJ + 1
            NF = 2 * F2
            m_ = A.mark()
            wch = [A.alloc(f"hyw{i}", [128, 8, 128], BF16) for i in range(6)]
            tz = [[A.alloc(f"tz{i}_{q}", [128, 512]) for q in range(3)] for i in range(2)]
            xvb = A.alloc("xvb", [128, 8, 512], BF16)
            xvt = [A.alloc(f"xvt{i}", [128, D], BF16) for i in range(2)]
            hnT = A.alloc("hnT", [128, 8, n + 2], BF16)
            norm_to_hnT(xsrc, n, MODT[0], MODT[1], hnT, 1)
            it = 0
            for w0 in range(0, n, 510):
                ncols = min(512, n + 2 - w0)
                nv = ncols - 2
                for m in range(8):
                    p = it % 2
                    it += 1
                    for q in range(3):
                        wq = wch[p * 3 + q]
                        k.dma(k.sp, wq[:], Win.t[q * 8 + m], rd=[Win], wr=[wq], sembuf=wq)
                        bank = PB[p * 3 + q]
                        k.mm(bank[:, :ncols], [(wq[:, kk, :], hnT[:, kk, w0:w0 + ncols]) for kk in range(8)], rd=[wq, hnT], wr=[bank])
                        conv_taps(k.act, bank, ncols, nv, 1, [0, 1, 2], cw, q * 8 + m, cw[:, q * 8 + m, 3:4], tz[p][q])
                    k.op(k.pool, I("tensor_tensor", out=tz[p][1][:, :nv], in0=tz[p][1][:, :nv], in1=tz[p][2][:, :nv], op=ALU.mult),
                         rd=[tz[p][1], tz[p][2]], wr=[tz[p][1]])
                    k.dma(k.pool, X0.t[m * 128:(m + 1) * 128, w0:w0 + nv], tz[p][0][:, :nv], rd=[tz[p][0]], wrp=[X0], sembuf=tz[p][0])
                    k.dma(k.pool, XV.t[m * 128:(m + 1) * 128, w0:w0 + nv], tz[p][1][:, :nv], rd=[tz[p][1]], wrp=[XV], sembuf=tz[p][1])
                    k.op(k.act, I("copy", out=xvb[:, m, :nv], in_=tz[p][1][:, :nv]), rd=[tz[p][1]], wr=[xvb] if m == 0 else (), wrp=[xvb] if m else ())
                for kt in range((nv + 127) // 128):
                    nt = min(128, nv - kt * 128)
                    p = kt % 2
                    pv = pbf(6 + p)
                    k.mmv([(pv[:nt, m * 128:(m + 1) * 128], xvb[:, m, kt * 128:kt * 128 + nt], identb[:], True, True) for m in range(8)],
                          rd=[xvb, identb], wr=[PB[6 + p]], transpose=True)
                    k.op(k.dve, I("tensor_copy", out=xvt[p][:nt, :], in_=pv[:nt, :]), rd=[PB[6 + p]], wr=[xvt[p]])
                    k.dma(k.pool, TOK.t[w0 + kt * 128:w0 + kt * 128 + nt, :], xvt[p][:nt, :], rd=[xvt[p]], wrp=[TOK], sembuf=xvt[p])
            A.release(m_)
            fft_fwd(TOK, n, False, KF[n], (Cc, Cs, Cns))
            m_ = A.mark()
            qas = [A.alloc(f"qa{i}", [NF, 128, 128], BF16) for i in range(2)]
            yc = A.alloc("yc", [128, SEQ])
            xvl = A.alloc("xvl", [128, SEQ])
            x0l = A.alloc("x0l", [128, SEQ])
            y2c = [A.alloc(f"y2c{i}", [128, SEQ], BF16) for i in range(2)]
            mi = A.alloc("mi", [NF, 128, 32], BF16)
            k.dma(k.sp, mi[:, :, 0:nJ], CI[f"c_mi{n}"][:, :, :], rd=[CI[f"c_mi{n}"]], wr=[mi])
            ngrp = 16
            for m in range(8):
                qa = qas[m % 2]
                for tq in range(4):
                    k.dma(k.sp, qa[:, tq * 32:(tq + 1) * 32, :], QD.t[tq * 32:(tq + 1) * 32, 0:NF, m * 128:(m + 1) * 128].rearrange("t f c -> f t c"),
                          rd=[QD], wr=[qa] if tq == 0 else (), wrp=[qa] if tq else (), sembuf=qa)
                k.dma(k.sp, xvl[:, 0:n], XV.t[m * 128:(m + 1) * 128, 0:n], rd=[XV], wr=[xvl], sembuf=xvl)
                k.dma(k.sp, x0l[:, 0:n], X0.t[m * 128:(m + 1) * 128, 0:n], rd=[X0], wr=[x0l], sembuf=x0l)
                for tg in range(128 // ngrp):
                    bank = PB[tg % 4]
                    k.mmv([(bank[:, tl * nJ:(tl + 1) * nJ], qa[:, tg * ngrp + tl, :], mi[:, tg * ngrp + tl, 0:nJ], True, True) for tl in range(ngrp)],
                          rd=[qa, mi], wr=[bank])
                    qe = k.act if tg % 2 == 0 else k.dve
                    cast_op(qe, yc[:, 0:n].rearrange("p (T t) -> p T t", t=128)[:, :, tg * ngrp:(tg + 1) * ngrp],
                            bank[:, 0:ngrp * nJ].rearrange("p (t T) -> p T t", T=nJ), [bank], [yc] if tg == 0 else (), [yc] if tg else ())
                k.op(k.act, I("activation", out=xvl[:, 0:n], in_=xvl[:, 0:n], func=AF.Identity, scale=sk[:, m, 0:1], bias=0.0), rd=[xvl, sk], wr=[xvl])
                k.op(k.dve, I("scalar_tensor_tensor", out=yc[:, 0:n], in0=yc[:, 0:n], scalar=invl1[:, m:m + 1], in1=xvl[:, 0:n], op0=ALU.mult, op1=ALU.add),
                     rd=[yc, invl1, xvl], wr=[yc])
                y2 = y2c[m % 2]
                nh = n // 2
                k.op(k.pool, I("tensor_tensor", out=y2[:, 0:nh], in0=yc[:, 0:nh], in1=x0l[:, 0:nh], op=ALU.mult), rd=[yc, x0l], wr=[y2])
                k.op(k.dve, I("tensor_tensor", out=y2[:, nh:n], in0=yc[:, nh:n], in1=x0l[:, nh:n], op=ALU.mult), rd=[yc, x0l], wrp=[y2])
                k.dma(k.pool, Y2.t[m, :, 0:n], y2[:, 0:n], rd=[y2], wrp=[Y2], sembuf=y2)
            A.release(m_)
            m_ = A.mark()
            y2w = [A.alloc(f"y2w{i}", [128, 8, 512], BF16) for i in range(2)]
            for wi_ in range((n + 511) // 512):
                c0 = wi_ * 512
                ncw = min(512, n - c0)
                yw = y2w[wi_ % 2]
                k.dma(k.sp, yw[:, :, 0:ncw], Y2.t[:, :, c0:c0 + ncw].rearrange("m p t -> p m t"), rd=[Y2], wr=[yw], sembuf=yw)
                for kt in range(ncw // 128):
                    pb = [PB[4 + 2 * (kt % 2)], PB[5 + 2 * (kt % 2)]]
                    for h in range(2):
                        k.mm(pb[h][:, :], [(yw[:, m, kt * 128:(kt + 1) * 128], wo[:, m, h * 512:(h + 1) * 512]) for m in range(8)], rd=[yw, wo], wr=[pb[h]])
                    residual_out(pb, 128, xsrc, xdst, c0 + kt * 128, MODT[2])
            A.release(m_)

        ctx_needed = [True, True, False, False]
        cur = [XIN[0], XIN[1]]
        curs = [SIN[0], SIN[1]]
        pp = 0
        sub = 0

        def nextbufs(which, b):
            nonlocal_pp = None
            return None

        xflip = [0, 0]
        sflip = [0, 0]

        def xdst_for(b):
            d_ = XS[xflip[b]][b]
            xflip[b] ^= 1
            return d_

        def sdst_for(b):
            d_ = SS[sflip[b]][b]
            sflip[b] ^= 1
            return d_

        for layer in range(depth):
            kind = layer % 2
            j = layer // 2
            keep = ctx_needed[layer]
            if layer == 2:
                for b in range(2):
                    dst = xdst_for(b)
                    for w_ in range(64):
                        k.dma(k.sp if w_ % 2 else k.pool, dst.t[w_ * 64:(w_ + 1) * 64, :],
                              cur[b].t.rearrange("(r w) d -> w r d", w=64)[w_], rd=[cur[b]], wrp=[dst])
                    cur[b] = dst
            compute_mod(layer)
            mL = A.mark()
            if kind == 0:
                cs_ = rg_layer_consts(j)
                st = [[A.alloc(f"rg_st{d_}_{m}", [128, 1]) for m in range(8)] for d_ in range(2)]
                for b in range(2):
                    for d_ in range(2):
                        for m in range(8):
                            k.op(k.pool, I("memset", ap=st[d_][m][:], constant=0.0), wr=[st[d_][m]])
                    bcast_mod(layer, 2)
                    sd_ = sdst_for(b) if keep else None
                    rg_seq(j, curs[b], sd_, CTX, cs_, st, keep)
                    snew = sd_
                    bcast_mod(layer, b)
                    xd = xdst_for(b)
                    rg_seq(j, cur[b], xd, SEQ, cs_, st, True)
                    cur[b] = xd
                    if keep:
                        curs[b] = snew
            else:
                cs_ = hy_layer_consts(j)
                invl1 = {}
                ns = [SEQ, CTX] if keep else [SEQ]
                for n in ns:
                    invl1[n] = A.alloc(f"invl1_{n}", [128, 8])
                    hy_filter(j, n, (cs_[3], cs_[4], cs_[5]), invl1[n])
                for b in range(2):
                    if keep:
                        bcast_mod(layer, 2)
                        sd_ = sdst_for(b)
                        hy_seq(j, curs[b], sd_, CTX, cs_, invl1[CTX])
                        curs[b] = sd_
                    bcast_mod(layer, b)
                    xd = xdst_for(b)
                    hy_seq(j, cur[b], xd, SEQ, cs_, invl1[SEQ])
                    cur[b] = xd
            A.release(mL)
            if stop_after == (layer, 'mix'):
                break
            mL = A.mark()
            fc = ffn_layer_consts(layer)
            for b in range(2):
                if keep:
                    bcast_mod(layer, 2)
                    sd_ = sdst_for(b)
                    ffn_seq(layer, curs[b], sd_, CTX, *fc)
                    curs[b] = sd_
                bcast_mod(layer, b)
                xd = xdst_for(b)
                ffn_seq(layer, cur[b], xd, SEQ, *fc)
                cur[b] = xd
            A.release(mL)

        colmajor = depth > 2
        m_ = A.mark()
        fg = A.alloc("fg", [128, D])
        k.dma(k.sp, fg[:], IN['final_g'].t.partition_broadcast(128), rd=[IN['final_g']], wr=[fg])
        junk = A.alloc("fjunk", [128, D], BF16)
        ss = [A.alloc(f"fss{i}", [128, 1]) for i in range(2)]
        sd = [A.alloc(f"fsd{i}", [128, 1]) for i in range(2)]
        rs = [A.alloc(f"frs{i}", [128, 1]) for i in range(2)]
        for b in range(2):
            for i in range(SEQ // 128):
                xt = next_xio()
                k.dma(k.sp, xt[:], cur[b].t[i * 128:(i + 1) * 128, :], rd=[cur[b]], wr=[xt], sembuf=xt)
                p = i % 2
                k.op(k.act, I("activation", out=junk[:], in_=xt[:], func=AF.Square, accum_out=ss[p][:]), rd=[xt], wr=[junk, ss[p]])
                k.op(k.act, I("activation", out=sd[p][:], in_=ss[p][:], func=AF.Sqrt, scale=1.0 / D, bias=1e-6), rd=[ss[p]], wr=[sd[p]])
                k.op(k.dve, I("reciprocal", out=rs[p][:], in_=sd[p][:]), rd=[sd[p]], wr=[rs[p]])
                xo = next_xio()
                k.op(k.dve, I("scalar_tensor_tensor", out=xo[:], in0=xt[:], scalar=rs[p][:], in1=fg[:], op0=ALU.mult, op1=ALU.mult),
                     rd=[xt, rs[p], fg], wr=[xo])
                if colmajor:
                    ov = OUTB[b].t.rearrange("(r w) d -> w r d", w=64)
                    k.dma(k.pool, ov[2 * i], xo[0:64, :], rd=[xo], wrp=[OUTB[b]], sembuf=xo)
                    k.dma(k.pool, ov[2 * i + 1], xo[64:128, :], rd=[xo], wrp=[OUTB[b]], sembuf=xo)
                else:
                    k.dma(k.pool, OUTB[b].t[i * 128:(i + 1) * 128, :], xo[:], rd=[xo], wrp=[OUTB[b]], sembuf=xo)
        A.release(m_)
        k.finish([OUTB[0], OUTB[1]])
        print("build: ops", k.nops, "sems", len(k.sems))
    return nc, consts


_CACHE = {}


def kernel(**inputs):
    if "nc" not in _CACHE:
        _CACHE["nc"] = build()
    nc, consts = _CACHE["nc"]
    in_maps = []
    for core in range(NCORE):
        m = {}
        for nm in INPUT_NAMES:
            a = np.asarray(inputs[nm])
            if nm in ('x', 'c', 'ctx'):
                a = a[2 * core:2 * core + 2]
            m[nm] = np.ascontiguousarray(a, dtype=np.float32)
        for nm, arr in consts.items():
            m[nm] = arr
        in_maps.append(m)
    res = run_bass_kernel_spmd(nc, in_maps, core_ids=list(range(NCORE)))
    out = np.concatenate([np.asarray(r["out"]) for r in res.results], axis=0)
    return out.astype(np.float32)
```

```python
import math
import numpy as np
import ml_dtypes
import concourse.bass as bass
import concourse.mybir as mybir
from concourse.bass_utils import run_bass_kernel_spmd
from contextlib import ExitStack

F32 = mybir.dt.float32
BF16 = mybir.dt.bfloat16
I32 = mybir.dt.int32
U8 = mybir.dt.uint8
AF = mybir.ActivationFunctionType
ALU = mybir.AluOpType

D = 1024
SEQ = 4096
CTX = 256
DFF = 2816
NCORE = 8
TWO_PI = 2.0 * math.pi
DEBUG_DUMP = ()


def I(name, **kw):
    return lambda e: getattr(e, name)(**kw)


class Buf:
    __slots__ = ("name", "t", "w", "r", "pr", "sem", "semv", "key")

    def __init__(s, name, t=None):
        s.name = name
        s.t = t
        s.w = {}
        s.r = {}
        s.pr = {}
        s.sem = None
        s.semv = 0
        s.key = None

    def __getitem__(s, k):
        return s.t[k]


class Q:
    def __init__(s, kb, name, eng):
        s.name = name
        s.eng = eng
        s.sem = kb.newsem("q_" + name)
        s.cnt = 0
        s.seen = {}
        s.ops = []
        s.shsem = None
        s.shv = 0


class KB:
    def __init__(s, nc, es):
        s.nc = nc
        s.es = es
        s.sems = []
        s.pe = Q(s, "pe", nc.tensor)
        s.act = Q(s, "act", nc.scalar)
        s.dve = Q(s, "dve", nc.vector)
        s.pool = Q(s, "pool", nc.gpsimd)
        s.sp = Q(s, "sp", nc.sync)
        s.qs = [s.pe, s.act, s.dve, s.pool, s.sp]
        s.semcache = {}
        s.nops = 0

    def newsem(s, name):
        h = s.es.enter_context(s.nc.semaphore(f"{name}_{len(s.sems)}"))
        s.sems.append(h)
        return len(s.sems) - 1

    def ps(s, name, shape, dt=F32):
        return s.es.enter_context(s.nc.psum_tensor(name, list(shape), dt))

    def dram(s, name, shape, dt=F32):
        kind = "ExternalOutput" if (DEBUG_DUMP and name in DEBUG_DUMP) else "Internal"
        return Buf(name, s.nc.dram_tensor(name, list(shape), dt, kind=kind).ap())

    def _waits(s, q, rd, wr, wrp, same_ok=False):
        need = {}

        def add(d):
            for k, v in d.items():
                if need.get(k, 0) < v:
                    need[k] = v

        for b in rd:
            add(b.w)
        for b in wr:
            add(b.w)
            add(b.r)
        for b in wrp:
            add(b.r)
            add(b.pr)
        out = []
        for k, v in need.items():
            if same_ok and k == q.sem:
                continue
            if q.seen.get(k, 0) >= v:
                continue
            q.seen[k] = v
            out.append((k, v))
        return out

    def _mark(s, sem, val, rd, wr, wrp):
        for b in rd:
            if b.r.get(sem, 0) < val:
                b.r[sem] = val
        for b in wr:
            pr = dict(b.w)
            for k_, v_ in b.r.items():
                if pr.get(k_, 0) < v_:
                    pr[k_] = v_
            b.pr = pr
            b.w = {sem: val}
            b.r = {}
        for b in wrp:
            if b.w.get(sem, 0) < val:
                b.w[sem] = val

    def op(s, q, fn, rd=(), wr=(), wrp=()):
        waits = s._waits(q, rd, wr, wrp)
        q.cnt += 1
        q.ops.append((waits, fn, (q.sem, 1)))
        s._mark(q.sem, q.cnt, rd, wr, wrp)
        s.nops += 1

    def dma(s, q, out, in_, rd=(), wr=(), wrp=(), sembuf=None, **kw):
        waits = s._waits(q, rd, wr, wrp)
        if sembuf is not None:
            if sembuf.sem is None:
                key = sembuf.key
                if key is not None and key in s.semcache:
                    sembuf.sem, sembuf.semv = s.semcache[key]
                else:
                    sembuf.sem = s.newsem("d_" + sembuf.name)
            sembuf.semv += 16
            sem, val = sembuf.sem, sembuf.semv
            if sembuf.key is not None:
                s.semcache[sembuf.key] = (sem, val)
        else:
            if q.shsem is None:
                q.shsem = s.newsem("sh_" + q.name)
            if q.shv > 0 and q.seen.get(q.shsem, 0) < q.shv:
                q.seen[q.shsem] = q.shv
                waits.append((q.shsem, q.shv))
            q.shv += 16
            sem, val = q.shsem, q.shv
        q.ops.append((waits, lambda e: e.dma_start(out=out, in_=in_, **kw), (sem, 16)))
        s._mark(sem, val, rd, wr, wrp)
        s.nops += 1

    def mmv(s, items, rd=(), wr=(), wrp=(), transpose=False):
        q = s.pe
        waits = s._waits(q, rd, wr, wrp, same_ok=True)
        q.cnt += 1

        def fn(e):
            ins = None
            for (o, l, r, st, sp) in items:
                if transpose:
                    ins = e.transpose(o, l, r)
                else:
                    ins = e.matmul(o, l, r, start=st, stop=sp)
            return ins

        q.ops.append((waits, fn, (q.sem, 1)))
        s._mark(q.sem, q.cnt, rd, wr, wrp)
        s.nops += len(items)

    def mm(s, out, pairs, rd=(), wr=()):
        n = len(pairs)
        s.mmv([(out, l, r, i == 0, i == n - 1) for i, (l, r) in enumerate(pairs)], rd=rd, wr=wr)

    def finish(s, final_bufs):
        nc = s.nc
        for q in s.qs:
            waits = s._waits(q, final_bufs, (), ())
            q.ops.append((waits, None, None))
        with nc.Block() as block:
            def emit(q):
                def body(e):
                    for waits, fn, inc in q.ops:
                        for k, v in waits:
                            e.wait_ge(s.sems[k], v)
                        if fn is not None:
                            ins = fn(e)
                            ins.then_inc(s.sems[inc[0]], inc[1])
                return body
            block.tensor(emit(s.pe))
            block.scalar(emit(s.act))
            block.vector(emit(s.dve))
            block.gpsimd(emit(s.pool))
            block.sync(emit(s.sp))


class Arena:
    def __init__(s, tensor, size):
        s.t = tensor
        s.size = size
        s.off = 0
        s.hist = []

    def alloc(s, name, shape, dt=F32):
        esz = 2 if dt == BF16 else 4
        n = int(np.prod(shape[1:])) * esz
        nal = (n + 63) // 64 * 64
        off = s.off
        s.off += nal
        assert s.off <= s.size, (name, s.off, s.size)
        ap = s.t[:, off:off + n].bitcast(dt)
        if len(shape) == 3:
            ap = ap.rearrange("p (a b) -> p a b", a=shape[1])
        if shape[0] < 128:
            ap = ap[0:shape[0]]
        b = Buf(name, ap)
        b.key = (off, n)
        keep = []
        for (o, e, ob) in s.hist:
            if o < off + nal and e > off:
                for d in (ob.w, ob.r):
                    for k, v in d.items():
                        if b.r.get(k, 0) < v:
                            b.r[k] = v
                if o >= off and e <= off + nal:
                    continue
            keep.append((o, e, ob))
        keep.append((off, off + nal, b))
        s.hist = keep
        return b

    def mark(s):
        return s.off

    def release(s, m):
        s.off = m


def _fft_consts(n):
    nJ = n // 128
    N = 2 * n
    F2 = nJ + 1
    J = np.arange(nJ)[:, None, None]
    jj = np.arange(128)[None, :, None]
    f2 = np.arange(F2)[None, None, :]
    ang = 2 * np.pi * ((f2 * (128 * J + jj)) % N) / N
    ma = np.concatenate([np.cos(ang), -np.sin(ang)], axis=2)
    w = np.full(F2, 2.0)
    w[0] = 1.0
    w[F2 - 1] = 1.0
    f2b = np.arange(F2)[:, None, None]
    tt = np.arange(128)[None, :, None]
    To = np.arange(nJ)[None, None, :]
    tf = 128 * (To + nJ // 2) + tt
    ang2 = 2 * np.pi * ((f2b * tf) % N) / N
    mi = np.concatenate([w[:, None, None] / N * np.cos(ang2), -w[:, None, None] / N * np.sin(ang2)], axis=0)
    return ma.astype(ml_dtypes.bfloat16), mi.astype(ml_dtypes.bfloat16)


def _filter_consts(n):
    t = np.linspace(0.0, 1.0, n, dtype=np.float32)[:, None]
    w = (np.float32(2.0 * math.pi / n) * np.arange(n, dtype=np.float32))[:, None]
    bands = np.linspace(1e-4, 15, 16, dtype=np.float32)[None, :]
    z = np.concatenate([t, np.cos(bands * w), -np.sin(bands * w)], axis=-1).astype(np.float32)
    centre = n // 2
    dist = (np.abs(np.arange(n) - centre).astype(np.float32) / np.float32(centre)).astype(np.float32)
    return np.ascontiguousarray(z.T), np.ascontiguousarray(dist.reshape(n // 128, 128).T)


def _consts():
    c = {}
    a = np.arange(128)
    ang = 2 * np.pi * ((a[:, None] * a[None, :]) % 128) / 128
    c["c_cos"] = np.cos(ang).astype(ml_dtypes.bfloat16)
    c["c_sin"] = np.sin(ang).astype(ml_dtypes.bfloat16)
    c["c_nsin"] = (-np.sin(ang)).astype(ml_dtypes.bfloat16)
    c["c_identf"] = np.eye(128, dtype=np.float32)
    c["c_identb"] = np.eye(128).astype(ml_dtypes.bfloat16)
    deltas = np.linspace(math.log(1e-2) / 1.5, math.log(1e-2) / 0.3, D, dtype=np.float32)
    c["c_nad"] = (-np.abs(deltas)).astype(np.float32)
    for n in (SEQ, CTX):
        ma, mi = _fft_consts(n)
        zt, dist = _filter_consts(n)
        c[f"c_ma{n}"] = ma
        c[f"c_mi{n}"] = mi
        c[f"c_zt{n}"] = zt
        c[f"c_dist{n}"] = dist
    return c


INPUT_NAMES = ['x', 'c', 'ctx', 'c_ctx', 'mod_w', 'mod_b', 'norm1_g', 'norm2_g', 'final_g',
               'rg_w_in', 'rg_conv_w', 'rg_conv_b', 'rg_w_a', 'rg_b_a', 'rg_w_i', 'rg_b_i', 'rg_lam', 'rg_w_out',
               'hy_w_in', 'hy_short_w', 'hy_short_b', 'hy_pe_w1', 'hy_pe_b1', 'hy_pe_w2', 'hy_pe_b2', 'hy_pe_w3',
               'hy_pe_b3', 'hy_pe_w4', 'hy_freq', 'hy_skip', 'hy_w_out',
               'ffn_w_up', 'ffn_conv_w', 'ffn_conv_b', 'ffn_w_down']

SHAPES = {
    'x': [2, SEQ, D], 'c': [2, D], 'ctx': [2, CTX, D], 'c_ctx': [D], 'mod_w': [4, D, 6 * D], 'mod_b': [4, 6 * D],
    'norm1_g': [4, D], 'norm2_g': [4, D], 'final_g': [D], 'rg_w_in': [2, D, 2 * D], 'rg_conv_w': [2, 4, D],
    'rg_conv_b': [2, D], 'rg_w_a': [2, 2, 4, 256, 256], 'rg_b_a': [2, 2, D], 'rg_w_i': [2, 2, 4, 256, 256],
    'rg_b_i': [2, 2, D], 'rg_lam': [2, 2, D], 'rg_w_out': [2, D, D], 'hy_w_in': [2, D, 3 * D],
    'hy_short_w': [2, 3, 3 * D], 'hy_short_b': [2, 3 * D], 'hy_pe_w1': [2, 33, 64], 'hy_pe_b1': [2, 64],
    'hy_pe_w2': [2, 64, 64], 'hy_pe_b2': [2, 64], 'hy_pe_w3': [2, 64, 64], 'hy_pe_b3': [2, 64],
    'hy_pe_w4': [2, 64, D], 'hy_freq': [2, 64], 'hy_skip': [2, D], 'hy_w_out': [2, D, D],
    'ffn_w_up': [4, D, 2 * DFF], 'ffn_conv_w': [4, 3, 2 * DFF], 'ffn_conv_b': [4, 2 * DFF], 'ffn_w_down': [4, DFF, D],
}


def build(depth=4, stop_after=None):
    nc = bass.Bass("TRN2", target_bir_lowering=False)
    es = ExitStack()
    consts = _consts()
    with es:
        k = KB(nc, es)
        IN = {}
        for nm in INPUT_NAMES:
            IN[nm] = Buf(nm, nc.dram_tensor(nm, SHAPES[nm], F32, kind="ExternalInput").ap())
        CI = {}
        for nm, arr in consts.items():
            dt = BF16 if arr.dtype == ml_dtypes.bfloat16 else F32
            CI[nm] = Buf(nm, nc.dram_tensor(nm, list(arr.shape), dt, kind="ExternalInput").ap())
        OUT = Buf("out", nc.dram_tensor("out", [2, SEQ, D], F32, kind="ExternalOutput").ap())

        arena_t = es.enter_context(nc.sbuf_tensor("arena", [128, 206 * 1024], U8))
        A = Arena(arena_t, 206 * 1024)
        PSt = [k.ps(f"ps{i}", [128, 1024]) for i in range(4)]
        PB = []
        for i in range(8):
            PB.append(Buf(f"bank{i}", PSt[i // 2][:, (i % 2) * 512:(i % 2) * 512 + 512]))

        def pbf(i):
            return PB[i].t.bitcast(BF16)

        XS = [[Buf(f"x{p}_{b}", None) for b in range(2)] for p in range(2)]
        SS = [[Buf(f"s{p}_{b}", None) for b in range(2)] for p in range(2)]
        for p in range(2):
            xt_ = nc.dram_tensor(f"xs{p}", [2, SEQ, D], F32, kind="Internal").ap()
            st_ = nc.dram_tensor(f"ss{p}", [2, CTX, D], F32, kind="Internal").ap()
            for b in range(2):
                XS[p][b].t = xt_[b]
                SS[p][b].t = st_[b]
        XIN = [Buf(f"xin{b}", IN['x'].t[b]) for b in range(2)]
        SIN = [Buf(f"sin{b}", IN['ctx'].t[b]) for b in range(2)]
        OUTB = [Buf(f"out{b}", OUT.t[b]) for b in range(2)]

        def wS(name, K, M):
            return k.dram(name, [M // 128, 128, K // 128, 128], BF16)

        def wN(name, K, M):
            return k.dram(name, [K, M], BF16)

        W = {}
        for j in range(2):
            W[f"rg_in{j}"] = wS(f"w_rg_in{j}", D, 2 * D)
            W[f"rg_out{j}"] = wN(f"w_rg_out{j}", D, D)
            for d_ in range(2):
                for h in range(4):
                    W[f"rg_a{j}{d_}{h}"] = wS(f"w_rg_a{j}{d_}{h}", 256, 256)
                    W[f"rg_i{j}{d_}{h}"] = wS(f"w_rg_i{j}{d_}{h}", 256, 256)
            W[f"hy_in{j}"] = wS(f"w_hy_in{j}", D, 3 * D)
            W[f"hy_out{j}"] = wN(f"w_hy_out{j}", D, D)
        for i in range(4):
            W[f"up{i}"] = wS(f"w_up{i}", D, 2 * DFF)
            W[f"down{i}"] = wN(f"w_down{i}", DFF, D)
        FM = [k.dram(f"fm{i}", [D, SEQ]) for i in range(3)]
        TOK = k.dram("tok", [SEQ, D], BF16)
        KTOK = k.dram("ktok", [SEQ, D], BF16)
        PD = k.dram("pd", [66, 128, D], BF16)
        QD = k.dram("qd", [128, 66, D], BF16)
        KF = {SEQ: k.dram("kf4096", [33, 2, 128, D]), CTX: k.dram("kf256", [3, 2, 128, D])}

        identf = A.alloc("identf", [128, 128])
        identb = A.alloc("identb", [128, 128], BF16)
        ones = A.alloc("ones", [128, 128])
        cactT = A.alloc("cactT", [128, 8, 3])
        MODT = [A.alloc(f"modt{i}", [128, D]) for i in range(3)]
        MOD3 = k.dram("mod3", [3, 6 * D])
        Y2 = k.dram("y2", [8, 128, SEQ], BF16)
        xio = [A.alloc(f"xio{i}", [128, D]) for i in range(4)]
        k.dma(k.sp, identf[:], CI["c_identf"][:], rd=[CI["c_identf"]], wr=[identf])
        k.dma(k.sp, identb[:], CI["c_identb"][:], rd=[CI["c_identb"]], wr=[identb])
        k.op(k.pool, I("memset", ap=ones[:], constant=1.0), wr=[ones])
        def row(ap):
            return ap.rearrange("(o n) -> o n", o=1)
        PERSIST = A.mark()
        xio_i = [0]

        def next_xio():
            b = xio[xio_i[0] % 4]
            xio_i[0] += 1
            return b

        rr = [0]

        def anyeng():
            rr[0] += 1
            return [k.act, k.dve, k.pool][rr[0] % 3]

        def cast_op(q, out, in_, rd, wr, wrp=()):
            if q is k.act:
                k.op(q, I("copy", out=out, in_=in_), rd=rd, wr=wr, wrp=wrp)
            else:
                k.op(q, I("tensor_copy", out=out, in_=in_), rd=rd, wr=wr, wrp=wrp)

        def cast_weight(src_buf, src2d, K, M, dst, layout, stf, stb, cnt):
            for kc in range(K // 128):
                sf = stf[cnt[0] % 2]
                sb_ = stb[cnt[0] % 2]
                cnt[0] += 1
                k.dma(k.sp, sf[:, :M], src2d[kc * 128:(kc + 1) * 128, :], rd=[src_buf], wr=[sf], sembuf=sf)
                cast_op(anyeng(), sb_[:, :M], sf[:, :M], [sf], [sb_])
                if layout == 'S':
                    for g0 in range(0, M // 128, 16):
                        g1 = min(M // 128, g0 + 16)
                        k.dma(k.pool, dst.t[g0:g1, :, kc, :].rearrange("m p i -> p m i"),
                              sb_[:, g0 * 128:g1 * 128].rearrange("p (m i) -> p m i", i=128), rd=[sb_], wrp=[dst], sembuf=sb_)
                else:
                    k.dma(k.pool, dst.t[kc * 128:(kc + 1) * 128, :], sb_[:, :M], rd=[sb_], wrp=[dst], sembuf=sb_)

        m0 = A.mark()
        stf = [A.alloc(f"stf{i}", [128, 2 * DFF]) for i in range(2)]
        stb = [A.alloc(f"stb{i}", [128, 2 * DFF], BF16) for i in range(2)]
        cnt = [0]
        for i in range(depth):
            j = i // 2
            if i % 2 == 0:
                cast_weight(IN['rg_w_in'], IN['rg_w_in'].t[j], D, 2 * D, W[f"rg_in{j}"], 'S', stf, stb, cnt)
                cast_weight(IN['rg_w_out'], IN['rg_w_out'].t[j], D, D, W[f"rg_out{j}"], 'N', stf, stb, cnt)
                for d_ in range(2):
                    for h in range(4):
                        cast_weight(IN['rg_w_a'], IN['rg_w_a'].t[j, d_, h], 256, 256, W[f"rg_a{j}{d_}{h}"], 'S', stf, stb, cnt)
                        cast_weight(IN['rg_w_i'], IN['rg_w_i'].t[j, d_, h], 256, 256, W[f"rg_i{j}{d_}{h}"], 'S', stf, stb, cnt)
            else:
                cast_weight(IN['hy_w_in'], IN['hy_w_in'].t[j], D, 3 * D, W[f"hy_in{j}"], 'S', stf, stb, cnt)
                cast_weight(IN['hy_w_out'], IN['hy_w_out'].t[j], D, D, W[f"hy_out{j}"], 'N', stf, stb, cnt)
            cast_weight(IN['ffn_w_up'], IN['ffn_w_up'].t[i], D, 2 * DFF, W[f"up{i}"], 'S', stf, stb, cnt)
            cast_weight(IN['ffn_w_down'], IN['ffn_w_down'].t[i], DFF, D, W[f"down{i}"], 'N', stf, stb, cnt)
        A.release(m0)

        m0 = A.mark()
        crow = A.alloc("crow", [3, D])
        k.dma(k.sp, crow[0:2, :], IN['c'][:, :], rd=[IN['c']], wr=[crow])
        k.dma(k.sp, crow[2:3, :], row(IN['c_ctx'].t), rd=[IN['c_ctx']], wrp=[crow])
        crow2 = A.alloc("crow2", [3, D])
        k.op(k.act, I("activation", out=crow2[:], in_=crow[:], func=AF.Silu), rd=[crow], wr=[crow2])
        k.mmv([(PB[0][:, kk * 3:kk * 3 + 3], crow2[0:3, kk * 128:(kk + 1) * 128], identf[0:3, 0:3], True, True) for kk in range(8)],
              rd=[crow2, identf], wr=[PB[0]], transpose=True)
        k.op(k.dve, I("tensor_copy", out=cactT[:].rearrange("p a b -> p (a b)"), in_=PB[0][:, 0:24]), rd=[PB[0]], wr=[cactT])
        A.release(m0)

        def load_cols(dst, rows):
            R = len(rows)
            ncols = rows[0][1].shape[0]
            nch = ncols // 128
            m_ = A.mark()
            st = A.alloc("lc_st", [R, ncols])
            for r, (sbuf_, ap) in enumerate(rows):
                k.dma(k.sp, st[r:r + 1, :], row(ap), rd=[sbuf_], wrp=[st])
            done = 0
            while done < nch:
                nb = min(nch - done, 512 // R)
                k.mmv([(PB[0][:, q_ * R:(q_ + 1) * R], st[0:R, (done + q_) * 128:(done + q_ + 1) * 128], identf[0:R, 0:R], True, True)
                       for q_ in range(nb)], rd=[st, identf], wr=[PB[0]], transpose=True)
                k.op(k.dve, I("tensor_copy", out=dst[:, done:done + nb, :].rearrange("p a b -> p (a b)"), in_=PB[0][:, 0:nb * R]),
                     rd=[PB[0]], wrp=[dst])
                done += nb
            A.release(m_)

        def compute_mod(layer):
            m_ = A.mark()
            mw = [A.alloc(f"mw{i}", [128, 8, 512]) for i in range(2)]
            mb = A.alloc("mb", [1, 6 * D])
            m3 = A.alloc("m3", [3, 6 * D])
            k.dma(k.sp, mb[:], row(IN['mod_b'].t[layer]), rd=[IN['mod_b']], wr=[mb])
            for ns in range(12):
                t = mw[ns % 2]
                k.dma(k.sp, t[:], IN['mod_w'].t[layer].rearrange("(kk p) n -> p kk n", p=128)[:, :, ns * 512:(ns + 1) * 512],
                      rd=[IN['mod_w']], wr=[t], sembuf=t)
                bank = PB[ns % 2]
                pairs = [(cactT[:, kk, :], t[:, kk, :]) for kk in range(8)] + [(ones[0:1, 0:3], mb[0:1, ns * 512:(ns + 1) * 512])]
                k.mm(bank[0:3, :], pairs, rd=[cactT, t, ones, mb], wr=[bank])
                k.op(k.act, I("copy", out=m3[:, ns * 512:(ns + 1) * 512], in_=bank[0:3, :]), rd=[bank], wrp=[m3])
            k.dma(k.pool, MOD3[:, :], m3[:], rd=[m3], wr=[MOD3])
            A.release(m_)

        def bcast_mod(layer, r, which):
            m_ = A.mark()
            gn = A.alloc("gn", [128, D])
            for part in range(3 * which, 3 * which + 3):
                kind = part % 3
                dst = MODT[{0: 1, 1: 0, 2: 2}[kind]]
                k.dma(k.sp, dst[:], MOD3.t[r, part * D:(part + 1) * D].partition_broadcast(128), rd=[MOD3], wr=[dst])
                if kind == 1:
                    nm = 'norm1_g' if part < 3 else 'norm2_g'
                    k.dma(k.sp, gn[:], IN[nm].t[layer].partition_broadcast(128), rd=[IN[nm]], wr=[gn])
                    k.op(k.dve, I("scalar_tensor_tensor", out=dst[:], in0=dst[:], scalar=1.0, in1=gn[:], op0=ALU.add, op1=ALU.mult),
                         rd=[dst, gn], wr=[dst])
            A.release(m_)

        def norm_to_hnT(xsrc, L, Amod, Bmod, hnT, npad):
            m_ = A.mark()
            junk = A.alloc("junk", [128, D], BF16)
            hnb = [A.alloc(f"hnb{i}", [128, D], BF16) for i in range(2)]
            tt0 = A.alloc("ntmp0", [128, D])
            tt_ = [tt0, tt0]
            ss = [A.alloc(f"ss{i}", [128, 1]) for i in range(2)]
            sd = [A.alloc(f"sd{i}", [128, 1]) for i in range(2)]
            rs = [A.alloc(f"rs{i}", [128, 1]) for i in range(2)]
            k.op(k.pool, I("memset", ap=hnT[:, :, 0:1], constant=0.0), wrp=[hnT])
            k.op(k.pool, I("memset", ap=hnT[:, :, L + 1:L + 1 + npad], constant=0.0), wrp=[hnT])
            for i in range(L // 128):
                xt = next_xio()
                k.dma(k.sp, xt[:], xsrc.t[i * 128:(i + 1) * 128, :], rd=[xsrc], wr=[xt], sembuf=xt)
                p = i % 2
                k.op(k.act, I("activation", out=junk[:], in_=xt[:], func=AF.Square, accum_out=ss[p][:]), rd=[xt], wr=[junk, ss[p]])
                k.op(k.act, I("activation", out=sd[p][:], in_=ss[p][:], func=AF.Sqrt, scale=1.0 / D, bias=1e-6), rd=[ss[p]], wr=[sd[p]])
                k.op(k.dve, I("reciprocal", out=rs[p][:], in_=sd[p][:]), rd=[sd[p]], wr=[rs[p]])
                k.op(k.dve, I("scalar_tensor_tensor", out=tt_[p][:], in0=xt[:], scalar=rs[p][:], in1=Amod[:], op0=ALU.mult, op1=ALU.mult),
                     rd=[xt, rs[p], Amod], wr=[tt_[p]])
                k.op(k.pool, I("tensor_tensor", out=hnb[p][:], in0=tt_[p][:], in1=Bmod[:], op=ALU.add), rd=[tt_[p], Bmod], wr=[hnb[p]])
                bank = PB[6 + p]
                pv = pbf(6 + p)
                k.mmv([(pv[:, kk * 128:(kk + 1) * 128], hnb[p][:, kk * 128:(kk + 1) * 128], identb[:], True, True) for kk in range(8)],
                      rd=[hnb[p], identb], wr=[bank], transpose=True)
                q = k.act if i % 2 == 0 else k.dve
                cast_op(q, hnT[:, :, 1 + i * 128:1 + (i + 1) * 128], pv[:, :].rearrange("p (a b) -> p a b", a=8), [bank], (), [hnT])
            A.release(m_)

        def conv_taps(q_act, bank, ncols, nv, off1, taps, cw, cidx, bias_ap, dst):
            k.op(k.act, I("activation", out=dst[:, :nv], in_=bank[:, off1:off1 + nv], func=AF.Identity,
                          scale=cw[:, cidx, taps[off1]:taps[off1] + 1], bias=bias_ap), rd=[bank, cw], wr=[dst])
            for t in range(len(taps)):
                if t == off1:
                    continue
                k.op(k.dve, I("scalar_tensor_tensor", out=dst[:, :nv], in0=bank[:, t:t + nv], scalar=cw[:, cidx, taps[t]:taps[t] + 1],
                              in1=dst[:, :nv], op0=ALU.mult, op1=ALU.add), rd=[bank, cw, dst], wr=[dst])

        def residual_out(po_banks, nt, xsrc, xdst, row0, Gmod, dst_is_final=False):
            xt = next_xio()
            k.dma(k.sp, xt[:nt, :], xsrc.t[row0:row0 + nt, :], rd=[xsrc], wr=[xt], sembuf=xt)
            xo = next_xio()
            for h in range(2):
                hs = slice(h * 512, (h + 1) * 512)
                k.op(k.dve, I("tensor_tensor", out=xo[:nt, hs], in0=po_banks[h][:nt, :], in1=Gmod[:nt, hs], op=ALU.mult),
                     rd=[po_banks[h], Gmod], wr=[xo] if h == 0 else (), wrp=[xo] if h == 1 else ())
            k.op(k.pool, I("tensor_tensor", out=xo[:nt, :], in0=xo[:nt, :], in1=xt[:nt, :], op=ALU.add), rd=[xt, xo], wr=[xo])
            k.dma(k.pool, xdst.t[row0:row0 + nt, :], xo[:nt, :], rd=[xo], wrp=[xdst], sembuf=xo)

        def ffn_layer_consts(layer):
            cw = A.alloc("ffn_cw", [128, 44, 4])
            load_cols(cw, [(IN['ffn_conv_w'], IN['ffn_conv_w'].t[layer, t]) for t in range(3)] + [(IN['ffn_conv_b'], IN['ffn_conv_b'].t[layer])])
            wd = A.alloc("ffn_wd", [128, 22, D], BF16)
            k.dma(k.sp, wd[:], W[f"down{layer}"].t.rearrange("(j p) n -> p j n", p=128), rd=[W[f"down{layer}"]], wr=[wd])
            return cw, wd

        def ffn_seq(layer, xsrc, xdst, L, cw, wd):
            m_ = A.mark()
            wg = [A.alloc(f"wg{i}", [128, 8, 128], BF16) for i in range(2)]
            wu = [A.alloc(f"wu{i}", [128, 8, 128], BF16) for i in range(2)]
            tg = [A.alloc(f"tg{i}", [128, 512]) for i in range(2)]
            tu = [A.alloc(f"tu{i}", [128, 512]) for i in range(2)]
            sg = tg
            act = A.alloc("act", [128, 22, 512], BF16)
            hnT = A.alloc("hnT", [128, 8, L + 2], BF16)
            norm_to_hnT(xsrc, L, MODT[0], MODT[1], hnT, 1)
            Wup = W[f"up{layer}"]
            it = 0
            for w0 in range(0, L, 510):
                ncols = min(512, L + 2 - w0)
                nv = ncols - 2
                for j in range(22):
                    p = it % 2
                    it += 1
                    k.dma(k.sp, wg[p][:], Wup.t[j], rd=[Wup], wr=[wg[p]], sembuf=wg[p])
                    k.dma(k.sp, wu[p][:], Wup.t[22 + j], rd=[Wup], wr=[wu[p]], sembuf=wu[p])
                    bg, bu = PB[2 * p], PB[2 * p + 1]
                    k.mm(bg[:, :ncols], [(wg[p][:, kk, :], hnT[:, kk, w0:w0 + ncols]) for kk in range(8)], rd=[wg[p], hnT], wr=[bg])
                    k.mm(bu[:, :ncols], [(wu[p][:, kk, :], hnT[:, kk, w0:w0 + ncols]) for kk in range(8)], rd=[wu[p], hnT], wr=[bu])
                    conv_taps(k.act, bg, ncols, nv, 1, [0, 1, 2], cw, j, cw[:, j, 3:4], tg[p])
                    conv_taps(k.act, bu, ncols, nv, 1, [0, 1, 2], cw, 22 + j, cw[:, 22 + j, 3:4], tu[p])
                    k.op(k.act, I("activation", out=sg[p][:, :nv], in_=tg[p][:, :nv], func=AF.Silu), rd=[tg[p]], wr=[sg[p]])
                    k.op(k.pool, I("tensor_tensor", out=act[:, j, :nv], in0=sg[p][:, :nv], in1=tu[p][:, :nv], op=ALU.mult),
                         rd=[sg[p], tu[p]], wrp=[act])
                for kt in range((nv + 127) // 128):
                    nt = min(128, nv - kt * 128)
                    pb = [PB[4 + 2 * (kt % 2)], PB[5 + 2 * (kt % 2)]]
                    for h in range(2):
                        k.mm(pb[h][:nt, :], [(act[:, j, kt * 128:kt * 128 + nt], wd[:, j, h * 512:(h + 1) * 512]) for j in range(22)],
                             rd=[act, wd], wr=[pb[h]])
                    residual_out(pb, nt, xsrc, xdst, w0 + kt * 128, MODT[2])
            A.release(m_)

        def rg_layer_consts(j):
            cw = A.alloc("rg_cw", [128, 8, 5])
            load_cols(cw, [(IN['rg_conv_w'], IN['rg_conv_w'].t[j, t]) for t in range(4)] + [(IN['rg_conv_b'], IN['rg_conv_b'].t[j])])
            gv = A.alloc("rg_gv", [128, 8, 6])
            load_cols(gv, [(IN[nm], IN[nm].t[j, d_]) for d_ in range(2) for nm in ('rg_b_a', 'rg_b_i', 'rg_lam')])
            cl = A.alloc("rg_cl", [128, 8, 2])
            tmp = A.alloc("rg_cltmp", [128, 8, 2])
            for d_ in range(2):
                k.op(k.act, I("activation", out=tmp[:, :, d_:d_ + 1], in_=gv[:, :, d_ * 3 + 2:d_ * 3 + 3], func=AF.Exp, scale=-1.0),
                     rd=[gv], wrp=[tmp])
            k.op(k.act, I("activation", out=cl[:], in_=tmp[:], func=AF.Ln, scale=1.0, bias=1.0), rd=[tmp], wr=[cl])
            k.op(k.dve, I("tensor_scalar", out=cl[:], in0=cl[:], scalar1=-8.0, scalar2=0.0, op0=ALU.mult, op1=ALU.add), rd=[cl], wr=[cl])
            hv = A.alloc("rg_hv", [128, 8, 6])
            for d_ in range(2):
                k.op(k.dve, I("tensor_scalar", out=hv[:, :, d_ * 3:d_ * 3 + 2], in0=gv[:, :, d_ * 3:d_ * 3 + 2], scalar1=0.5, scalar2=0.0,
                              op0=ALU.mult, op1=ALU.add), rd=[gv], wr=[hv] if d_ == 0 else (), wrp=[hv] if d_ else ())
                k.op(k.dve, I("tensor_scalar", out=hv[:, :, d_ * 3 + 2:d_ * 3 + 3], in0=cl[:, :, d_:d_ + 1], scalar1=0.5, scalar2=0.0,
                              op0=ALU.mult, op1=ALU.add), rd=[cl], wrp=[hv])
            wa = A.alloc("rg_wa", [128, 64, 128], BF16)
            wi = A.alloc("rg_wi", [128, 64, 128], BF16)
            for d_ in range(2):
                for h in range(4):
                    for mo in range(2):
                        b0 = ((d_ * 4 + h) * 2 + mo) * 2
                        k.dma(k.sp, wa[:, b0:b0 + 2, :], W[f"rg_a{j}{d_}{h}"].t[mo], rd=[W[f"rg_a{j}{d_}{h}"]], wrp=[wa])
                        k.dma(k.sp, wi[:, b0:b0 + 2, :], W[f"rg_i{j}{d_}{h}"].t[mo], rd=[W[f"rg_i{j}{d_}{h}"]], wrp=[wi])
            return cw, hv, cl, wa, wi

        def rg_gates_pair(d_, m0, nv, ub, hv, wa, wi, uf, st, rev, bufs_all):
            ms = (m0, m0 + 1)
            bufs = bufs_all[2 * ((m0 // 2) % 2):2 * ((m0 // 2) % 2) + 2]
            for idx, m in enumerate(ms):
                h, mo = m // 2, m % 2
                b0 = ((d_ * 4 + h) * 2 + mo) * 2
                br, bi = PB[2 * idx], PB[2 * idx + 1]
                k.mm(br[:, :nv], [(wa[:, b0 + kk, :], ub[:, 2 * h + kk, :nv]) for kk in range(2)], rd=[wa, ub], wr=[br])
                k.mm(bi[:, :nv], [(wi[:, b0 + kk, :], ub[:, 2 * h + kk, :nv]) for kk in range(2)], rd=[wi, ub], wr=[bi])
            for idx, m in enumerate(ms):
                r_, g_, t_, h_ = bufs[idx]
                br, bi = PB[2 * idx], PB[2 * idx + 1]
                k.op(k.act, I("activation", out=r_[:, :nv], in_=br[:, :nv], func=AF.Tanh, bias=hv[:, m, d_ * 3:d_ * 3 + 1], scale=0.5), rd=[br, hv], wr=[r_])
                k.op(k.act, I("activation", out=g_[:, :nv], in_=bi[:, :nv], func=AF.Tanh, bias=hv[:, m, d_ * 3 + 1:d_ * 3 + 2], scale=0.5), rd=[bi, hv], wr=[g_])
            for idx, m in enumerate(ms):
                r_, g_, t_, h_ = bufs[idx]
                k.op(k.act, I("activation", out=r_[:, :nv], in_=r_[:, :nv], func=AF.Exp, scale=hv[:, m, d_ * 3 + 2:d_ * 3 + 3],
                              bias=hv[:, m, d_ * 3 + 2:d_ * 3 + 3]), rd=[r_, hv], wr=[r_])
                k.op(k.act, I("activation", out=t_[:, :nv], in_=r_[:, :nv], func=AF.Square), rd=[r_], wr=[t_])
            for idx, m in enumerate(ms):
                r_, g_, t_, h_ = bufs[idx]
                k.op(k.act, I("activation", out=t_[:, :nv], in_=t_[:, :nv], func=AF.Sqrt, scale=-1.0, bias=1.0), rd=[t_], wr=[t_])
            outs = []
            for idx, m in enumerate(ms):
                r_, g_, t_, h_ = bufs[idx]
                a_ = r_
                k.op(k.dve, I("scalar_tensor_tensor", out=g_[:, :nv], in0=g_[:, :nv], scalar=1.0, in1=uf[:, m, :nv], op0=ALU.add, op1=ALU.mult),
                     rd=[g_, uf], wr=[g_])
                k.op(k.dve, I("scalar_tensor_tensor", out=t_[:, :nv], in0=t_[:, :nv], scalar=0.5, in1=g_[:, :nv], op0=ALU.mult, op1=ALU.mult),
                     rd=[t_, g_], wr=[t_])
                sm = st[d_][m]
                if not rev:
                    k.op(k.dve, I("tensor_tensor_scan", out=h_[:, :nv], data0=a_[:, :nv], data1=t_[:, :nv], initial=sm[:, 0:1],
                                  op0=ALU.mult, op1=ALU.add), rd=[a_, t_, sm], wr=[h_])
                    k.op(k.act, I("copy", out=sm[:, 0:1], in_=h_[:, nv - 1:nv]), rd=[h_], wr=[sm])
                else:
                    k.op(k.dve, I("tensor_tensor_scan", out=h_[:, :nv][:, ::-1], data0=a_[:, :nv][:, ::-1],
                                  data1=t_[:, :nv][:, ::-1], initial=sm[:, 0:1], op0=ALU.mult, op1=ALU.add), rd=[a_, t_, sm], wr=[h_])
                    k.op(k.act, I("copy", out=sm[:, 0:1], in_=h_[:, 0:1]), rd=[h_], wr=[sm])
                outs.append(h_)
            return outs

        def rg_seq(j, xsrc, xdst, L, consts_, st, write_out):
            cw, hv, cl, wa, wi = consts_
            Win = W[f"rg_in{j}"]
            GG, UU, HF = FM
            m_ = A.mark()
            wch = [A.alloc(f"rgw{i}", [128, 8, 128], BF16) for i in range(2)]
            ggt = [A.alloc(f"ggt{i}", [128, 512]) for i in range(2)]
            uf = A.alloc("uf", [128, 8, 512])
            ub = A.alloc("ub", [128, 8, 512], BF16)
            bufs = [[A.alloc(f"rgb{i}_{q}", [128, 512]) for q in range(4)] for i in range(4)]
            wins = []
            for w0 in range(0, L, 509):
                ncols = min(512, L + 3 - w0)
                wins.append((w0, ncols, ncols - 3))
            it = 0
            m1 = A.mark()
            hnT = A.alloc("hnT", [128, 8, L + 3], BF16)
            norm_to_hnT(xsrc, L, MODT[0], MODT[1], hnT, 2)
            for (w0, ncols, nv) in wins:
                for m in range(16):
                    p = it % 2
                    it += 1
                    k.dma(k.sp, wch[p][:], Win.t[m], rd=[Win], wr=[wch[p]], sembuf=wch[p])
                    bank = PB[4 + p]
                    k.mm(bank[:, :ncols], [(wch[p][:, kk, :], hnT[:, kk, w0:w0 + ncols]) for kk in range(8)], rd=[wch[p], hnT], wr=[bank])
                    if m < 8:
                        g = ggt[p]
                        k.op(k.act, I("activation", out=g[:, :nv], in_=bank[:, 1:1 + nv], func=AF.Gelu), rd=[bank], wr=[g])
                        k.dma(k.pool, GG.t[m * 128:(m + 1) * 128, w0:w0 + nv], g[:, :nv], rd=[g], wrp=[GG], sembuf=g)
                    else:
                        mm_ = m - 8
                        k.op(k.act, I("activation", out=uf[:, mm_, :nv], in_=bank[:, 1:1 + nv], func=AF.Identity,
                                      scale=cw[:, mm_, 1:2], bias=cw[:, mm_, 4:5]), rd=[bank, cw], wr=[uf] if mm_ == 0 else (), wrp=[uf] if mm_ else ())
                        for t in (0, 2, 3):
                            k.op(k.dve, I("scalar_tensor_tensor", out=uf[:, mm_, :nv], in0=bank[:, t:t + nv], scalar=cw[:, mm_, t:t + 1],
                                          in1=uf[:, mm_, :nv], op0=ALU.mult, op1=ALU.add), rd=[bank, cw, uf], wrp=[uf])
                        k.op(k.pool, I("tensor_copy", out=ub[:, mm_, :nv], in_=uf[:, mm_, :nv]), rd=[uf], wr=[ub] if mm_ == 0 else (), wrp=[ub] if mm_ else ())
                k.dma(k.pool, UU.t.rearrange("(m p) t -> p m t", p=128)[:, :, w0:w0 + nv], uf[:, :, :nv], rd=[uf], wrp=[UU], sembuf=uf)
                for m0 in range(0, 8, 2):
                    hs_ = rg_gates_pair(0, m0, nv, ub, hv, wa, wi, uf, st, False, bufs)
                    for idx, h_ in enumerate(hs_):
                        m = m0 + idx
                        k.dma(k.pool, HF.t[m * 128:(m + 1) * 128, w0:w0 + nv], h_[:, :nv], rd=[h_], wrp=[HF], sembuf=h_)
            A.release(m1)
            m1 = A.mark()
            hfl = [A.alloc(f"hfl{i}", [128, 512]) for i in range(2)]
            wo = A.alloc("rg_wo", [128, 8, D], BF16)
            yb = A.alloc("yb", [128, 8, 512], BF16)
            if write_out:
                k.dma(k.sp, wo[:], W[f"rg_out{j}"].t.rearrange("(kk p) n -> p kk n", p=128), rd=[W[f"rg_out{j}"]], wr=[wo])
            for (w0, ncols, nv) in reversed(wins):
                k.dma(k.sp, uf[:, :, :nv], UU.t.rearrange("(m p) t -> p m t", p=128)[:, :, w0:w0 + nv], rd=[UU], wr=[uf], sembuf=uf)
                for mm_ in range(8):
                    k.op(k.pool, I("tensor_copy", out=ub[:, mm_, :nv], in_=uf[:, mm_, :nv]), rd=[uf], wr=[ub] if mm_ == 0 else (), wrp=[ub] if mm_ else ())
                for m in range(8):
                    if m % 2 == 0:
                        hs_ = rg_gates_pair(1, m, nv, ub, hv, wa, wi, uf, st, True, bufs)
                    h_ = hs_[m % 2]
                    p = m % 2
                    if write_out:
                        k.dma(k.sp, hfl[p][:, :nv], HF.t[m * 128:(m + 1) * 128, w0:w0 + nv], rd=[HF], wr=[hfl[p]], sembuf=hfl[p])
                        k.dma(k.sp, ggt[p][:, :nv], GG.t[m * 128:(m + 1) * 128, w0:w0 + nv], rd=[GG], wr=[ggt[p]], sembuf=ggt[p])
                        k.op(k.pool, I("tensor_tensor", out=hfl[p][:, :nv], in0=hfl[p][:, :nv], in1=h_[:, :nv], op=ALU.add), rd=[hfl[p], h_], wr=[hfl[p]])
                        k.op(k.dve, I("tensor_tensor", out=yb[:, m, :nv], in0=hfl[p][:, :nv], in1=ggt[p][:, :nv], op=ALU.mult),
                             rd=[hfl[p], ggt[p]], wr=[yb] if m == 0 else (), wrp=[yb] if m else ())
                if write_out:
                    for kt in range((nv + 127) // 128):
                        nt = min(128, nv - kt * 128)
                        pb = [PB[4 + 2 * (kt % 2)], PB[5 + 2 * (kt % 2)]]
                        for h in range(2):
                            k.mm(pb[h][:nt, :], [(yb[:, m, kt * 128:kt * 128 + nt], wo[:, m, h * 512:(h + 1) * 512]) for m in range(8)],
                                 rd=[yb, wo], wr=[pb[h]])
                        residual_out(pb, nt, xsrc, xdst, w0 + kt * 128, MODT[2])
            A.release(m1)
            A.release(m_)

        def fft_fwd(src_tok, n, filter_mode, kf, consts_fft):
            Cc, Cs, Cns = consts_fft
            nJ = n // 128
            F2 = nJ + 1
            NF = 2 * F2
            m_ = A.mark()
            ua = [A.alloc(f"ua{i}", [nJ, 8, D], BF16) for i in range(2)]
            pa = [A.alloc(f"pa{i}", [NF, D], BF16) for i in range(6)]
            ma = A.alloc("ma", [nJ, 128, NF], BF16)
            k.dma(k.sp, ma[:], CI[f"c_ma{n}"][:, :, :], rd=[CI[f"c_ma{n}"]], wr=[ma])
            srcv = src_tok.t[0:n, :].rearrange("(J q) c -> J q c", q=128)
            for jc in range(16):
                u = ua[jc % 2]
                k.dma(k.sp, u[:], srcv[:, jc * 8:(jc + 1) * 8, :], rd=[src_tok], wr=[u], sembuf=u)
                for jl in range(8):
                    jj = jc * 8 + jl
                    p = jj % 6
                    banks = [PB[2 * (jj % 4)], PB[2 * (jj % 4) + 1]]
                    for h in range(2):
                        k.mm(banks[h][0:NF, :], [(ma[:, jj, :], u[:, jl, h * 512:(h + 1) * 512])], rd=[ma, u], wr=[banks[h]])
                    q = k.act if jj % 2 == 0 else k.dve
                    for h in range(2):
                        cast_op(q, pa[p][:, h * 512:(h + 1) * 512], banks[h][0:NF, :], [banks[h]], [pa[p]] if h == 0 else (), [pa[p]] if h else ())
                    k.dma(k.pool, PD.t[0:NF, jj, :], pa[p][:], rd=[pa[p]], wrp=[PD], sembuf=pa[p])
            A.release(m_)
            m_ = A.mark()
            pbt = [A.alloc(f"pbt{i}", [128, 2, D], BF16) for i in range(2)]
            if filter_mode:
                xo = [A.alloc(f"kfo{i}", [128, 2, D]) for i in range(2)]
            else:
                kft = [A.alloc(f"kft{i}", [128, 2, D]) for i in range(2)]
                tm = [A.alloc(f"ytm{i}", [128, D]) for i in range(4)]
                yy = [A.alloc(f"yy{i}", [128, 2, D], BF16) for i in range(2)]
                qt = [A.alloc(f"qt{i}", [128, 2, D], BF16) for i in range(2)]
            for f2 in range(F2):
                p = f2 % 2
                t = pbt[p]
                k.dma(k.sp, t[:, 0, :], PD.t[f2], rd=[PD], wr=[t], sembuf=t)
                k.dma(k.sp, t[:, 1, :], PD.t[F2 + f2], rd=[PD], wrp=[t], sembuf=t)
                for h in range(2):
                    hs = slice(h * 512, (h + 1) * 512)
                    k.mm(PB[h][:, :], [(Cc[:], t[:, 0, hs]), (Cs[:], t[:, 1, hs])], rd=[Cc, Cs, t], wr=[PB[h]])
                    k.mm(PB[2 + h][:, :], [(Cc[:], t[:, 1, hs]), (Cns[:], t[:, 0, hs])], rd=[Cc, Cns, t], wr=[PB[2 + h]])
                if filter_mode:
                    o = xo[p]
                    for ri in range(2):
                        for h in range(2):
                            hs = slice(h * 512, (h + 1) * 512)
                            first = (ri == 0 and h == 0)
                            k.op(k.act if h == 0 else k.dve, I("copy" if h == 0 else "tensor_copy", out=o[:, ri, hs], in_=PB[2 * ri + h][:, :]),
                                 rd=[PB[2 * ri + h]], wr=[o] if first else (), wrp=() if first else [o])
                    k.dma(k.pool, kf.t[f2].rearrange("r p c -> p r c"), o[:], rd=[o], wrp=[kf], sembuf=o)
                else:
                    kt_ = kft[p]
                    k.dma(k.sp, kt_[:], kf.t[f2].rearrange("r p c -> p r c"), rd=[kf], wr=[kt_], sembuf=kt_)
                    y = yy[p]
                    for h in range(2):
                        hs = slice(h * 512, (h + 1) * 512)
                        xr, xi = PB[h], PB[2 + h]
                        k.op(k.dve, I("tensor_tensor", out=tm[0][:, hs], in0=xr[:, :], in1=kt_[:, 0, hs], op=ALU.mult), rd=[xr, kt_], wr=[tm[0]] if h == 0 else (), wrp=[tm[0]] if h else ())
                        k.op(k.dve, I("tensor_tensor", out=tm[1][:, hs], in0=xi[:, :], in1=kt_[:, 1, hs], op=ALU.mult), rd=[xi, kt_], wr=[tm[1]] if h == 0 else (), wrp=[tm[1]] if h else ())
                        k.op(k.dve, I("tensor_tensor", out=tm[2][:, hs], in0=xr[:, :], in1=kt_[:, 1, hs], op=ALU.mult), rd=[xr, kt_], wr=[tm[2]] if h == 0 else (), wrp=[tm[2]] if h else ())
                        k.op(k.dve, I("tensor_tensor", out=tm[3][:, hs], in0=xi[:, :], in1=kt_[:, 0, hs], op=ALU.mult), rd=[xi, kt_], wr=[tm[3]] if h == 0 else (), wrp=[tm[3]] if h else ())
                    k.op(k.pool, I("tensor_tensor", out=y[:, 0, :], in0=tm[0][:], in1=tm[1][:], op=ALU.subtract), rd=[tm[0], tm[1]], wr=[y])
                    k.op(k.pool, I("tensor_tensor", out=y[:, 1, :], in0=tm[2][:], in1=tm[3][:], op=ALU.add), rd=[tm[2], tm[3]], wrp=[y])
                    for h in range(2):
                        hs = slice(h * 512, (h + 1) * 512)
                        k.mm(PB[4 + h][:, :], [(Cc[:], y[:, 0, hs]), (Cns[:], y[:, 1, hs])], rd=[Cc, Cns, y], wr=[PB[4 + h]])
                        k.mm(PB[6 + h][:, :], [(Cc[:], y[:, 1, hs]), (Cs[:], y[:, 0, hs])], rd=[Cc, Cs, y], wr=[PB[6 + h]])
                    q_ = qt[p]
                    for ri in range(2):
                        for h in range(2):
                            hs = slice(h * 512, (h + 1) * 512)
                            first = (ri == 0 and h == 0)
                            k.op(k.act, I("copy", out=q_[:, ri, hs], in_=PB[4 + 2 * ri + h][:, :]), rd=[PB[4 + 2 * ri + h]],
                                 wr=[q_] if first else (), wrp=() if first else [q_])
                    k.dma(k.pool, QD.t[:, f2, :], q_[:, 0, :], rd=[q_], wrp=[QD], sembuf=q_)
                    k.dma(k.pool, QD.t[:, F2 + f2, :], q_[:, 1, :], rd=[q_], wrp=[QD], sembuf=q_)
            A.release(m_)

        def hy_filter(j, n, consts_fft, invl1):
            nJ = n // 128
            m_ = A.mark()
            zt = A.alloc("zt", [33, n])
            k.dma(k.sp, zt[:], CI[f"c_zt{n}"][:, :], rd=[CI[f"c_zt{n}"]], wr=[zt])
            dist = A.alloc("dist", [128, nJ])
            k.dma(k.sp, dist[:], CI[f"c_dist{n}"][:, :], rd=[CI[f"c_dist{n}"]], wr=[dist])
            nad = A.alloc("nad", [128, D])
            k.dma(k.sp, nad[:], CI["c_nad"].t.partition_broadcast(128), rd=[CI["c_nad"]], wr=[nad])
            w1 = A.alloc("pw1", [33, 64])
            k.dma(k.sp, w1[:], IN['hy_pe_w1'].t[j], rd=[IN['hy_pe_w1']], wr=[w1])
            w2 = A.alloc("pw2", [64, 64])
            k.dma(k.sp, w2[:], IN['hy_pe_w2'].t[j], rd=[IN['hy_pe_w2']], wr=[w2])
            w3 = A.alloc("pw3", [64, 64])
            k.dma(k.sp, w3[:], IN['hy_pe_w3'].t[j], rd=[IN['hy_pe_w3']], wr=[w3])
            w4 = A.alloc("pw4", [64, D])
            k.dma(k.sp, w4[:], IN['hy_pe_w4'].t[j], rd=[IN['hy_pe_w4']], wr=[w4])
            fv = A.alloc("fv", [128, 1, 4])
            st = A.alloc("fst", [4, 64])
            for r, nm in enumerate(('hy_freq', 'hy_pe_b1', 'hy_pe_b2', 'hy_pe_b3')):
                k.dma(k.sp, st[r:r + 1, :], row(IN[nm].t[j]), rd=[IN[nm]], wrp=[st])
            k.mmv([(PB[0][0:64, 0:4], st[0:4, 0:64], identf[0:4, 0:4], True, True)], rd=[st, identf], wr=[PB[0]], transpose=True)
            k.op(k.dve, I("tensor_copy", out=fv[0:64, 0, :], in_=PB[0][0:64, 0:4]), rd=[PB[0]], wr=[fv])
            sc = A.alloc("fsc", [64, 4])
            k.op(k.dve, I("tensor_scalar", out=sc[:, 0:1], in0=fv[0:64, 0, 0:1], scalar1=1.0 / TWO_PI, scalar2=0.0, op0=ALU.mult, op1=ALU.add), rd=[fv], wr=[sc])
            for l in range(1, 4):
                k.op(k.dve, I("tensor_tensor", out=sc[:, l:l + 1], in0=fv[0:64, 0, l:l + 1], in1=sc[:, 0:1], op=ALU.mult), rd=[fv, sc], wr=[sc])
            hT = [A.alloc(f"hT{i}", [64, n]) for i in range(2)]
            t1 = A.alloc("ft1", [64, 512])
            ti = A.alloc("fti", [64, 512], I32)
            t2 = A.alloc("ft2", [64, 512])
            for cw_ in range(n // 512 if n >= 512 else 1):
                nc_ = min(512, n)
                cs = slice(cw_ * 512, cw_ * 512 + nc_)
                for l in range(3):
                    bank = PB[l % 2]
                    if l == 0:
                        k.mm(bank[0:64, :nc_], [(w1[:], zt[:, cs])], rd=[w1, zt], wr=[bank])
                    else:
                        wl = w2 if l == 1 else w3
                        k.mm(bank[0:64, :nc_], [(wl[:], hT[(l - 1) % 2][:, cs])], rd=[wl, hT[(l - 1) % 2]], wr=[bank])
                    k.op(k.dve, I("tensor_scalar", out=t1[:, :nc_], in0=bank[0:64, :nc_], scalar1=sc[:, 0:1], scalar2=sc[:, l + 1:l + 2],
                                  op0=ALU.mult, op1=ALU.add), rd=[bank, sc], wr=[t1])
                    k.op(k.dve, I("tensor_copy", out=ti[:, :nc_], in_=t1[:, :nc_]), rd=[t1], wr=[ti])
                    k.op(k.pool, I("tensor_copy", out=t2[:, :nc_], in_=ti[:, :nc_]), rd=[ti], wr=[t2])
                    k.op(k.dve, I("tensor_tensor", out=t1[:, :nc_], in0=t1[:, :nc_], in1=t2[:, :nc_], op=ALU.subtract), rd=[t1, t2], wr=[t1])
                    dst = hT[l % 2]
                    k.op(k.act, I("activation", out=dst[:, cs], in_=t1[:, :nc_], func=AF.Sin, scale=TWO_PI), rd=[t1], wrp=[dst])
            h3 = hT[0]
            kw = [A.alloc(f"kw{i}", [128, D]) for i in range(2)]
            win = [A.alloc(f"win{i}", [128, D]) for i in range(2)]
            kwb = [A.alloc(f"kwb{i}", [128, D], BF16) for i in range(2)]
            l1row = A.alloc("l1row", [1, D])
            for a in range(nJ):
                p = a % 2
                for h in range(2):
                    k.mm(PB[2 + h][:, :], [(h3[:, a * 128:(a + 1) * 128], w4[:, h * 512:(h + 1) * 512])], rd=[h3, w4], wr=[PB[2 + h]])
                k.op(k.act, I("activation", out=win[p][:], in_=nad[:], func=AF.Exp, scale=dist[:, a:a + 1]), rd=[nad, dist], wr=[win[p]])
                for h in range(2):
                    hs = slice(h * 512, (h + 1) * 512)
                    k.op(k.dve, I("tensor_tensor", out=kw[p][:, hs], in0=PB[2 + h][:, :], in1=win[p][:, hs], op=ALU.mult), rd=[PB[2 + h], win[p]],
                         wr=[kw[p]] if h == 0 else (), wrp=[kw[p]] if h else ())
                k.op(k.pool, I("tensor_copy", out=kwb[p][:], in_=kw[p][:]), rd=[kw[p]], wr=[kwb[p]])
                k.dma(k.pool, KTOK.t[a * 128:(a + 1) * 128, :], kwb[p][:], rd=[kwb[p]], wrp=[KTOK], sembuf=kwb[p])
                k.op(k.act, I("activation", out=win[p][:], in_=kw[p][:], func=AF.Abs), rd=[kw[p]], wr=[win[p]])
                for h in range(2):
                    k.mmv([(PB[4 + h][0:1, :], ones[:, 0:1], win[p][:, h * 512:(h + 1) * 512], a == 0, a == nJ - 1)], rd=[ones, win[p]],
                          wr=[PB[4 + h]] if a == 0 else (), wrp=() if a == 0 else [PB[4 + h]])
            for h in range(2):
                k.op(k.act, I("copy", out=l1row[:, h * 512:(h + 1) * 512], in_=PB[4 + h][0:1, :]), rd=[PB[4 + h]], wr=[l1row] if h == 0 else (), wrp=[l1row] if h else ())
            k.mmv([(PB[0][:, m:m + 1], l1row[0:1, m * 128:(m + 1) * 128], ones[0:1, 0:1], True, True) for m in range(8)], rd=[l1row, ones], wr=[PB[0]])
            k.op(k.dve, I("reciprocal", out=invl1[:], in_=PB[0][:, 0:8]), rd=[PB[0]], wr=[invl1])
            A.release(m_)
            fft_fwd(KTOK, n, True, KF[n], consts_fft)

        def hy_layer_consts(j):
            cw = A.alloc("hy_cw", [128, 24, 4])
            load_cols(cw, [(IN['hy_short_w'], IN['hy_short_w'].t[j, t]) for t in range(3)] + [(IN['hy_short_b'], IN['hy_short_b'].t[j])])
            sk = A.alloc("hy_sk", [128, 8, 1])
            load_cols(sk, [(IN['hy_skip'], IN['hy_skip'].t[j])])
            wo = A.alloc("hy_wo", [128, 8, D], BF16)
            k.dma(k.sp, wo[:], W[f"hy_out{j}"].t.rearrange("(kk p) n -> p kk n", p=128), rd=[W[f"hy_out{j}"]], wr=[wo])
            Cc = A.alloc("Cc", [128, 128], BF16)
            Cs = A.alloc("Cs", [128, 128], BF16)
            Cns = A.alloc("Cns", [128, 128], BF16)
            k.dma(k.sp, Cc[:], CI["c_cos"][:, :], rd=[CI["c_cos"]], wr=[Cc])
            k.dma(k.sp, Cs[:], CI["c_sin"][:, :], rd=[CI["c_sin"]], wr=[Cs])
            k.dma(k.sp, Cns[:], CI["c_nsin"][:, :], rd=[CI["c_nsin"]], wr=[Cns])
            return cw, sk, wo, Cc, Cs, Cns

        def hy_seq(j, xsrc, xdst, n, consts_, invl1):
            cw, sk, wo, Cc, Cs, Cns = consts_
            Win = W[f"hy_in{j}"]
            X0, XV, _ = FM
            nJ = n // 128
            F2 = nJ + 1
            NF = 2 * F2
            m_ = A.mark()
            wch = [A.alloc(f"hyw{i}", [128, 8, 128], BF16) for i in range(6)]
            tz = [[A.alloc(f"tz{i}_{q}", [128, 512]) for q in range(3)] for i in range(2)]
            xvb = A.alloc("xvb", [128, 8, 512], BF16)
            xvt = [A.alloc(f"xvt{i}", [128, D], BF16) for i in range(2)]
            hnT = A.alloc("hnT", [128, 8, n + 2], BF16)
            norm_to_hnT(xsrc, n, MODT[0], MODT[1], hnT, 1)
            it = 0
            for w0 in range(0, n, 510):
                ncols = min(512, n + 2 - w0)
                nv = ncols - 2
                for m in range(8):
                    p = it % 2
                    it += 1
                    for q in range(3):
                        wq = wch[p * 3 + q]
                        k.dma(k.sp, wq[:], Win.t[q * 8 + m], rd=[Win], wr=[wq], sembuf=wq)
                        bank = PB[p * 3 + q]
                        k.mm(bank[:, :ncols], [(wq[:, kk, :], hnT[:, kk, w0:w0 + ncols]) for kk in range(8)], rd=[wq, hnT], wr=[bank])
                        conv_taps(k.act, bank, ncols, nv, 1, [0, 1, 2], cw, q * 8 + m, cw[:, q * 8 + m, 3:4], tz[p][q])
                    k.op(k.pool, I("tensor_tensor", out=tz[p][1][:, :nv], in0=tz[p][1][:, :nv], in1=tz[p][2][:, :nv], op=ALU.mult),
                         rd=[tz[p][1], tz[p][2]], wr=[tz[p][1]])
                    k.dma(k.pool, X0.t[m * 128:(m + 1) * 128, w0:w0 + nv], tz[p][0][:, :nv], rd=[tz[p][0]], wrp=[X0], sembuf=tz[p][0])
                    k.dma(k.pool, XV.t[m * 128:(m + 1) * 128, w0:w0 + nv], tz[p][1][:, :nv], rd=[tz[p][1]], wrp=[XV], sembuf=tz[p][1])
                    k.op(k.act, I("copy", out=xvb[:, m, :nv], in_=tz[p][1][:, :nv]), rd=[tz[p][1]], wr=[xvb] if m == 0 else (), wrp=[xvb] if m else ())
                for kt in range((nv + 127) // 128):
                    nt = min(128, nv - kt * 128)
                    p = kt % 2
                    pv = pbf(6 + p)
                    k.mmv([(pv[:nt, m * 128:(m + 1) * 128], xvb[:, m, kt * 128:kt * 128 + nt], identb[:], True, True) for m in range(8)],
                          rd=[xvb, identb], wr=[PB[6 + p]], transpose=True)
                    k.op(k.dve, I("tensor_copy", out=xvt[p][:nt, :], in_=pv[:nt, :]), rd=[PB[6 + p]], wr=[xvt[p]])
                    k.dma(k.pool, TOK.t[w0 + kt * 128:w0 + kt * 128 + nt, :], xvt[p][:nt, :], rd=[xvt[p]], wrp=[TOK], sembuf=xvt[p])
            A.release(m_)
            fft_fwd(TOK, n, False, KF[n], (Cc, Cs, Cns))
            m_ = A.mark()
            qas = [A.alloc(f"qa{i}", [NF, 128, 128], BF16) for i in range(2)]
            yc = A.alloc("yc", [128, SEQ])
            xvl = A.alloc("xvl", [128, SEQ])
            x0l = A.alloc("x0l", [128, SEQ])
            y2c = [A.alloc(f"y2c{i}", [128, SEQ], BF16) for i in range(2)]
            mi = A.alloc("mi", [NF, 128, 32], BF16)
            k.dma(k.sp, mi[:, :, 0:nJ], CI[f"c_mi{n}"][:, :, :], rd=[CI[f"c_mi{n}"]], wr=[mi])
            ngrp = 16
            for m in range(8):
                qa = qas[m % 2]
                for tq in range(4):
                    k.dma(k.sp, qa[:, tq * 32:(tq + 1) * 32, :], QD.t[tq * 32:(tq + 1) * 32, 0:NF, m * 128:(m + 1) * 128].rearrange("t f c -> f t c"),
                          rd=[QD], wr=[qa] if tq == 0 else (), wrp=[qa] if tq else (), sembuf=qa)
                k.dma(k.sp, xvl[:, 0:n], XV.t[m * 128:(m + 1) * 128, 0:n], rd=[XV], wr=[xvl], sembuf=xvl)
                k.dma(k.sp, x0l[:, 0:n], X0.t[m * 128:(m + 1) * 128, 0:n], rd=[X0], wr=[x0l], sembuf=x0l)
                for tg in range(128 // ngrp):
                    bank = PB[tg % 4]
                    k.mmv([(bank[:, tl * nJ:(tl + 1) * nJ], qa[:, tg * ngrp + tl, :], mi[:, tg * ngrp + tl, 0:nJ], True, True) for tl in range(ngrp)],
                          rd=[qa, mi], wr=[bank])
                    qe = k.act if tg % 2 == 0 else k.dve
                    cast_op(qe, yc[:, 0:n].rearrange("p (T t) -> p T t", t=128)[:, :, tg * ngrp:(tg + 1) * ngrp],
                            bank[:, 0:ngrp * nJ].rearrange("p (t T) -> p T t", T=nJ), [bank], [yc] if tg == 0 else (), [yc] if tg else ())
                k.op(k.act, I("activation", out=xvl[:, 0:n], in_=xvl[:, 0:n], func=AF.Identity, scale=sk[:, m, 0:1], bias=0.0), rd=[xvl, sk], wr=[xvl])
                k.op(k.dve, I("scalar_tensor_tensor", out=yc[:, 0:n], in0=yc[:, 0:n], scalar=invl1[:, m:m + 1], in1=xvl[:, 0:n], op0=ALU.mult, op1=ALU.add),
                     rd=[yc, invl1, xvl], wr=[yc])
                y2 = y2c[m % 2]
                nh = n // 2
                k.op(k.pool, I("tensor_tensor", out=y2[:, 0:nh], in0=yc[:, 0:nh], in1=x0l[:, 0:nh], op=ALU.mult), rd=[yc, x0l], wr=[y2])
                k.op(k.dve, I("tensor_tensor", out=y2[:, nh:n], in0=yc[:, nh:n], in1=x0l[:, nh:n], op=ALU.mult), rd=[yc, x0l], wrp=[y2])
                k.dma(k.pool, Y2.t[m, :, 0:n], y2[:, 0:n], rd=[y2], wrp=[Y2], sembuf=y2)
            A.release(m_)
            m_ = A.mark()
            y2w = [A.alloc(f"y2w{i}", [128, 8, 512], BF16) for i in range(2)]
            for wi_ in range((n + 511) // 512):
                c0 = wi_ * 512
                ncw = min(512, n - c0)
                yw = y2w[wi_ % 2]
                k.dma(k.sp, yw[:, :, 0:ncw], Y2.t[:, :, c0:c0 + ncw].rearrange("m p t -> p m t"), rd=[Y2], wr=[yw], sembuf=yw)
                for kt in range(ncw // 128):
                    pb = [PB[4 + 2 * (kt % 2)], PB[5 + 2 * (kt % 2)]]
                    for h in range(2):
                        k.mm(pb[h][:, :], [(yw[:, m, kt * 128:(kt + 1) * 128], wo[:, m, h * 512:(h + 1) * 512]) for m in range(8)], rd=[yw, wo], wr=[pb[h]])
                    residual_out(pb, 128, xsrc, xdst, c0 + kt * 128, MODT[2])
            A.release(m_)

        ctx_needed = [True, True, False, False]
        cur = [XIN[0], XIN[1]]
        curs = [SIN[0], SIN[1]]
        pp = 0
        sub = 0

        def nextbufs(which, b):
            nonlocal_pp = None
            return None

        xflip = [0, 0]
        sflip = [0, 0]

        def xdst_for(b):
            d_ = XS[xflip[b]][b]
            xflip[b] ^= 1
            return d_

        def sdst_for(b):
            d_ = SS[sflip[b]][b]
            sflip[b] ^= 1
            return d_

        for layer in range(depth):
            kind = layer % 2
            j = layer // 2
            keep = ctx_needed[layer]
            if layer == 2:
                for b in range(2):
                    dst = xdst_for(b)
                    for w_ in range(64):
                        k.dma(k.sp if w_ % 2 else k.pool, dst.t[w_ * 64:(w_ + 1) * 64, :],
                              cur[b].t.rearrange("(r w) d -> w r d", w=64)[w_], rd=[cur[b]], wrp=[dst])
                    cur[b] = dst
            compute_mod(layer)
            mL = A.mark()
            if kind == 0:
                cs_ = rg_layer_consts(j)
                st = [[A.alloc(f"rg_st{d_}_{m}", [128, 1]) for m in range(8)] for d_ in range(2)]
                for b in range(2):
                    for d_ in range(2):
                        for m in range(8):
                            k.op(k.pool, I("memset", ap=st[d_][m][:], constant=0.0), wr=[st[d_][m]])
                    bcast_mod(layer, 2, 0)
                    sd_ = sdst_for(b) if keep else None
                    rg_seq(j, curs[b], sd_, CTX, cs_, st, keep)
                    snew = sd_
                    bcast_mod(layer, b, 0)
                    xd = xdst_for(b)
                    rg_seq(j, cur[b], xd, SEQ, cs_, st, True)
                    cur[b] = xd
                    if keep:
                        curs[b] = snew
            else:
                cs_ = hy_layer_consts(j)
                invl1 = {}
                ns = [SEQ, CTX] if keep else [SEQ]
                for n in (SEQ, CTX):
                    invl1[n] = A.alloc(f"invl1_{n}", [128, 8])
                for n in ns:
                    hy_filter(j, n, (cs_[3], cs_[4], cs_[5]), invl1[n])
                for b in range(2):
                    if keep:
                        bcast_mod(layer, 2, 0)
                        sd_ = sdst_for(b)
                        hy_seq(j, curs[b], sd_, CTX, cs_, invl1[CTX])
                        curs[b] = sd_
                    bcast_mod(layer, b, 0)
                    xd = xdst_for(b)
                    hy_seq(j, cur[b], xd, SEQ, cs_, invl1[SEQ])
                    cur[b] = xd
            A.release(mL)
            if stop_after == (layer, 'mix'):
                break
            mL = A.mark()
            fc = ffn_layer_consts(layer)
            for b in range(2):
                if keep:
                    bcast_mod(layer, 2, 1)
                    sd_ = sdst_for(b)
                    ffn_seq(layer, curs[b], sd_, CTX, *fc)
                    curs[b] = sd_
                bcast_mod(layer, b, 1)
                xd = xdst_for(b)
                ffn_seq(layer, cur[b], xd, SEQ, *fc)
                cur[b] = xd
            A.release(mL)

        colmajor = depth > 2
        m_ = A.mark()
        fg = A.alloc("fg", [128, D])
        k.dma(k.sp, fg[:], IN['final_g'].t.partition_broadcast(128), rd=[IN['final_g']], wr=[fg])
        junk = A.alloc("fjunk", [128, D], BF16)
        ss = [A.alloc(f"fss{i}", [128, 1]) for i in range(2)]
        sd = [A.alloc(f"fsd{i}", [128, 1]) for i in range(2)]
        rs = [A.alloc(f"frs{i}", [128, 1]) for i in range(2)]
        for b in range(2):
            for i in range(SEQ // 128):
                xt = next_xio()
                k.dma(k.sp, xt[:], cur[b].t[i * 128:(i + 1) * 128, :], rd=[cur[b]], wr=[xt], sembuf=xt)
                p = i % 2
                k.op(k.act, I("activation", out=junk[:], in_=xt[:], func=AF.Square, accum_out=ss[p][:]), rd=[xt], wr=[junk, ss[p]])
                k.op(k.act, I("activation", out=sd[p][:], in_=ss[p][:], func=AF.Sqrt, scale=1.0 / D, bias=1e-6), rd=[ss[p]], wr=[sd[p]])
                k.op(k.dve, I("reciprocal", out=rs[p][:], in_=sd[p][:]), rd=[sd[p]], wr=[rs[p]])
                xo = next_xio()
                k.op(k.dve, I("scalar_tensor_tensor", out=xo[:], in0=xt[:], scalar=rs[p][:], in1=fg[:], op0=ALU.mult, op1=ALU.mult),
                     rd=[xt, rs[p], fg], wr=[xo])
                if colmajor:
                    ov = OUTB[b].t.rearrange("(r w) d -> w r d", w=64)
                    k.dma(k.pool, ov[2 * i], xo[0:64, :], rd=[xo], wrp=[OUTB[b]], sembuf=xo)
                    k.dma(k.pool, ov[2 * i + 1], xo[64:128, :], rd=[xo], wrp=[OUTB[b]], sembuf=xo)
                else:
                    k.dma(k.pool, OUTB[b].t[i * 128:(i + 1) * 128, :], xo[:], rd=[xo], wrp=[OUTB[b]], sembuf=xo)
        A.release(m_)
        k.finish([OUTB[0], OUTB[1]])
        print("build: ops", k.nops, "sems", len(k.sems))
    return nc, consts


_CACHE = {}


def kernel(**inputs):
    if "nc" not in _CACHE:
        _CACHE["nc"] = build()
    nc, consts = _CACHE["nc"]
    in_maps = []
    for core in range(NCORE):
        m = {}
        for nm in INPUT_NAMES:
            a = np.asarray(inputs[nm])
            if nm in ('x', 'c', 'ctx'):
                a = a[2 * core:2 * core + 2]
            m[nm] = np.ascontiguousarray(a, dtype=np.float32)
        for nm, arr in consts.items():
            m[nm] = arr
        in_maps.append(m)
    res = run_bass_kernel_spmd(nc, in_maps, core_ids=list(range(NCORE)))
    out = np.concatenate([np.asarray(r["out"]) for r in res.results], axis=0)
    return out.astype(np.float32)
```
